# Optimizing a Trainium2 kernel written in Bass

```python
import math
import jax, jax.numpy as jnp
from jax import lax
import numpy as np

D_MODEL = 4096
BATCH = 4
SEQ = 4096
DEPTH = 1

CHUNK = 64
N_MEM = 256
SSM_WIDTH = D_MODEL // 2
CONV_WIDTH = D_MODEL - SSM_WIDTH
MIX_WIDTH = SSM_WIDTH + CONV_WIDTH
SSM_GROUP = 16
SSM_GROUPS = SSM_WIDTH // SSM_GROUP
SSM_STATE = 64
SHORT_CONV = 3
XATTN_HEADS = 4
XATTN_HEAD_DIM = D_MODEL // XATTN_HEADS
D_FF = ((8 * D_MODEL // 3 + 255) // 256) * 256
FFN_CONV = 3
EPS = 1e-6
DT_MIN = 1e-3
DT_MAX = 1e-1
PROJ_IN_WIDTH = SSM_WIDTH + 3 * CONV_WIDTH

kernel_name = "hymba_s5_shortconv_convffn_memxattn"


def rmsnorm(x, g):
    xf = x.astype(jnp.float32)
    y = xf * lax.rsqrt(jnp.mean(xf * xf, axis=-1, keepdims=True) + EPS)
    return (y * g.astype(jnp.float32)).astype(x.dtype)


def causal_dwconv(x, w):
    k_width = w.shape[0]
    length = x.shape[1]
    xp = jnp.pad(x, ((0, 0), (k_width - 1, 0), (0, 0)))
    y = xp[:, 0:length] * w[0]
    for k in range(1, k_width):
        y = y + xp[:, k:k + length] * w[k]
    return y


def _cmul(ar, ai, br, bi):
    return ar * br - ai * bi, ar * bi + ai * br


def _scan_combine(e1, e2):
    a1r, a1i, b1r, b1i = e1
    a2r, a2i, b2r, b2i = e2
    ar, ai = _cmul(a2r, a2i, a1r, a1i)
    br, bi = _cmul(a2r, a2i, b1r, b1i)
    return ar, ai, br + b2r, bi + b2i


def s5_mixer(u, lam_re, lam_im, log_step, b_re, b_im, c_re, c_im, d, glu_w, glu_b):
    f32 = jnp.float32
    bsz, length, _ = u.shape
    G, H, P, T = SSM_GROUPS, SSM_GROUP, SSM_STATE, CHUNK
    uf = u.astype(f32).reshape(bsz, length, G, H)
    lr = lam_re.astype(f32)
    li = lam_im.astype(f32)
    step = jnp.exp(log_step.astype(f32))[:, None]
    mag = jnp.exp(lr * step)
    ang = li * step
    abar_re, abar_im = mag * jnp.cos(ang), mag * jnp.sin(ang)
    den = lr * lr + li * li
    nr, ni = abar_re - 1.0, abar_im
    coef_re = (nr * lr + ni * li) / den
    coef_im = (ni * lr - nr * li) / den
    br_f, bi_f = b_re.astype(f32), b_im.astype(f32)
    bbar_re = coef_re[..., None] * br_f - coef_im[..., None] * bi_f
    bbar_im = coef_re[..., None] * bi_f + coef_im[..., None] * br_f
    cr_f, ci_f = c_re.astype(f32), c_im.astype(f32)
    kk = jnp.arange(1, T + 1, dtype=f32)[:, None, None]
    pmag = jnp.exp(lr * step * kk)
    pang = li * step * kk
    pow_re = (pmag * jnp.cos(pang))[:, None]
    pow_im = (pmag * jnp.sin(pang))[:, None]
    a_re = jnp.broadcast_to(abar_re, (T, 1, G, P))
    a_im = jnp.broadcast_to(abar_im, (T, 1, G, P))
    n_chunks = length // T
    uc = uf.reshape(bsz, n_chunks, T, G, H).transpose(1, 2, 0, 3, 4)

    def chunk_step(carry, u_c):
        sr, si = carry
        bu_re = jnp.einsum('tbgh,gph->tbgp', u_c, bbar_re)
        bu_im = jnp.einsum('tbgh,gph->tbgp', u_c, bbar_im)
        _, _, hr, hi = lax.associative_scan(_scan_combine, (a_re, a_im, bu_re, bu_im), axis=0)
        cr, ci = _cmul(pow_re, pow_im, sr[None], si[None])
        hr = hr + cr
        hi = hi + ci
        y = jnp.einsum('tbgp,ghp->tbgh', hr, cr_f) - jnp.einsum('tbgp,ghp->tbgh', hi, ci_f)
        return (hr[-1], hi[-1]), y

    init = (jnp.zeros((bsz, G, P), f32), jnp.zeros((bsz, G, P), f32))
    _, ys = lax.scan(chunk_step, init, uc)
    y = ys.transpose(2, 0, 1, 3, 4).reshape(bsz, length, G, H)
    y = y + d.astype(f32).reshape(G, H) * uf
    z = jax.nn.gelu(y.reshape(bsz, length, SSM_WIDTH)).astype(u.dtype)
    return z * jax.nn.sigmoid(z @ glu_w + glu_b)


def setup_inputs(seed: int = 0) -> dict:
    key = jax.random.key(seed)
    ks = jax.random.split(key, 32)
    f32 = jnp.float32
    nrm = lambda k, shape, scale: jax.random.normal(k, shape, f32) * scale
    gain = lambda k, n: 1.0 + 0.02 * jax.random.normal(k, (DEPTH, n), f32)
    G, H, P = SSM_GROUPS, SSM_GROUP, SSM_STATE
    lam_im_base = math.pi * jnp.arange(P, dtype=f32)
    return {
        "x": jax.random.normal(ks[0], (BATCH, SEQ, D_MODEL), f32),
        "mem": jax.random.normal(ks[1], (BATCH, N_MEM, D_MODEL), f32),
        "norm_mix_g": gain(ks[2], D_MODEL),
        "w_in": nrm(ks[3], (DEPTH, D_MODEL, PROJ_IN_WIDTH), D_MODEL ** -0.5),
        "ssm_lambda_re": -0.5 + 0.01 * jax.random.normal(ks[4], (DEPTH, G, P), f32),
        "ssm_lambda_im": lam_im_base + 0.01 * jax.random.normal(ks[5], (DEPTH, G, P), f32),
        "ssm_log_step": jax.random.uniform(ks[6], (DEPTH, G), f32, math.log(DT_MIN), math.log(DT_MAX)),
        "ssm_b_re": nrm(ks[7], (DEPTH, G, P, H), (2 * H) ** -0.5),
        "ssm_b_im": nrm(ks[8], (DEPTH, G, P, H), (2 * H) ** -0.5),
        "ssm_c_re": nrm(ks[9], (DEPTH, G, H, P), P ** -0.5),
        "ssm_c_im": nrm(ks[10], (DEPTH, G, H, P), P ** -0.5),
        "ssm_d": nrm(ks[11], (DEPTH, SSM_WIDTH), 1.0),
        "ssm_glu_w": nrm(ks[12], (DEPTH, SSM_WIDTH, SSM_WIDTH), SSM_WIDTH ** -0.5),
        "ssm_glu_b": nrm(ks[13], (DEPTH, SSM_WIDTH), 0.01),
        "conv_w": nrm(ks[14], (DEPTH, SHORT_CONV, CONV_WIDTH), SHORT_CONV ** -0.5),
        "out_norm_ssm_g": gain(ks[15], SSM_WIDTH),
        "out_norm_conv_g": gain(ks[16], CONV_WIDTH),
        "w_out": nrm(ks[17], (DEPTH, MIX_WIDTH, D_MODEL), MIX_WIDTH ** -0.5),
        "norm_xattn_g": gain(ks[18], D_MODEL),
        "norm_mem_g": gain(ks[19], D_MODEL),
        "xattn_wq": nrm(ks[20], (DEPTH, D_MODEL, D_MODEL), D_MODEL ** -0.5),
        "xattn_wk": nrm(ks[21], (DEPTH, D_MODEL, D_MODEL), D_MODEL ** -0.5),
        "xattn_wv": nrm(ks[22], (DEPTH, D_MODEL, D_MODEL), D_MODEL ** -0.5),
        "xattn_wo": nrm(ks[23], (DEPTH, D_MODEL, D_MODEL), D_MODEL ** -0.5),
        "norm_ffn_g": gain(ks[24], D_MODEL),
        "ffn_w_up": nrm(ks[25], (DEPTH, D_MODEL, 2 * D_FF), D_MODEL ** -0.5),
        "ffn_conv_w": nrm(ks[26], (DEPTH, FFN_CONV, D_FF), FFN_CONV ** -0.5),
        "ffn_conv_b": nrm(ks[27], (DEPTH, D_FF), 0.01),
        "ffn_w_down": nrm(ks[28], (DEPTH, D_FF, D_MODEL), D_FF ** -0.5),
        "norm_final_g": 1.0 + 0.02 * jax.random.normal(ks[29], (D_MODEL,), f32),
    }


def reference(x, mem, norm_mix_g, w_in, ssm_lambda_re, ssm_lambda_im, ssm_log_step,
              ssm_b_re, ssm_b_im, ssm_c_re, ssm_c_im, ssm_d, ssm_glu_w, ssm_glu_b,
              conv_w, out_norm_ssm_g, out_norm_conv_g, w_out,
              norm_xattn_g, norm_mem_g, xattn_wq, xattn_wk, xattn_wv, xattn_wo,
              norm_ffn_g, ffn_w_up, ffn_conv_w, ffn_conv_b, ffn_w_down, norm_final_g):
    bsz, length, _ = x.shape
    n_mem = mem.shape[1]
    split_pts = [SSM_WIDTH, SSM_WIDTH + CONV_WIDTH, SSM_WIDTH + 2 * CONV_WIDTH]
    for l in range(DEPTH):
        h = rmsnorm(x, norm_mix_g[l])
        proj = h @ w_in[l]
        u_ssm, gate_b, gate_c, v = jnp.split(proj, split_pts, axis=-1)
        y_ssm = s5_mixer(u_ssm, ssm_lambda_re[l], ssm_lambda_im[l], ssm_log_step[l],
                         ssm_b_re[l], ssm_b_im[l], ssm_c_re[l], ssm_c_im[l], ssm_d[l],
                         ssm_glu_w[l], ssm_glu_b[l])
        y_conv = gate_b * causal_dwconv(gate_c * v, conv_w[l])
        mixed = jnp.concatenate([rmsnorm(y_ssm, out_norm_ssm_g[l]),
                                 rmsnorm(y_conv, out_norm_conv_g[l])], axis=-1)
        x = x + mixed @ w_out[l]
        hq = rmsnorm(x, norm_xattn_g[l])
        hm = rmsnorm(mem, norm_mem_g[l])
        q = (hq @ xattn_wq[l]).reshape(bsz, length, XATTN_HEADS, XATTN_HEAD_DIM)
        k = (hm @ xattn_wk[l]).reshape(bsz, n_mem, XATTN_HEADS, XATTN_HEAD_DIM)
        vv = (hm @ xattn_wv[l]).reshape(bsz, n_mem, XATTN_HEADS, XATTN_HEAD_DIM)
        s = jnp.einsum('bqhd,bkhd->bhqk', q, k).astype(jnp.float32) * (XATTN_HEAD_DIM ** -0.5)
        p = jax.nn.softmax(s, axis=-1).astype(x.dtype)
        o = jnp.einsum('bhqk,bkhd->bqhd', p, vv).reshape(bsz, length, D_MODEL)
        x = x + o @ xattn_wo[l]
        h = rmsnorm(x, norm_ffn_g[l])
        up = h @ ffn_w_up[l]
        a, g = jnp.split(up, [D_FF], axis=-1)
        a = causal_dwconv(a, ffn_conv_w[l]) + ffn_conv_b[l]
        x = x + (jax.nn.silu(a) * g) @ ffn_w_down[l]
    return rmsnorm(x, norm_final_g)
```

```python
import math, contextlib
import numpy as np
import concourse.bass as bass
import concourse.mybir as mybir
from concourse.bass_utils import run_bass_kernel_spmd

F32 = mybir.dt.float32
BF16 = mybir.dt.bfloat16
I32 = mybir.dt.int32
AF = mybir.ActivationFunctionType
ALU = mybir.AluOpType
AX = mybir.AxisListType

D = 4096
NTOK = 2048
HALO = 8
NT = NTOK + HALO
NPREV = 2048
NMEM = 256
DFF = 11008
EPS = 1e-6
BLK_OWN = [(0, 8)] + [(8 + 512 * i, 512) for i in range(4)]
BLK_PREV = [(512 * i, 512) for i in range(4)]
TWO_PI = 2.0 * math.pi
PI_SAFE = 3.14159

PC = {}
_o = 0
for _n, _l in [("g_mix", 32), ("ssm_d", 16), ("glu_b", 16), ("cw0", 16), ("cw1", 16), ("cw2", 16),
               ("g_omix", 32), ("g_xattn", 32), ("g_mem", 32), ("g_ffn", 32),
               ("fw0", 86), ("fw1", 86), ("fw2", 86), ("fcb", 86), ("g_final", 32)]:
    PC[_n] = _o
    _o += _l
NPC = 640
assert _o <= NPC


class DSem:
    _n = 0

    def __init__(self, sem):
        self.sem = sem
        self.cnt = 0
        DSem._n += 1
        self.uid = DSem._n


class Scope:
    def __init__(self, cx):
        self.cx = cx
        self.es = contextlib.ExitStack()
        self.dsems = []

    def __enter__(self):
        self.es.__enter__()
        return self

    def __exit__(self, *a):
        for d in self.dsems:
            self.cx.dsems.remove(d)
        return self.es.__exit__(*a)

    def sb(self, name, shape, dt=F32):
        self.cx.uid += 1
        return self.es.enter_context(self.cx.nc.sbuf_tensor("%s_%d" % (name, self.cx.uid), list(shape), dt))

    def ps(self, name, shape, dt=F32):
        self.cx.uid += 1
        return self.es.enter_context(self.cx.nc.psum_tensor("%s_%d" % (name, self.cx.uid), list(shape), dt))

    def dsem(self, name):
        self.cx.uid += 1
        d = DSem(self.es.enter_context(self.cx.nc.semaphore("%s_%d" % (name, self.cx.uid))))
        self.dsems.append(d)
        self.cx.dsems.append(d)
        return d


class Ctx:
    def __init__(self, nc):
        self.nc = nc
        self.eng = {"pe": nc.tensor, "act": nc.scalar, "dve": nc.vector, "pool": nc.gpsimd, "sp": nc.sync}
        self.uid = 0
        self.gs = contextlib.ExitStack()
        self.sem = {}
        self.cnt = {}
        self.waited = {}
        for e in ("pe", "act", "dve", "pool", "sp"):
            self.sem[e] = self.gs.enter_context(nc.semaphore("tk_" + e))
            self.cnt[e] = 0
        self.dsems = []
        self.serial = False

    def scope(self):
        return Scope(self)

    def wait(self, cons, tick):
        if tick is None:
            return
        if tick[0] == "dma":
            d, v = tick[1], tick[2]
            k = (cons, "dma%d" % d.uid)
            if self.waited.get(k, 0) >= v:
                return
            self.eng[cons].wait_ge(d.sem, v)
            self.waited[k] = v
        else:
            prod, v = tick
            if v <= 0:
                return
            k = (cons, prod)
            if self.waited.get(k, 0) >= v:
                return
            self.eng[cons].wait_ge(self.sem[prod], v)
            self.waited[k] = v

    def _serial_waits(self, cons):
        for e in ("pe", "act", "dve", "pool"):
            if e != cons:
                self.wait(cons, (e, self.cnt[e]))
        for d in self.dsems:
            if d.cnt > 0:
                self.wait(cons, ("dma", d, d.cnt))

    def op(self, e, fn, deps=()):
        for d in deps:
            self.wait(e, d)
        if self.serial:
            self._serial_waits(e)
        c = self.cnt[e]
        self.wait(e, (e, c))
        inst = fn(self.eng[e])
        inst.then_inc(self.sem[e], 1)
        self.cnt[e] = c + 1
        return (e, c + 1)

    def pe(self, fn, deps=(), tick=False):
        for d in deps:
            self.wait("pe", d)
        if self.serial:
            self._serial_waits("pe")
            tick = True
        inst = fn(self.eng["pe"])
        if tick:
            inst.then_inc(self.sem["pe"], 1)
            self.cnt["pe"] += 1
            return ("pe", self.cnt["pe"])
        return None

    def dma(self, dsem, out, in_, deps=(), q="sp"):
        for d in deps:
            self.wait(q, d)
        if self.serial:
            self._serial_waits(q)
        self.eng[q].dma_start(out=out, in_=in_).then_inc(dsem.sem, 16)
        dsem.cnt += 16
        return ("dma", dsem, dsem.cnt)

    def barrier(self):
        for e in ("pe", "act", "dve", "pool"):
            self.wait("sp", (e, self.cnt[e]))
        for d in self.dsems:
            if d.cnt > 0:
                self.wait("sp", ("dma", d, d.cnt))
        for d in self.dsems:
            if d.cnt > 0:
                self.eng["sp"].sem_clear(d.sem)
                d.cnt = 0
                for k in [k for k in self.waited if k[1] == "dma%d" % d.uid]:
                    del self.waited[k]
        self.eng["sp"].sem_inc(self.sem["sp"], 1)
        self.cnt["sp"] += 1
        for e in ("pe", "act", "dve", "pool"):
            self.wait(e, ("sp", self.cnt["sp"]))

    def tt(self, e, out, a, b, op, deps=()):
        return self.op(e, lambda E: E.tensor_tensor(out=out, in0=a, in1=b, op=op), deps)

    def ts(self, e, out, a, s1, s2, op0, op1=None, deps=()):
        if op1 is None:
            return self.op(e, lambda E: E.tensor_scalar(out=out, in0=a, scalar1=s1, scalar2=None, op0=op0), deps)
        return self.op(e, lambda E: E.tensor_scalar(out=out, in0=a, scalar1=s1, scalar2=s2, op0=op0, op1=op1), deps)

    def stt(self, e, out, in0, scalar, in1, op0, op1, deps=()):
        return self.op(e, lambda E: E.scalar_tensor_tensor(out=out, in0=in0, scalar=scalar, in1=in1, op0=op0, op1=op1), deps)

    def cp(self, e, out, in_, deps=()):
        if e == "act":
            return self.op(e, lambda E: E.copy(out=out, in_=in_), deps)
        return self.op(e, lambda E: E.tensor_copy(out=out, in_=in_), deps)

    def act(self, out, in_, func, bias=None, scale=1.0, accum=None, deps=()):
        kw = {}
        if bias is not None:
            kw["bias"] = bias
        if accum is not None:
            kw["accum_out"] = accum
        return self.op("act", lambda E: E.activation(out=out, in_=in_, func=func, scale=scale, **kw), deps)

    def memset(self, e, ap, v, deps=()):
        return self.op(e, lambda E: E.memset(ap, v), deps)


def tmax(*ticks):
    return [t for t in ticks if t is not None]


def load_consts(cx, G, T):
    cx.serial = True
    G.consts = G.sb("consts", [128, 266], F32)
    G.cmaskb = G.sb("cmaskb", [128, 8, 128], BF16)
    G.mask8b = G.sb("mask8b", [128, 8], BF16)
    G.identb = G.sb("identb", [128, 128], BF16)
    G.ones = G.sb("ones", [128, 128], F32)
    G.pcol = G.sb("pcol", [128, NPC], F32)
    G.h1 = G.sb("h1", [128, 128], F32)
    G.h2 = G.sb("h2", [128, 128], F32)
    G.eps = G.sb("epsc", [128, 1], F32)
    G.a8 = G.sb("a8g", [128, 2, 128], F32)
    G.ident = G.consts[:, 0:128]
    G.bdmask = G.consts[:, 128:256]
    G.mask8 = G.consts[:, 256:264]
    G.sgn = G.consts[:, 264:265]
    G.flag = G.consts[:, 265:266]
    with cx.scope() as sc:
        ld = sc.dsem("ld")
        cmf = sc.sb("cmf", [128, 8, 128], F32)
        pv = sc.sb("pv", [128, 5, 128], F32)
        ps = sc.ps("ps", [128, 2, 512], F32)
        cx.dma(ld, G.consts[:], T["consts"])
        cx.dma(ld, cmf[:], T["cmask"])
        cx.dma(ld, pv[:], T["pvec"].rearrange("(a p) m -> p a m", p=128))
        cx.cp("dve", G.cmaskb[:], cmf[:])
        cx.cp("dve", G.mask8b[:], G.mask8)
        cx.cp("dve", G.identb[:], G.ident)
        cx.memset("dve", G.ones[:], 1.0)
        cx.memset("dve", G.eps[:], EPS)
        cx.memset("dve", G.h1[:], 0.0)
        cx.memset("dve", G.h2[:], 0.0)
        for a in range(5):
            bank, off = (0, a * 128) if a < 4 else (1, 0)
            cx.pe(lambda E, a=a, bank=bank, off=off: E.transpose(ps[:, bank, off:off + 128], pv[:, a, :], G.ident))
        cx.cp("dve", G.pcol[:, 0:512], ps[:, 0, :])
        cx.cp("dve", G.pcol[:, 512:640], ps[:, 1, 0:128])
        cx.barrier()
    cx.serial = False


def ssm_setup(cx, G, T):
    cx.serial = True
    with cx.scope() as sc:
        ld = sc.dsem("ld")
        st = sc.dsem("st")
        sin = sc.sb("ssmin", [128, 384 + 4 * 2048], F32)
        cx.dma(ld, sin[:], T["ssmin"])
        lr2 = sin[:, 0:128]
        li2 = sin[:, 128:256]
        lst = sin[:, 256:384]
        Bs = sin[:, 384:384 + 2048]
        Bx = sin[:, 384 + 2048:384 + 4096]
        Cs = sin[:, 384 + 4096:384 + 6144]
        Cx = sin[:, 384 + 6144:384 + 8192]
        tabs = sc.sb("tabs", [128, 9, 2, 128], F32)
        w = [sc.sb("w%d" % i, [128, 128], F32) for i in range(8)]
        wi = sc.sb("wi", [128, 128], I32)
        step, lrs, lis, mag, kf, kf2, r, sc_ = w
        cx.act(step[:], lst, AF.Exp)
        cx.tt("dve", lrs[:], lr2, step[:], ALU.mult)
        cx.tt("dve", lis[:], li2, step[:], ALU.mult)
        cx.memset("dve", tabs[:, 0, 0, :], 1.0)
        cx.memset("dve", tabs[:, 0, 1, :], 0.0)
        for k in range(1, 9):
            cx.act(mag[:], lrs[:], AF.Exp, scale=float(k))
            for which, shift in ((1, 0.0), (0, 0.25)):
                cx.ts("dve", kf[:], lis[:], k / TWO_PI, shift, ALU.mult, ALU.add)
                cx.cp("dve", wi[:], kf[:])
                cx.cp("dve", kf2[:], wi[:])
                cx.tt("dve", r[:], kf[:], kf2[:], ALU.subtract)
                cx.ts("dve", r[:], r[:], TWO_PI, PI_SAFE, ALU.mult, ALU.min)
                cx.ts("dve", r[:], r[:], -PI_SAFE, None, ALU.max)
                cx.act(sc_[:], r[:], AF.Sin)
                cx.tt("dve", tabs[:, k, which, :], mag[:], sc_[:], ALU.mult)
        den, nr, t1, t2, cr, ci, cisg, tmp = w
        cx.tt("dve", den[:], lr2, lr2, ALU.mult)
        cx.tt("dve", t1[:], li2, li2, ALU.mult)
        cx.tt("dve", den[:], den[:], t1[:], ALU.add)
        cx.op("dve", lambda E: E.reciprocal(out=den[:], in_=den[:]))
        cx.ts("dve", nr[:], tabs[:, 1, 0, :], -1.0, None, ALU.add)
        ni = tabs[:, 1, 1, :]
        cx.tt("dve", t1[:], nr[:], lr2, ALU.mult)
        cx.tt("dve", t2[:], ni, li2, ALU.mult)
        cx.tt("dve", t1[:], t1[:], t2[:], ALU.add)
        cx.tt("dve", cr[:], t1[:], den[:], ALU.mult)
        cx.tt("dve", t1[:], ni, lr2, ALU.mult)
        cx.tt("dve", t2[:], nr[:], li2, ALU.mult)
        cx.tt("dve", t1[:], t1[:], t2[:], ALU.subtract)
        cx.tt("dve", ci[:], t1[:], den[:], ALU.mult)
        cx.ts("dve", cisg[:], ci[:], G.sgn, None, ALU.mult)

        def bc(tab):
            return tab.unsqueeze(2).broadcast_to([128, 128, 16])

        def v3(ap):
            return ap.rearrange("p (g h) -> p g h", h=16)

        big = [sc.sb("big%d" % i, [128, 2048], F32) for i in range(5)]
        bbs, bbx, ta, tb, sk = big
        cx.tt("dve", v3(ta[:]), v3(Bs), bc(cr[:]), ALU.mult)
        cx.tt("dve", v3(tb[:]), v3(Bx), bc(cisg[:]), ALU.mult)
        cx.tt("dve", bbs[:], ta[:], tb[:], ALU.add)
        cx.tt("dve", v3(ta[:]), v3(Bx), bc(cr[:]), ALU.mult)
        cx.tt("dve", v3(tb[:]), v3(Bs), bc(cisg[:]), ALU.mult)
        cx.tt("dve", bbx[:], ta[:], tb[:], ALU.subtract)
        ps = sc.ps("ps", [128, 4, 512], F32)
        stg = [sc.sb("stg%d" % i, [128, 16, 128], BF16) for i in range(2)]
        stg_t = [None, None]
        nstg = [0]
        sa, sb_ = w[0], w[1]

        def emit_T(src, dst_dram, kidx):
            s = stg[nstg[0] % 2]
            nstg[0] += 1
            for q in range(4):
                for j in range(4):
                    cc = q * 4 + j
                    cx.pe(lambda E, q=q, j=j, cc=cc: E.transpose(ps[:, q, j * 128:(j + 1) * 128], src[:, cc * 128:(cc + 1) * 128], G.ident))
                cx.cp("act", s[:, q * 4:(q + 1) * 4, :], ps[:, q, :].rearrange("p (a m) -> p a m", m=128))
            cx.dma(st, dst_dram[:, :, kidx, :].rearrange("c p m -> p c m"), s[:])

        def emit_plain(src, dst_dram, kidx):
            s = stg[nstg[0] % 2]
            nstg[0] += 1
            cx.cp("act", s[:], src.rearrange("p (c m) -> p c m", m=128))
            cx.dma(st, dst_dram[:, :, kidx, :].rearrange("c p m -> p c m"), s[:])

        for k in range(8):
            ARk = tabs[:, k, 0, :]
            AIk = tabs[:, k, 1, :]
            cx.ts("dve", sa[:], AIk, G.sgn, None, ALU.mult)
            cx.tt("dve", v3(ta[:]), v3(bbs[:]), bc(ARk), ALU.mult)
            cx.tt("dve", v3(tb[:]), v3(bbx[:]), bc(sa[:]), ALU.mult)
            cx.tt("dve", sk[:], ta[:], tb[:], ALU.add)
            emit_T(sk, T["SkT"], k)
            cx.ts("dve", sa[:], ARk, G.sgn, None, ALU.mult)
            cx.tt("dve", v3(ta[:]), v3(bbx[:]), bc(sa[:]), ALU.mult)
            cx.tt("dve", v3(tb[:]), v3(bbs[:]), bc(AIk), ALU.mult)
            cx.tt("dve", sk[:], ta[:], tb[:], ALU.subtract)
            emit_T(sk, T["SJkT"], k)
        for tau in range(9):
            ARk = tabs[:, tau, 0, :]
            AIk = tabs[:, tau, 1, :]
            cx.ts("dve", sa[:], ARk, G.sgn, -1.0, ALU.mult, ALU.mult)
            cx.tt("dve", v3(ta[:]), v3(Cs), bc(sa[:]), ALU.mult)
            cx.tt("dve", v3(tb[:]), v3(Cx), bc(AIk), ALU.mult)
            cx.tt("dve", sk[:], ta[:], tb[:], ALU.subtract)
            if tau >= 1:
                emit_plain(sk[:], T["Rk"], tau - 1)
            if tau <= 7:
                s = stg[nstg[0] % 2]
                nstg[0] += 1
                for q in range(4):
                    for j in range(4):
                        cc = q * 4 + j
                        cx.pe(lambda E, q=q, j=j, cc=cc: E.matmul(ps[:, q, j * 128:(j + 1) * 128], lhsT=bbs[:, cc * 128:(cc + 1) * 128],
                                                                  rhs=sk[:, cc * 128:(cc + 1) * 128], start=True, stop=True))
                    cx.tt("dve", s[:, q * 4:(q + 1) * 4, :], ps[:, q, :].rearrange("p (a m) -> p a m", m=128),
                          G.bdmask.unsqueeze(1).broadcast_to([128, 4, 128]), ALU.mult)
                cx.dma(st, T["Kbd"][:, :, tau, :].rearrange("c p m -> p c m"), s[:])
        cx.cp("dve", G.a8[:], tabs[:, 8, :, :])
        cx.barrier()
    cx.serial = False


def load_norm(cx, G, AT, src, tiles, gcol, xT_out=None):
    with cx.scope() as sc:
        ld = [sc.dsem("ld%d" % i) for i in range(2)]
        st = sc.dsem("st")
        xtok = [sc.sb("xtok%d" % i, [128, D], F32) for i in range(2)]
        xT = sc.sb("xTt", [128, 32, 128], F32)
        junk = sc.sb("junk", [128, D], BF16)
        ssq = sc.sb("ssq", [128, 1], F32)
        rstd = sc.sb("rstd", [128, 1], F32)
        diag = sc.sb("diag", [128, 128], F32)
        rbs = sc.sb("rbs", [128, 128], F32)
        ps = sc.ps("ps", [128, 8, 512], F32)
        NB = 6
        bank_rel = [None] * NB
        slot_rel = [None, None]
        xT_rel = None
        st_tick = None
        rb_rel = None
        nj = 0
        for ti, (r0, nr, c0) in enumerate(tiles):
            sl = ti % 2
            t_ld = cx.dma(ld[sl], xtok[sl][0:nr, :], src[r0:r0 + nr, :], deps=[slot_rel[sl]] if slot_rel[sl] else [])
            t_sq = cx.act(junk[0:nr, :], xtok[sl][0:nr, :], AF.Square, accum=ssq[0:nr, :], deps=[t_ld])
            t_sq = cx.act(rstd[0:nr, :], ssq[0:nr, :], AF.Sqrt, bias=G.eps[0:nr, :], scale=1.0 / D)
            t_r = cx.op("dve", lambda E: E.reciprocal(out=rstd[0:nr, :], in_=rstd[0:nr, :]), deps=[t_sq])
            t_d = cx.ts("dve", diag[0:nr, 0:nr], G.ident[0:nr, 0:nr], rstd[0:nr, :], None, ALU.mult, deps=[rb_rel] if rb_rel else [])
            t_rb = cx.pe(lambda E: E.matmul(ps[:, 7, 0:nr], lhsT=G.ones[0:nr, :], rhs=diag[0:nr, 0:nr], start=True, stop=True),
                         deps=[t_d], tick=True)
            ev_ticks = []
            for grp in range(8):
                b = nj % NB
                nj += 1
                deps = [t_ld]
                if bank_rel[b]:
                    deps.append(bank_rel[b])
                for j in range(4):
                    kc = grp * 4 + j
                    tk = cx.pe(lambda E, b=b, j=j, kc=kc: E.transpose(ps[:, b, j * 128:j * 128 + nr], xtok[sl][0:nr, kc * 128:(kc + 1) * 128],
                                                                      G.ident[0:nr, 0:nr]), deps=deps if j == 0 else (), tick=(j == 3))
                d2 = [tk]
                if grp == 0:
                    if xT_rel:
                        d2 += xT_rel
                t_ev = cx.cp("act", xT[:, grp * 4:(grp + 1) * 4, 0:nr], ps[:, b, :].rearrange("p (a m) -> p a m", m=128)[:, :, 0:nr], deps=d2)
                bank_rel[b] = t_ev
                ev_ticks.append(t_ev)
            slot_rel[sl] = tk
            t_rbs = cx.cp("act", rbs[:, 0:nr], ps[:, 7, 0:nr], deps=[t_rb] + (xT_rel if xT_rel else []))
            last = None
            lastp = None
            for kc in range(32):
                last = cx.stt("dve", AT[:, kc, c0:c0 + nr], xT[:, kc, 0:nr], G.pcol[:, gcol + kc:gcol + kc + 1], rbs[:, 0:nr],
                              ALU.mult, ALU.mult, deps=[ev_ticks[kc // 4], t_rbs])
            lastp = None
            rb_rel = t_rbs
            xT_rel = [last, lastp]
            if xT_out is not None:
                st_tick = cx.dma(st, xT_out.rearrange("(kc p) n -> p kc n", p=128)[:, :, c0:c0 + nr], xT[:, :, 0:nr], deps=[ev_ticks[-1]])
                xT_rel.append(st_tick)
        cx.barrier()


class Gemm:
    def __init__(self, cx, sc, KC, nwst=4, nwbf=2, piece=4):
        self.cx = cx
        self.KC = KC
        self.piece = piece
        self.npieces = (KC + piece - 1) // piece
        self.wst = [sc.sb("wst%d" % i, [128, piece, 128], F32) for i in range(nwst)]
        self.wld = [sc.dsem("wld%d" % i) for i in range(nwst)]
        self.wst_rel = [None] * nwst
        self.wbf = [sc.sb("wbf%d" % i, [128, KC, 128], BF16) for i in range(nwbf)]
        self.wbf_rel = [None] * nwbf
        self.nst = 0
        self.nbf = 0
        self.loaded = {}

    def load(self, W, k0, col0, key, ncols=128):
        cx = self.cx
        bs = self.nbf % len(self.wbf)
        self.nbf += 1
        wb = self.wbf[bs]
        last = None
        for p in range(self.npieces):
            kc0 = p * self.piece
            n = min(self.piece, self.KC - kc0)
            ss = self.nst % len(self.wst)
            self.nst += 1
            src = W[(k0 + kc0) * 128:(k0 + kc0 + n) * 128, col0:col0 + ncols].rearrange("(kc p) m -> p kc m", p=128)
            t_ld = cx.dma(self.wld[ss], self.wst[ss][:, 0:n, 0:ncols], src, deps=[self.wst_rel[ss]] if self.wst_rel[ss] else [])
            deps = [t_ld]
            if p == 0 and self.wbf_rel[bs]:
                deps.append(self.wbf_rel[bs])
            last = cx.cp("pool", wb[:, kc0:kc0 + n, 0:ncols], self.wst[ss][:, 0:n, 0:ncols], deps=deps)
            self.wst_rel[ss] = last
        self.loaded[key] = (wb, last, bs)
        return wb, last

    def release(self, key, tick):
        wb, last, bs = self.loaded.pop(key)
        self.wbf_rel[bs] = tick


def run_gemm(cx, sc, ps, banks, AT, KC, W, k0, coltiles, blocks, epilogue, prefetch=1, pre=None, post=None):
    g = Gemm(cx, sc, KC)
    nM = len(coltiles)
    for i in range(min(prefetch, nM)):
        g.load(W, k0, coltiles[i], i)
    bank_rel = {b: None for b in banks}
    nj = 0
    for mi in range(nM):
        if mi + prefetch < nM:
            g.load(W, k0, coltiles[mi + prefetch], mi + prefetch)
        wb, wt = g.loaded[mi][0], g.loaded[mi][1]
        if pre:
            pre(mi)
        tk = None
        for bi, (c0, n) in enumerate(blocks):
            b = banks[nj % len(banks)]
            nj += 1
            deps = [wt]
            if bank_rel[b]:
                deps.append(bank_rel[b])
            for kc in range(KC):
                tk = cx.pe(lambda E, b=b, kc=kc: E.matmul(ps[:, b, 0:n], lhsT=wb[:, kc, :], rhs=AT[:, kc, c0:c0 + n],
                                                         start=(kc == 0), stop=(kc == KC - 1)),
                           deps=deps if kc == 0 else (), tick=(kc == KC - 1))
            bank_rel[b] = epilogue(mi, bi, ps[:, b, 0:n], n, c0, tk)
        g.release(mi, tk)
        if post:
            post(mi)


def ssq_finalize(cx, sc, G, ps, bank, acc, blocks, dnorm, rstd_row, st, deps, work=None):
    rs = work if work is not None else sc.sb("rsfin", [128, NT], F32)
    last = None
    rel = None
    for (c0, n) in blocks:
        d = list(deps)
        if rel:
            d.append(rel)
        tk = cx.pe(lambda E: E.matmul(ps[:, bank, 0:n], lhsT=G.ones[:], rhs=acc[:, c0:c0 + n], start=True, stop=True), deps=d, tick=True)
        t1 = cx.act(rs[:, c0:c0 + n], ps[:, bank, 0:n], AF.Sqrt, bias=G.eps[:], scale=1.0 / dnorm, deps=[tk])
        rel = t1
        last = cx.op("dve", lambda E: E.reciprocal(out=rs[:, c0:c0 + n], in_=rs[:, c0:c0 + n]), deps=[t1])
    return cx.dma(st, rstd_row, rs[0:1, :], deps=[last])


def norm_load(cx, G, AT, src, KC, N, gcol, rstd_rows, rows_for_kc):
    with cx.scope() as sc:
        ld = [sc.dsem("nl%d" % i) for i in range(3)]
        rl = sc.dsem("rl")
        xin = [sc.sb("xin%d" % i, [128, N], F32) for i in range(3)]
        rel = [None] * 3
        rb = {}
        for r in sorted(set(rows_for_kc)):
            rb[r] = sc.sb("rb%d" % r, [128, N], F32)
            t_rb = cx.dma(rl, rb[r][:], rstd_rows[r:r + 1, :].broadcast_to([128, N]))
        for kc in range(KC):
            s = kc % 3
            t_ld = cx.dma(ld[s], xin[s][:], src[kc * 128:(kc + 1) * 128, :], deps=[rel[s]] if rel[s] else [])
            if kc % 3 != 2:
                rel[s] = cx.stt("dve", AT[:, kc, 0:N], xin[s][:], G.pcol[:, gcol + kc:gcol + kc + 1], rb[rows_for_kc[kc]][:],
                                ALU.mult, ALU.mult, deps=[t_ld, t_rb])
            else:
                cx.ts("pool", xin[s][:], xin[s][:], G.pcol[:, gcol + kc:gcol + kc + 1], None, ALU.mult, deps=[t_ld, t_rb])
                rel[s] = cx.tt("pool", AT[:, kc, 0:N], xin[s][:], rb[rows_for_kc[kc]][:], ALU.mult)
        cx.barrier()


def plain_load(cx, AT, src, KC, N):
    with cx.scope() as sc:
        ld = sc.dsem("pl")
        for kc0 in range(0, KC, 8):
            n = min(8, KC - kc0)
            cx.dma(ld, AT[:, kc0:kc0 + n, 0:N], src[kc0 * 128:(kc0 + n) * 128, :].rearrange("(kc p) n -> p kc n", p=128))
        cx.barrier()


def ssm_main(cx, G, sc0, uT, T, N, own):
    NCH = N // 8
    if own:
        subs = [(0, 1)] + [(1 + 32 * i, 32) for i in range(8)]
    else:
        subs = [(32 * i, 32) for i in range(8)]
    with cx.scope() as sc:
        Hst = sc.sb("Hst", [128, 128, NCH + 1], BF16) if own else None
        t1 = sc.sb("sct1", [128, 128], F32)
        t2 = sc.sb("sct2", [128, 128], F32)
        t3 = sc.sb("sct3", [128, 128], F32)
        t4 = sc.sb("sct4", [128, 128], F32)
        sinj = cx.scope()
        sinj.__enter__()
        skl = [sinj.dsem("skl%d" % i) for i in range(2)]
        skt = [sinj.sb("skt%d" % i, [128, 2, 8, 128], BF16) for i in range(2)]
        skt_rel = [None, None]
        um = [sinj.sb("um%d" % i, [128, 8, 256], BF16) for i in range(2)]
        um_rel = [None, None]
        bst = [sinj.sb("bst%d" % i, [128, 2, 128, 32], BF16) for i in range(2)]
        bst_rel = [None, None]
        a8 = G.a8
        t_a8 = None
        ps = sc.ps("ps", [128, 8, 512], F32)
        banks = list(range(8))
        bank_rel = [None] * 8
        nj = 0
        nl = 0
        AR8 = a8[:, 0, :]
        AI8 = a8[:, 1, :]
        t_state = None
        if own:
            t_state = cx.cp("act", Hst[:, :, 0], G.h1[:])
        for si, (cb, ncb) in enumerate(subs):
            bs = si % 2
            n = ncb * 8
            c0 = cb * 8
            ev_last = None
            for cc in range(16):
                s = nl % 2
                nl += 1
                d = [skt_rel[s]] if skt_rel[s] else []
                t_l1 = cx.dma(skl[s], skt[s][:, 0, :, :], T["SkT"][cc], deps=d)
                t_l2 = cx.dma(skl[s], skt[s][:, 1, :, :], T["SJkT"][cc])
                d = [um_rel[s]] if um_rel[s] else []
                t_um = cx.tt("pool", um[s][:, :, 0:n], uT[:, cc, c0:c0 + n].unsqueeze(1).broadcast_to([128, 8, n]),
                             G.mask8b[:, :].unsqueeze(2).broadcast_to([128, 8, n]), ALU.mult, deps=d)
                for dual in range(2):
                    b = banks[nj % 8]
                    nj += 1
                    deps = [t_l2, t_um]
                    if bank_rel[b]:
                        deps.append(bank_rel[b])
                    tk = None
                    for j in range(8):
                        tk = cx.pe(lambda E, b=b, j=j, s=s, dual=dual: E.matmul(
                            ps[:, b, 0:8 * ncb].rearrange("p (g c) -> p g c", c=ncb),
                            lhsT=skt[s][:, dual, 7 - j, :],
                            rhs=um[s][:, :, 0:n].rearrange("p g (c j) -> p g c j", j=8)[:, :, :, j],
                            start=(j == 0), stop=(j == 7)), deps=deps if j == 0 else (), tick=(j == 7))
                    d2 = [tk]
                    if cc == 0 and dual == 0 and bst_rel[bs]:
                        d2.append(bst_rel[bs])
                    ev = cx.cp("act", bst[bs][:, dual, cc * 8:(cc + 1) * 8, 0:ncb],
                               ps[:, b, 0:8 * ncb].rearrange("p (g c) -> p g c", c=ncb), deps=d2)
                    bank_rel[b] = ev
                    ev_last = ev
                skt_rel[s] = tk
                um_rel[s] = tk
            for c in range(ncb):
                b1 = bst[bs][:, 0, :, c]
                b2 = bst[bs][:, 1, :, c]
                d = [ev_last] if c == 0 else []
                cx.tt("dve", t1[:], AR8, G.h1[:], ALU.mult, deps=d)
                cx.tt("dve", t2[:], AI8, G.h2[:], ALU.mult)
                cx.tt("dve", t3[:], AR8, G.h2[:], ALU.mult)
                cx.tt("dve", t4[:], AI8, G.h1[:], ALU.mult)
                cx.tt("dve", t1[:], t1[:], t2[:], ALU.add)
                cx.tt("dve", t3[:], t3[:], t4[:], ALU.subtract)
                t_h1 = cx.tt("dve", G.h1[:], t1[:], b1, ALU.add)
                t_h2 = cx.tt("dve", G.h2[:], t3[:], b2, ALU.add)
                if own:
                    t_state = cx.cp("act", Hst[:, :, cb + c + 1], G.h1[:], deps=[t_h1])
                    cx.wait("dve", t_state)
            bst_rel[bs] = t_h2
        cx.barrier()
        sinj.__exit__(None, None, None)
        if not own:
            return
        kl = [sc.dsem("kl%d" % i) for i in range(2)]
        kb = [sc.sb("kb%d" % i, [128, 2, 8, 128], BF16) for i in range(2)]
        kb_rel = [None, None]
        rp = [sc.sb("rp%d" % i, [128, 8, 8, 128], BF16) for i in range(2)]
        rp_rel = [None, None]
        yt = sc.sb("yt", [128, 512], F32)
        y2 = sc.sb("y2", [128, 512], F32)
        sg = sc.sb("sg", [128, 512], F32)
        blocks = BLK_OWN
        bank_rel = [None] * 8
        nj = 0
        for cc in range(16):
            s = cc % 2
            d = [kb_rel[s]] if kb_rel[s] else []
            cx.dma(kl[s], kb[s][:, 0, :, :], T["Kbd"][cc], deps=d)
            t_kl = cx.dma(kl[s], kb[s][:, 1, :, :], T["Rk"][cc])
            t_rp = None
            for j in range(8):
                d = [t_kl]
                if j == 0 and rp_rel[s]:
                    d.append(rp_rel[s])
                t_rp = cx.tt("pool", rp[s][:, j, :, :], kb[s][:, 1, j, :].unsqueeze(1).broadcast_to([128, 8, 128]), G.cmaskb[:], ALU.mult, deps=d)
            bb = []
            for bi in range(len(blocks)):
                b = banks[nj % 8]
                nj += 1
                bb.append(b)
            first_deps = [t_kl, t_rp, t_state] + [bank_rel[b] for b in bb if bank_rel[b]]
            first = True
            for tau in range(8):
                for bi, (c0, n) in enumerate(blocks):
                    ncb = n // 8
                    b = bb[bi]
                    cx.pe(lambda E, b=b, tau=tau, c0=c0, n=n: E.matmul(
                        ps[:, b, 0:n].rearrange("p (c j) -> p c j", j=8)[:, :, tau:8],
                        lhsT=kb[s][:, 0, tau, :],
                        rhs=uT[:, cc, c0:c0 + n].rearrange("p (c j) -> p c j", j=8)[:, :, 0:8 - tau],
                        start=(tau == 0), stop=False), deps=first_deps if first else ())
                    first = False
            tk = None
            for gl in range(8):
                g = cc * 8 + gl
                for j in range(8):
                    for bi, (c0, n) in enumerate(blocks):
                        ncb = n // 8
                        cb = c0 // 8
                        b = bb[bi]
                        lastmm = (gl == 7 and j == 7)
                        tk = cx.pe(lambda E, b=b, j=j, gl=gl, g=g, cb=cb, ncb=ncb, n=n, lastmm=lastmm: E.matmul(
                            ps[:, b, 0:n].rearrange("p (c j) -> p c j", j=8)[:, :, j],
                            lhsT=rp[s][:, j, gl, :],
                            rhs=Hst[:, g, cb:cb + ncb],
                            start=False, stop=lastmm), tick=(lastmm and bi == len(blocks) - 1))
            kb_rel[s] = tk
            rp_rel[s] = tk
            dcol = G.pcol[:, PC["ssm_d"] + cc:PC["ssm_d"] + cc + 1]
            for bi, (c0, n) in enumerate(blocks):
                b = bb[bi]
                cx.stt("dve", yt[:, 0:n], uT[:, cc, c0:c0 + n], dcol, ps[:, b, 0:n], ALU.mult, ALU.add, deps=[tk])
                cx.tt("dve", y2[:, 0:n], yt[:, 0:n], yt[:, 0:n], ALU.mult)
                cx.ts("dve", y2[:, 0:n], y2[:, 0:n], 0.044715 * 1.5957691216057308, 1.5957691216057308, ALU.mult, ALU.add)
                t_w = cx.tt("dve", y2[:, 0:n], y2[:, 0:n], yt[:, 0:n], ALU.mult)
                t_sg = cx.act(sg[:, 0:n], y2[:, 0:n], AF.Sigmoid, deps=[t_w])
                t_z = cx.tt("dve", uT[:, cc, c0:c0 + n], yt[:, 0:n], sg[:, 0:n], ALU.mult, deps=[t_sg])
                bank_rel[b] = t_z
        cx.barrier()


def build_program(stop_after=None, force_dbg=False):
    nc = bass.Bass("TRN2", target_bir_lowering=False)
    T = {}

    def din(n, shp, dt=F32):
        T[n] = nc.dram_tensor(n, list(shp), dt, kind="ExternalInput").ap()

    dbg = (stop_after is not None) or force_dbg

    def dscr(n, shp, dt=F32):
        T[n] = nc.dram_tensor(n, list(shp), dt, kind=("ExternalOutput" if dbg else "Internal")).ap()

    din("xown", [NT, D])
    din("xprev", [NPREV, D])
    din("memt", [NMEM, D])
    din("consts", [128, 266])
    din("cmask", [128, 8, 128])
    din("pvec", [NPC, 128])
    din("ssmin", [128, 384 + 8192])
    order = ["setup", "prev", "win", "ssm", "glu", "wout", "attn", "down0", None]
    lvl = order.index(stop_after)
    if lvl >= 1:
        din("w_in", [D, 8192])
    if lvl >= 4:
        din("glu_w", [2048, 2048])
    if lvl >= 5:
        din("w_out", [D, D])
    if lvl >= 6:
        din("wq", [D, D])
        din("wk", [D, D])
        din("wv", [D, D])
        din("wo", [D, D])
    if lvl >= 7:
        din("w_up", [D, 2 * DFF])
        din("w_down", [DFF, D])
    T["out"] = nc.dram_tensor("out", [NTOK, D], F32, kind="ExternalOutput").ap()
    dscr("SkT", [16, 128, 8, 128], BF16)
    dscr("SJkT", [16, 128, 8, 128], BF16)
    dscr("Rk", [16, 128, 8, 128], BF16)
    dscr("Kbd", [16, 128, 8, 128], BF16)
    dscr("xT", [D, NT], F32)
    dscr("ymix", [D, NT], F32)
    dscr("uTp", [2048, NPREV], BF16)
    dscr("uTo", [2048, NT], BF16)
    dscr("qT", [D, NT], BF16)
    dscr("oT", [D, NT], BF16)
    dscr("actT", [DFF, NT], BF16)
    dscr("rstd", [4, NT], F32)
    if dbg:
        dscr("dbg_h", [128, 128], F32)
        dscr("dbg_z", [2048, NT], BF16)

    cx = Ctx(nc)
    with cx.gs:
        G = cx.scope()
        with G:
            load_consts(cx, G, T)
            ssm_setup(cx, G, T)
            if stop_after == "setup":
                return nc
            mixer(cx, G, T, stop_after)
            if stop_after in ("prev", "win", "ssm", "glu"):
                return nc
            rest(cx, G, T, stop_after)
    return nc


def gemm_u(cx, G, AT, T, dst, blocks, N):
    with cx.scope() as sc:
        ps = sc.ps("ps", [128, 8, 512], F32)
        st = [sc.dsem("st%d" % i) for i in range(2)]
        ust = [sc.sb("ust%d" % i, [128, N], BF16) for i in range(2)]
        rel = [None, None]
        state = {}

        def epi(mi, bi, bank, n, c0, tk):
            s = mi % 2
            deps = [tk]
            if bi == 0 and rel[s]:
                deps.append(rel[s])
            t = cx.cp("act", ust[s][:, c0:c0 + n], bank, deps=deps)
            state["last"] = t
            return t

        def post(mi):
            s = mi % 2
            rel[s] = cx.dma(st[s], dst[mi * 128:(mi + 1) * 128, :], ust[s][:, 0:N], deps=[state["last"]])

        run_gemm(cx, sc, ps, list(range(8)), AT, 32, T["w_in"], 0, [128 * m for m in range(16)], blocks, epi, post=post)
        cx.barrier()


def gemm_conv(cx, G, AT, T):
    with cx.scope() as sc:
        ps = sc.ps("ps", [128, 8, 512], F32)
        st = sc.dsem("st")
        st2 = sc.dsem("st2")
        cv = sc.sb("cv", [128, 2 + NT], F32)
        gb = sc.sb("gb", [128, NT], F32)
        yc = sc.sb("yc", [128, NT], F32)
        acc = sc.sb("acc", [128, NT], F32)
        t0 = cx.memset("dve", cv[:, 0:2], 0.0)
        cx.memset("dve", acc[:], 0.0)
        state = {"cv_rel": None, "gb_rel": None, "last": {}}
        coltiles = []
        for i in range(16):
            coltiles += [4096 + 128 * i, 6144 + 128 * i, 2048 + 128 * i]

        def epi(mi, bi, bank, n, c0, tk):
            i, which = mi // 3, mi % 3
            if which == 0:
                deps = [tk]
                if bi == 0 and state["cv_rel"]:
                    deps.append(state["cv_rel"])
                t = cx.cp("act", cv[:, 2 + c0:2 + c0 + n], bank, deps=deps)
                state["last"][(0, bi)] = t
            elif which == 1:
                t = cx.tt("dve", cv[:, 2 + c0:2 + c0 + n], bank, cv[:, 2 + c0:2 + c0 + n], ALU.mult, deps=[tk, state["last"][(0, bi)]])
                state["last"][1] = t
            else:
                deps = [tk]
                if bi == 0 and state["gb_rel"]:
                    deps += state["gb_rel"]
                t = cx.cp("act", gb[:, c0:c0 + n], bank, deps=deps)
                state["last"][2] = t
            return t

        def post(mi):
            i, which = mi // 3, mi % 3
            if which != 2:
                return
            w0 = G.pcol[:, PC["cw0"] + i:PC["cw0"] + i + 1]
            w1 = G.pcol[:, PC["cw1"] + i:PC["cw1"] + i + 1]
            w2 = G.pcol[:, PC["cw2"] + i:PC["cw2"] + i + 1]
            cx.ts("dve", yc[:], cv[:, 2:2 + NT], w2, None, ALU.mult, deps=[state["last"][1], state["last"][2]])
            cx.stt("dve", yc[:], cv[:, 1:1 + NT], w1, yc[:], ALU.mult, ALU.add)
            t_c = cx.stt("dve", yc[:], cv[:, 0:NT], w0, yc[:], ALU.mult, ALU.add)
            state["cv_rel"] = t_c
            t_y = cx.tt("dve", gb[:], gb[:], yc[:], ALU.mult)
            t_s = cx.act(yc[:], gb[:], AF.Square, deps=[t_y])
            t_a = cx.tt("pool", acc[:], acc[:], yc[:], ALU.add, deps=[t_s])
            cx.wait("dve", t_a)
            t_st = cx.dma(st, T["ymix"][2048 + 128 * i:2048 + 128 * (i + 1), :], gb[:], deps=[t_y])
            state["gb_rel"] = [t_st, t_s]
            state["acc"] = t_a

        run_gemm(cx, sc, ps, list(range(7)), AT, 32, T["w_in"], 0, coltiles, BLK_OWN, epi, post=post)
        ssq_finalize(cx, sc, G, ps, 7, acc, BLK_OWN, 2048.0, T["rstd"][1:2, :], st2, [state["acc"]], work=yc)
        cx.barrier()


def gemm_glu(cx, G, zT, T):
    with cx.scope() as sc:
        ps = sc.ps("ps", [128, 8, 512], F32)
        st = [sc.dsem("st%d" % i) for i in range(2)]
        st2 = sc.dsem("st2")
        gt = sc.sb("gt", [128, 512], F32)
        yst = [sc.sb("yst%d" % i, [128, NT], F32) for i in range(2)]
        sq = sc.sb("sq", [128, NT], F32)
        acc = sc.sb("acc", [128, NT], F32)
        cx.memset("dve", acc[:], 0.0)
        rel = [None, None]
        state = {}

        def epi(mi, bi, bank, n, c0, tk):
            s = mi % 2
            bcol = G.pcol[:, PC["glu_b"] + mi:PC["glu_b"] + mi + 1]
            t_g = cx.act(gt[:, 0:n], bank, AF.Sigmoid, bias=bcol, deps=[tk] + ([state["y"]] if "y" in state else []))
            deps = [t_g]
            if bi == 0 and rel[s]:
                deps += rel[s]
            state["y"] = cx.tt("dve", yst[s][:, c0:c0 + n], zT[:, mi, c0:c0 + n], gt[:, 0:n], ALU.mult, deps=deps)
            return t_g

        def post(mi):
            s = mi % 2
            d = [state["y"]] + ([state["acc"]] if "acc" in state else [])
            t_s = cx.act(sq[:], yst[s][:], AF.Square, deps=d)
            state["acc"] = cx.tt("pool", acc[:], acc[:], sq[:], ALU.add, deps=[t_s])
            t_st = cx.dma(st[s], T["ymix"][128 * mi:128 * (mi + 1), :], yst[s][:], deps=[state["y"]])
            rel[s] = [t_st, t_s]

        run_gemm(cx, sc, ps, list(range(7)), zT, 16, T["glu_w"], 0, [128 * m for m in range(16)], BLK_OWN, epi, post=post)
        ssq_finalize(cx, sc, G, ps, 7, acc, BLK_OWN, 2048.0, T["rstd"][0:1, :], st2, [state["acc"]], work=sq)
        cx.barrier()


def dbg_dump(cx, T, name, ap):
    with cx.scope() as sc:
        d = sc.dsem("dbg")
        cx.serial = True
        cx.dma(d, T[name], ap)
        cx.barrier()
        cx.serial = False


def mixer(cx, G, T, stop_after):
    tiles_prev = [(128 * i, 128, 128 * i) for i in range(16)]
    tiles_own = [(0, 8, 0)] + [(8 + 128 * i, 128, 8 + 128 * i) for i in range(16)]
    with cx.scope() as s1:
        AT = s1.sb("AT", [128, 32, NT], BF16)
        load_norm(cx, G, AT, T["xprev"], tiles_prev, PC["g_mix"])
        gemm_u(cx, G, AT, T, T["uTp"], BLK_PREV, NPREV)
    with cx.scope() as s2:
        uT = s2.sb("uT", [128, 16, NT], BF16)
        plain_load(cx, uT, T["uTp"], 16, NPREV)
        ssm_main(cx, G, s2, uT, T, NPREV, own=False)
    if stop_after == "prev":
        dbg_dump(cx, T, "dbg_h", G.h1[:])
        return
    with cx.scope() as s3:
        AT = s3.sb("AT", [128, 32, NT], BF16)
        load_norm(cx, G, AT, T["xown"], tiles_own, PC["g_mix"], xT_out=T["xT"])
        gemm_u(cx, G, AT, T, T["uTo"], BLK_OWN, NT)
        gemm_conv(cx, G, AT, T)
    if stop_after == "win":
        return
    with cx.scope() as s4:
        uT = s4.sb("uT", [128, 16, NT], BF16)
        plain_load(cx, uT, T["uTo"], 16, NT)
        ssm_main(cx, G, s4, uT, T, NT, own=True)
        if stop_after == "ssm":
            with cx.scope() as sc:
                d = sc.dsem("dbg")
                cx.serial = True
                cx.dma(d, T["dbg_z"].rearrange("(kc p) n -> p kc n", p=128), uT[:])
                cx.barrier()
                cx.serial = False
            return
        gemm_glu(cx, G, uT, T)


def gemm_resid(cx, G, AT, KC, W, k0, T, final_ssq):
    with cx.scope() as sc:
        ps = sc.ps("ps", [128, 8, 512], F32)
        ld = [sc.dsem("xl%d" % i) for i in range(2)]
        st = [sc.dsem("xs%d" % i) for i in range(2)]
        st2 = sc.dsem("st2")
        xr = [sc.sb("xr%d" % i, [128, NT], F32) for i in range(2)]
        rel = [None, None]
        ldt = [None, None]
        state = {}
        if final_ssq:
            sq = sc.sb("sq", [128, NT], F32)
            acc = sc.sb("acc", [128, NT], F32)
            cx.memset("dve", acc[:], 0.0)

        def pre(mi):
            s = mi % 2
            ldt[s] = cx.dma(ld[s], xr[s][:], T["xT"][128 * mi:128 * (mi + 1), :], deps=rel[s] if rel[s] else [])

        def epi(mi, bi, bank, n, c0, tk):
            s = mi % 2
            t = cx.tt("dve", xr[s][:, c0:c0 + n], bank, xr[s][:, c0:c0 + n], ALU.add, deps=[tk, ldt[s]])
            state["last"] = t
            return t

        def post(mi):
            s = mi % 2
            r = []
            if final_ssq:
                d = [state["last"]] + ([state["acc"]] if "acc" in state else [])
                t_s = cx.act(sq[:], xr[s][:], AF.Square, deps=d)
                state["acc"] = cx.tt("pool", acc[:], acc[:], sq[:], ALU.add, deps=[t_s])
                r.append(t_s)
            t_st = cx.dma(st[s], T["xT"][128 * mi:128 * (mi + 1), :], xr[s][:], deps=[state["last"]])
            r.append(t_st)
            rel[s] = r

        banks = list(range(7)) if final_ssq else list(range(8))
        run_gemm(cx, sc, ps, banks, AT, KC, W, k0, [128 * m for m in range(32)], BLK_OWN, epi, pre=pre, post=post)
        if final_ssq:
            ssq_finalize(cx, sc, G, ps, 7, acc, BLK_OWN, float(D), T["rstd"][2:3, :], st2, [state["acc"]], work=sq)
        cx.barrier()


def gemm_store(cx, G, AT, W, dst):
    with cx.scope() as sc:
        ps = sc.ps("ps", [128, 8, 512], F32)
        st = [sc.dsem("st%d" % i) for i in range(2)]
        qs = [sc.sb("qs%d" % i, [128, NT], BF16) for i in range(2)]
        rel = [None, None]
        state = {}

        def epi(mi, bi, bank, n, c0, tk):
            s = mi % 2
            deps = [tk]
            if bi == 0 and rel[s]:
                deps.append(rel[s])
            t = cx.cp("act", qs[s][:, c0:c0 + n], bank, deps=deps)
            state["last"] = t
            return t

        def post(mi):
            s = mi % 2
            rel[s] = cx.dma(st[s], dst[mi * 128:(mi + 1) * 128, :], qs[s][:], deps=[state["last"]])

        run_gemm(cx, sc, ps, list(range(8)), AT, 32, W, 0, [128 * m for m in range(32)], BLK_OWN, epi, post=post)
        cx.barrier()


def gemm_up(cx, G, AT, T):
    with cx.scope() as sc:
        ps = sc.ps("ps", [128, 8, 512], F32)
        st = [sc.dsem("st%d" % i) for i in range(2)]
        a_sb = sc.sb("a_sb", [128, 2 + NT], F32)
        g_sb = sc.sb("g_sb", [128, NT], F32)
        tt_ = sc.sb("tconv", [128, NT], F32)
        ast = [sc.sb("ast%d" % i, [128, NT], BF16) for i in range(2)]
        cx.memset("dve", a_sb[:, 0:2], 0.0)
        rel = [None, None]
        state = {"a_rel": None, "g_rel": None}
        coltiles = []
        for i in range(86):
            coltiles += [128 * i, DFF + 128 * i]

        def epi(mi, bi, bank, n, c0, tk):
            i, which = mi // 2, mi % 2
            if which == 0:
                deps = [tk]
                if bi == 0 and state["a_rel"]:
                    deps.append(state["a_rel"])
                t = cx.cp("act", a_sb[:, 2 + c0:2 + c0 + n], bank, deps=deps)
                state["la"] = t
            else:
                deps = [tk]
                if bi == 0 and state["g_rel"]:
                    deps.append(state["g_rel"])
                t = cx.cp("act", g_sb[:, c0:c0 + n], bank, deps=deps)
                state["lg"] = t
            return t

        def post(mi):
            i, which = mi // 2, mi % 2
            if which != 1:
                return
            s = i % 2
            w0 = G.pcol[:, PC["fw0"] + i:PC["fw0"] + i + 1]
            w1 = G.pcol[:, PC["fw1"] + i:PC["fw1"] + i + 1]
            w2 = G.pcol[:, PC["fw2"] + i:PC["fw2"] + i + 1]
            cb = G.pcol[:, PC["fcb"] + i:PC["fcb"] + i + 1]
            cx.ts("dve", a_sb[:, 2:2 + HALO], a_sb[:, 2:2 + HALO], G.flag, None, ALU.mult, deps=[state["la"], state["lg"]])
            cx.ts("dve", tt_[:], a_sb[:, 2:2 + NT], w2, cb, ALU.mult, ALU.add)
            cx.stt("dve", tt_[:], a_sb[:, 1:1 + NT], w1, tt_[:], ALU.mult, ALU.add)
            t_c = cx.stt("dve", tt_[:], a_sb[:, 0:NT], w0, tt_[:], ALU.mult, ALU.add)
            state["a_rel"] = t_c
            t_s = cx.act(tt_[:], tt_[:], AF.Silu, deps=[t_c])
            t_m = cx.tt("dve", ast[s][:], tt_[:], g_sb[:], ALU.mult, deps=[t_s] + ([rel[s]] if rel[s] else []))
            state["g_rel"] = t_m
            rel[s] = cx.dma(st[s], T["actT"][128 * i:128 * (i + 1), :], ast[s][:], deps=[t_m])

        run_gemm(cx, sc, ps, list(range(8)), AT, 32, T["w_up"], 0, coltiles, BLK_OWN, epi, post=post)
        cx.barrier()


def attention(cx, G, T):
    with cx.scope() as sc:
        kT = sc.sb("kT", [128, 32, NMEM], BF16)
        vsb = sc.sb("vsb", [128, 2, D], BF16)
        with cx.scope() as s1:
            hmT = s1.sb("hmT", [128, 32, NMEM], BF16)
            load_norm(cx, G, hmT, T["memt"], [(0, 128, 0), (128, 128, 128)], PC["g_mem"])
            with cx.scope() as s2:
                ps = s2.ps("ps", [128, 8, 512], F32)

                def epi(mi, bi, bank, n, c0, tk):
                    return cx.cp("act", kT[:, mi, 0:NMEM], bank, deps=[tk])

                run_gemm(cx, s2, ps, list(range(8)), hmT, 32, T["wk"], 0, [128 * m for m in range(32)], [(0, NMEM)], epi)
                cx.barrier()
            with cx.scope() as s2:
                ps = s2.ps("ps", [128, 8, 512], F32)
                g = Gemm(cx, s2, 32)
                g.load(T["wv"], 0, 0, 0)
                bank_rel = [None] * 8
                nj = 0
                for mi in range(32):
                    if mi + 1 < 32:
                        g.load(T["wv"], 0, 128 * (mi + 1), mi + 1)
                    wb, wt = g.loaded[mi][0], g.loaded[mi][1]
                    tk = None
                    for tt_i in range(2):
                        b = nj % 8
                        nj += 1
                        deps = [wt] + ([bank_rel[b]] if bank_rel[b] else [])
                        for kc in range(32):
                            tk = cx.pe(lambda E, b=b, kc=kc, tt_i=tt_i: E.matmul(ps[:, b, 0:128], lhsT=hmT[:, kc, tt_i * 128:(tt_i + 1) * 128],
                                                                                 rhs=wb[:, kc, :], start=(kc == 0), stop=(kc == 31)),
                                       deps=deps if kc == 0 else (), tick=(kc == 31))
                        bank_rel[b] = cx.cp("act", vsb[:, tt_i, 128 * mi:128 * (mi + 1)], ps[:, b, 0:128], deps=[tk])
                    g.release(mi, tk)
                cx.barrier()
        with cx.scope() as s3:
            ps = s3.ps("ps", [128, 6, 512], F32)
            psb = s3.ps("psb", [128, 2, 1024], BF16)
            ql = s3.dsem("ql")
            od = s3.dsem("od")
            qh = s3.sb("qh", [128, 8, NT], BF16)
            pT = s3.sb("pT", [128, 2, NT], BF16)
            es = s3.sb("es", [128, NMEM], F32)
            pb = s3.sb("pb", [128, NMEM], BF16)
            mx = s3.sb("mx", [128, 1], F32)
            nmx = s3.sb("nmx", [128, 1], F32)
            sm = s3.sb("sm", [128, 1], F32)
            rs = s3.sb("rs", [128, 1], F32)
            ost = s3.sb("ost", [128, NT], BF16)
            tiles = [(0, 8)] + [(8 + 128 * i, 128) for i in range(16)]
            scale = 1.0 / 32.0
            cx.serial = True
            for h in range(4):
                cx.dma(ql, qh[:], T["qT"][1024 * h:1024 * (h + 1), :].rearrange("(kc p) n -> p kc n", p=128))
                for ti, (c0, nr) in enumerate(tiles):
                    b = ti % 3
                    for dc in range(8):
                        cx.pe(lambda E, b=b, dc=dc, c0=c0, nr=nr: E.matmul(ps[0:nr, b, 0:NMEM], lhsT=qh[:, dc, c0:c0 + nr], rhs=kT[:, h * 8 + dc, :],
                                                                          start=(dc == 0), stop=(dc == 7)))
                    cx.op("dve", lambda E: E.reduce_max(out=mx[0:nr, :], in_=ps[0:nr, b, 0:NMEM], axis=AX.X))
                    cx.ts("dve", nmx[0:nr, :], mx[0:nr, :], -scale, None, ALU.mult)
                    cx.act(es[0:nr, :], ps[0:nr, b, 0:NMEM], AF.Exp, bias=nmx[0:nr, :], scale=scale, accum=sm[0:nr, :])
                    cx.op("dve", lambda E: E.reciprocal(out=rs[0:nr, :], in_=sm[0:nr, :]))
                    cx.ts("dve", pb[0:nr, :], es[0:nr, :], rs[0:nr, :], None, ALU.mult)
                    bb = ti % 2
                    for kc in range(2):
                        cx.pe(lambda E, bb=bb, kc=kc, nr=nr: E.transpose(psb[:, bb, kc * 128:kc * 128 + nr], pb[0:nr, kc * 128:(kc + 1) * 128],
                                                                        G.identb[0:nr, 0:nr]))
                    cx.cp("act", pT[:, :, c0:c0 + nr], psb[:, bb, 0:256].rearrange("p (k m) -> p k m", m=128)[:, :, 0:nr])
                for dvt in range(8):
                    for bi, (c0, n) in enumerate(BLK_OWN):
                        b = 3 + (bi % 3)
                        for kc in range(2):
                            cx.pe(lambda E, b=b, kc=kc, c0=c0, n=n: E.matmul(ps[:, b, 0:n], lhsT=vsb[:, kc, h * 1024 + dvt * 128:h * 1024 + (dvt + 1) * 128],
                                                                            rhs=pT[:, kc, c0:c0 + n], start=(kc == 0), stop=(kc == 1)))
                        cx.cp("act", ost[:, c0:c0 + n], ps[:, b, 0:n])
                    cx.dma(od, T["oT"][128 * (h * 8 + dvt):128 * (h * 8 + dvt + 1), :], ost[:])
            cx.barrier()
            cx.serial = False


def final_out(cx, G, T):
    with cx.scope() as sc:
        ps = sc.ps("ps", [128, 8, 512], F32)
        ld = [sc.dsem("fl%d" % i) for i in range(2)]
        st = [sc.dsem("fs%d" % i) for i in range(2)]
        rl = sc.dsem("rl")
        xf = [sc.sb("xf%d" % i, [128, 32, 128], F32) for i in range(2)]
        orow = [sc.sb("orow%d" % i, [128, D], F32) for i in range(2)]
        rb = sc.sb("rb", [128, NT], F32)
        t_rb = cx.dma(rl, rb[:], T["rstd"][2:3, :].broadcast_to([128, NT]))
        xf_rel = [None, None]
        or_rel = [None, None]
        bank_rel = [None] * 8
        nj = 0
        gcol = PC["g_final"]
        for ti in range(16):
            s = ti % 2
            c0 = 8 + 128 * ti
            t_ld = cx.dma(ld[s], xf[s][:], T["xT"].rearrange("(kc p) n -> p kc n", p=128)[:, :, c0:c0 + 128], deps=[xf_rel[s]] if xf_rel[s] else [])
            cx.tt("dve", xf[s][:], xf[s][:], rb[:, c0:c0 + 128].unsqueeze(1).broadcast_to([128, 32, 128]), ALU.mult, deps=[t_ld, t_rb])
            t_n = cx.tt("dve", xf[s][:], xf[s][:], G.pcol[:, gcol:gcol + 32].unsqueeze(2).broadcast_to([128, 32, 128]), ALU.mult)
            tk = None
            last_ev = None
            for grp in range(8):
                b = nj % 8
                nj += 1
                deps = [t_n] + ([bank_rel[b]] if bank_rel[b] else [])
                for j in range(4):
                    kc = grp * 4 + j
                    tk = cx.pe(lambda E, b=b, j=j, kc=kc, s=s: E.transpose(ps[:, b, j * 128:(j + 1) * 128], xf[s][:, kc, :], G.ident),
                               deps=deps if j == 0 else (), tick=(j == 3))
                d2 = [tk]
                if grp == 0 and or_rel[s]:
                    d2.append(or_rel[s])
                last_ev = cx.cp("act", orow[s][:, grp * 512:(grp + 1) * 512], ps[:, b, :], deps=d2)
                bank_rel[b] = last_ev
            xf_rel[s] = tk
            or_rel[s] = cx.dma(st[s], T["out"][128 * ti:128 * (ti + 1), :], orow[s][:], deps=[last_ev])
        cx.barrier()


def rest(cx, G, T, stop_after):
    with cx.scope() as s:
        AT = s.sb("AT", [128, 32, NT], BF16)
        norm_load(cx, G, AT, T["ymix"], 32, NT, PC["g_omix"], T["rstd"], [0] * 16 + [1] * 16)
        gemm_resid(cx, G, AT, 32, T["w_out"], 0, T, True)
    if stop_after == "wout":
        return
    with cx.scope() as s:
        AT = s.sb("AT", [128, 32, NT], BF16)
        norm_load(cx, G, AT, T["xT"], 32, NT, PC["g_xattn"], T["rstd"], [2] * 32)
        gemm_store(cx, G, AT, T["wq"], T["qT"])
    attention(cx, G, T)
    with cx.scope() as s:
        AT = s.sb("AT", [128, 32, NT], BF16)
        plain_load(cx, AT, T["oT"], 32, NT)
        gemm_resid(cx, G, AT, 32, T["wo"], 0, T, True)
    if stop_after == "attn":
        return
    with cx.scope() as s:
        AT = s.sb("AT", [128, 32, NT], BF16)
        norm_load(cx, G, AT, T["xT"], 32, NT, PC["g_ffn"], T["rstd"], [2] * 32)
        gemm_up(cx, G, AT, T)
    for p, (k0, kc) in enumerate([(0, 29), (29, 29), (58, 28)]):
        with cx.scope() as s:
            AT = s.sb("AT", [128, 29, NT], BF16)
            plain_load(cx, AT, T["actT"][k0 * 128:(k0 + kc) * 128, :], kc, NT)
            gemm_resid(cx, G, AT, kc, T["w_down"], k0, T, p == 2)
        if stop_after == "down0":
            return
    final_out(cx, G, T)


def _host_inputs(inp, cores):
    f = np.float32
    x = np.asarray(inp["x"], f)
    mem = np.asarray(inp["mem"], f)
    ident = np.eye(128, dtype=f)
    bd = np.kron(np.eye(8, dtype=f), np.ones((16, 16), f))
    mask8 = np.kron(np.eye(8, dtype=f), np.ones((16, 1), f))
    sgn = np.concatenate([-np.ones((64, 1), f), np.ones((64, 1), f)], 0)
    cmask = np.zeros((128, 8, 128), f)
    for gl in range(8):
        cmask[:, gl, gl * 16:(gl + 1) * 16] = 1.0
    vecs = {
        "g_mix": inp["norm_mix_g"][0], "ssm_d": inp["ssm_d"][0], "glu_b": inp["ssm_glu_b"][0],
        "cw0": inp["conv_w"][0, 0], "cw1": inp["conv_w"][0, 1], "cw2": inp["conv_w"][0, 2],
        "g_omix": np.concatenate([inp["out_norm_ssm_g"][0], inp["out_norm_conv_g"][0]]),
        "g_xattn": inp["norm_xattn_g"][0], "g_mem": inp["norm_mem_g"][0], "g_ffn": inp["norm_ffn_g"][0],
        "fw0": inp["ffn_conv_w"][0, 0], "fw1": inp["ffn_conv_w"][0, 1], "fw2": inp["ffn_conv_w"][0, 2],
        "fcb": inp["ffn_conv_b"][0], "g_final": inp["norm_final_g"],
    }
    pvec = np.zeros((NPC, 128), f)
    for k, v in vecs.items():
        v = np.asarray(v, f).reshape(-1, 128)
        pvec[PC[k]:PC[k] + v.shape[0]] = v
    lr = np.asarray(inp["ssm_lambda_re"][0], f).T
    li = np.asarray(inp["ssm_lambda_im"][0], f).T
    ls = np.broadcast_to(np.asarray(inp["ssm_log_step"][0], f)[None, :], (128, 128))
    br = np.asarray(inp["ssm_b_re"][0], f).transpose(1, 0, 2).reshape(64, 2048)
    bi = np.asarray(inp["ssm_b_im"][0], f).transpose(1, 0, 2).reshape(64, 2048)
    cr = np.asarray(inp["ssm_c_re"][0], f).transpose(2, 0, 1).reshape(64, 2048)
    ci = np.asarray(inp["ssm_c_im"][0], f).transpose(2, 0, 1).reshape(64, 2048)
    ssmin = np.concatenate([
        np.concatenate([lr, lr], 0), np.concatenate([li, li], 0), ls,
        np.concatenate([br, bi], 0), np.concatenate([bi, br], 0),
        np.concatenate([cr, ci], 0), np.concatenate([ci, cr], 0)], 1).astype(f)
    shared = {
        "cmask": cmask, "pvec": pvec, "ssmin": np.ascontiguousarray(ssmin),
        "w_in": np.asarray(inp["w_in"][0], f), "glu_w": np.asarray(inp["ssm_glu_w"][0], f),
        "w_out": np.asarray(inp["w_out"][0], f), "wq": np.asarray(inp["xattn_wq"][0], f),
        "wk": np.asarray(inp["xattn_wk"][0], f), "wv": np.asarray(inp["xattn_wv"][0], f),
        "wo": np.asarray(inp["xattn_wo"][0], f), "w_up": np.asarray(inp["ffn_w_up"][0], f),
        "w_down": np.asarray(inp["ffn_w_down"][0], f),
    }
    maps = []
    for c in cores:
        b, half = c // 2, c % 2
        xown = np.zeros((NT, D), f)
        xprev = np.zeros((NPREV, D), f)
        if half == 0:
            xown[HALO:] = x[b, 0:NTOK]
        else:
            xown[:] = x[b, NTOK - HALO:2 * NTOK]
            xprev[HALO:] = x[b, 0:NTOK - HALO]
        consts = np.concatenate([ident, bd, mask8, sgn, np.full((128, 1), float(half), f)], 1).astype(f)
        m = dict(shared)
        m.update({"xown": xown, "xprev": xprev, "memt": np.ascontiguousarray(mem[b]), "consts": np.ascontiguousarray(consts)})
        maps.append(m)
    return maps


_NC_CACHE = {}


def kernel(**inputs):
    cores = list(range(8))
    maps = _host_inputs(inputs, cores)
    if "nc" not in _NC_CACHE:
        _NC_CACHE["nc"] = build_program(None)
    res = run_bass_kernel_spmd(_NC_CACHE["nc"], maps, core_ids=cores)
    out = np.zeros((4, 2 * NTOK, D), np.float32)
    for c in cores:
        out[c // 2, (c % 2) * NTOK:(c % 2 + 1) * NTOK] = res.results[c]["out"]
    return out
```

```python
import math, contextlib
import numpy as np
import concourse.bass as bass
import concourse.mybir as mybir
from concourse.bass_utils import run_bass_kernel_spmd

F32 = mybir.dt.float32
BF16 = mybir.dt.bfloat16
I32 = mybir.dt.int32
AF = mybir.ActivationFunctionType
ALU = mybir.AluOpType
AX = mybir.AxisListType

D = 4096
NTOK = 2048
HALO = 8
NT = NTOK + HALO
NPREV = 2048
NMEM = 256
DFF = 11008
EPS = 1e-6
BLK_OWN = [(0, 8)] + [(8 + 512 * i, 512) for i in range(4)]
BLK_PREV = [(512 * i, 512) for i in range(4)]
TWO_PI = 2.0 * math.pi
PI_SAFE = 3.14159

PC = {}
_o = 0
for _n, _l in [("g_mix", 32), ("ssm_d", 16), ("glu_b", 16), ("cw0", 16), ("cw1", 16), ("cw2", 16),
               ("g_omix", 32), ("g_xattn", 32), ("g_mem", 32), ("g_ffn", 32),
               ("fw0", 86), ("fw1", 86), ("fw2", 86), ("fcb", 86), ("g_final", 32)]:
    PC[_n] = _o
    _o += _l
NPC = 640
assert _o <= NPC


class DSem:
    _n = 0

    def __init__(self, sem):
        self.sem = sem
        self.cnt = 0
        DSem._n += 1
        self.uid = DSem._n


class Scope:
    def __init__(self, cx):
        self.cx = cx
        self.es = contextlib.ExitStack()
        self.dsems = []

    def __enter__(self):
        self.es.__enter__()
        return self

    def __exit__(self, *a):
        for d in self.dsems:
            self.cx.dsems.remove(d)
        return self.es.__exit__(*a)

    def sb(self, name, shape, dt=F32):
        self.cx.uid += 1
        return self.es.enter_context(self.cx.nc.sbuf_tensor("%s_%d" % (name, self.cx.uid), list(shape), dt))

    def ps(self, name, shape, dt=F32):
        self.cx.uid += 1
        return self.es.enter_context(self.cx.nc.psum_tensor("%s_%d" % (name, self.cx.uid), list(shape), dt))

    def dsem(self, name):
        self.cx.uid += 1
        d = DSem(self.es.enter_context(self.cx.nc.semaphore("%s_%d" % (name, self.cx.uid))))
        self.dsems.append(d)
        self.cx.dsems.append(d)
        return d


class Ctx:
    def __init__(self, nc):
        self.nc = nc
        self.eng = {"pe": nc.tensor, "act": nc.scalar, "dve": nc.vector, "pool": nc.gpsimd, "sp": nc.sync}
        self.uid = 0
        self.gs = contextlib.ExitStack()
        self.sem = {}
        self.cnt = {}
        self.waited = {}
        for e in ("pe", "act", "dve", "pool", "sp"):
            self.sem[e] = self.gs.enter_context(nc.semaphore("tk_" + e))
            self.cnt[e] = 0
        self.dsems = []
        self.serial = False

    def scope(self):
        return Scope(self)

    def wait(self, cons, tick):
        if tick is None:
            return
        if tick[0] == "dma":
            d, v = tick[1], tick[2]
            k = (cons, "dma%d" % d.uid)
            if self.waited.get(k, 0) >= v:
                return
            self.eng[cons].wait_ge(d.sem, v)
            self.waited[k] = v
        else:
            prod, v = tick
            if v <= 0:
                return
            k = (cons, prod)
            if self.waited.get(k, 0) >= v:
                return
            self.eng[cons].wait_ge(self.sem[prod], v)
            self.waited[k] = v

    def _serial_waits(self, cons):
        for e in ("pe", "act", "dve", "pool"):
            if e != cons:
                self.wait(cons, (e, self.cnt[e]))
        for d in self.dsems:
            if d.cnt > 0:
                self.wait(cons, ("dma", d, d.cnt))

    def op(self, e, fn, deps=()):
        for d in deps:
            self.wait(e, d)
        if self.serial:
            self._serial_waits(e)
        c = self.cnt[e]
        self.wait(e, (e, c))
        inst = fn(self.eng[e])
        inst.then_inc(self.sem[e], 1)
        self.cnt[e] = c + 1
        return (e, c + 1)

    def pe(self, fn, deps=(), tick=False):
        for d in deps:
            self.wait("pe", d)
        if self.serial:
            self._serial_waits("pe")
            tick = True
        inst = fn(self.eng["pe"])
        if tick:
            inst.then_inc(self.sem["pe"], 1)
            self.cnt["pe"] += 1
            return ("pe", self.cnt["pe"])
        return None

    def dma(self, dsem, out, in_, deps=(), q="sp"):
        for d in deps:
            self.wait(q, d)
        if self.serial:
            self._serial_waits(q)
        self.eng[q].dma_start(out=out, in_=in_).then_inc(dsem.sem, 16)
        dsem.cnt += 16
        return ("dma", dsem, dsem.cnt)

    def barrier(self):
        for e in ("pe", "act", "dve", "pool"):
            self.wait("sp", (e, self.cnt[e]))
        for d in self.dsems:
            if d.cnt > 0:
                self.wait("sp", ("dma", d, d.cnt))
        for d in self.dsems:
            if d.cnt > 0:
                self.eng["sp"].sem_clear(d.sem)
                d.cnt = 0
                for k in [k for k in self.waited if k[1] == "dma%d" % d.uid]:
                    del self.waited[k]
        self.eng["sp"].sem_inc(self.sem["sp"], 1)
        self.cnt["sp"] += 1
        for e in ("pe", "act", "dve", "pool"):
            self.wait(e, ("sp", self.cnt["sp"]))

    def tt(self, e, out, a, b, op, deps=()):
        return self.op(e, lambda E: E.tensor_tensor(out=out, in0=a, in1=b, op=op), deps)

    def ts(self, e, out, a, s1, s2, op0, op1=None, deps=()):
        if op1 is None:
            return self.op(e, lambda E: E.tensor_scalar(out=out, in0=a, scalar1=s1, scalar2=None, op0=op0), deps)
        return self.op(e, lambda E: E.tensor_scalar(out=out, in0=a, scalar1=s1, scalar2=s2, op0=op0, op1=op1), deps)

    def stt(self, e, out, in0, scalar, in1, op0, op1, deps=()):
        return self.op(e, lambda E: E.scalar_tensor_tensor(out=out, in0=in0, scalar=scalar, in1=in1, op0=op0, op1=op1), deps)

    def cp(self, e, out, in_, deps=()):
        if e == "act":
            return self.op(e, lambda E: E.copy(out=out, in_=in_), deps)
        return self.op(e, lambda E: E.tensor_copy(out=out, in_=in_), deps)

    def act(self, out, in_, func, bias=None, scale=1.0, accum=None, deps=(), saturate=None):
        kw = {}
        if saturate is not None:
            kw["saturate"] = saturate
        if bias is not None:
            kw["bias"] = bias
        if accum is not None:
            kw["accum_out"] = accum
        return self.op("act", lambda E: E.activation(out=out, in_=in_, func=func, scale=scale, **kw), deps)

    def memset(self, e, ap, v, deps=()):
        return self.op(e, lambda E: E.memset(ap, v), deps)


def tmax(*ticks):
    return [t for t in ticks if t is not None]


def load_consts(cx, G, T):
    cx.serial = True
    G.consts = G.sb("consts", [128, 266], F32)
    G.cmaskb = G.sb("cmaskb", [128, 8, 128], BF16)
    G.mask8b = G.sb("mask8b", [128, 8], BF16)
    G.identb = G.sb("identb", [128, 128], BF16)
    G.ones = G.sb("ones", [128, 128], F32)
    G.pcol = G.sb("pcol", [128, NPC], F32)
    G.h1 = G.sb("h1", [128, 128], F32)
    G.h2 = G.sb("h2", [128, 128], F32)
    G.eps = G.sb("epsc", [128, 1], F32)
    G.a8 = G.sb("a8g", [128, 2, 128], F32)
    G.ident = G.consts[:, 0:128]
    G.bdmask = G.consts[:, 128:256]
    G.mask8 = G.consts[:, 256:264]
    G.sgn = G.consts[:, 264:265]
    G.flag = G.consts[:, 265:266]
    with cx.scope() as sc:
        ld = sc.dsem("ld")
        cmf = sc.sb("cmf", [128, 8, 128], F32)
        pv = sc.sb("pv", [128, 5, 128], F32)
        ps = sc.ps("ps", [128, 2, 512], F32)
        cx.dma(ld, G.consts[:], T["consts"])
        cx.dma(ld, cmf[:], T["cmask"])
        cx.dma(ld, pv[:], T["pvec"].rearrange("(a p) m -> p a m", p=128))
        cx.cp("dve", G.cmaskb[:], cmf[:])
        cx.cp("dve", G.mask8b[:], G.mask8)
        cx.cp("dve", G.identb[:], G.ident)
        cx.memset("dve", G.ones[:], 1.0)
        cx.memset("dve", G.eps[:], EPS)
        cx.memset("dve", G.h1[:], 0.0)
        cx.memset("dve", G.h2[:], 0.0)
        for a in range(5):
            bank, off = (0, a * 128) if a < 4 else (1, 0)
            cx.pe(lambda E, a=a, bank=bank, off=off: E.transpose(ps[:, bank, off:off + 128], pv[:, a, :], G.ident))
        cx.cp("dve", G.pcol[:, 0:512], ps[:, 0, :])
        cx.cp("dve", G.pcol[:, 512:640], ps[:, 1, 0:128])
        cx.barrier()
    cx.serial = False


def ssm_setup(cx, G, T):
    cx.serial = True
    with cx.scope() as sc:
        ld = sc.dsem("ld")
        st = sc.dsem("st")
        sin = sc.sb("ssmin", [128, 384 + 4 * 2048], F32)
        cx.dma(ld, sin[:], T["ssmin"])
        lr2 = sin[:, 0:128]
        li2 = sin[:, 128:256]
        lst = sin[:, 256:384]
        Bs = sin[:, 384:384 + 2048]
        Bx = sin[:, 384 + 2048:384 + 4096]
        Cs = sin[:, 384 + 4096:384 + 6144]
        Cx = sin[:, 384 + 6144:384 + 8192]
        tabs = sc.sb("tabs", [128, 9, 2, 128], F32)
        w = [sc.sb("w%d" % i, [128, 128], F32) for i in range(8)]
        wi = sc.sb("wi", [128, 128], I32)
        step, lrs, lis, mag, kf, kf2, r, sc_ = w
        cx.act(step[:], lst, AF.Exp)
        cx.tt("dve", lrs[:], lr2, step[:], ALU.mult)
        cx.tt("dve", lis[:], li2, step[:], ALU.mult)
        cx.memset("dve", tabs[:, 0, 0, :], 1.0)
        cx.memset("dve", tabs[:, 0, 1, :], 0.0)
        for k in range(1, 9):
            cx.act(mag[:], lrs[:], AF.Exp, scale=float(k))
            for which, shift in ((1, 0.0), (0, 0.25)):
                cx.ts("dve", kf[:], lis[:], k / TWO_PI, shift, ALU.mult, ALU.add)
                cx.cp("dve", wi[:], kf[:])
                cx.cp("dve", kf2[:], wi[:])
                cx.tt("dve", r[:], kf[:], kf2[:], ALU.subtract)
                cx.ts("dve", r[:], r[:], TWO_PI, PI_SAFE, ALU.mult, ALU.min)
                cx.ts("dve", r[:], r[:], -PI_SAFE, None, ALU.max)
                cx.act(sc_[:], r[:], AF.Sin)
                cx.tt("dve", tabs[:, k, which, :], mag[:], sc_[:], ALU.mult)
        den, nr, t1, t2, cr, ci, cisg, tmp = w
        cx.tt("dve", den[:], lr2, lr2, ALU.mult)
        cx.tt("dve", t1[:], li2, li2, ALU.mult)
        cx.tt("dve", den[:], den[:], t1[:], ALU.add)
        cx.op("dve", lambda E: E.reciprocal(out=den[:], in_=den[:]))
        cx.ts("dve", nr[:], tabs[:, 1, 0, :], -1.0, None, ALU.add)
        ni = tabs[:, 1, 1, :]
        cx.tt("dve", t1[:], nr[:], lr2, ALU.mult)
        cx.tt("dve", t2[:], ni, li2, ALU.mult)
        cx.tt("dve", t1[:], t1[:], t2[:], ALU.add)
        cx.tt("dve", cr[:], t1[:], den[:], ALU.mult)
        cx.tt("dve", t1[:], ni, lr2, ALU.mult)
        cx.tt("dve", t2[:], nr[:], li2, ALU.mult)
        cx.tt("dve", t1[:], t1[:], t2[:], ALU.subtract)
        cx.tt("dve", ci[:], t1[:], den[:], ALU.mult)
        cx.ts("dve", cisg[:], ci[:], G.sgn, None, ALU.mult)

        def bc(tab):
            return tab.unsqueeze(2).broadcast_to([128, 128, 16])

        def v3(ap):
            return ap.rearrange("p (g h) -> p g h", h=16)

        big = [sc.sb("big%d" % i, [128, 2048], F32) for i in range(5)]
        bbs, bbx, ta, tb, sk = big
        cx.tt("dve", v3(ta[:]), v3(Bs), bc(cr[:]), ALU.mult)
        cx.tt("dve", v3(tb[:]), v3(Bx), bc(cisg[:]), ALU.mult)
        cx.tt("dve", bbs[:], ta[:], tb[:], ALU.add)
        cx.tt("dve", v3(ta[:]), v3(Bx), bc(cr[:]), ALU.mult)
        cx.tt("dve", v3(tb[:]), v3(Bs), bc(cisg[:]), ALU.mult)
        cx.tt("dve", bbx[:], ta[:], tb[:], ALU.subtract)
        ps = sc.ps("ps", [128, 4, 512], F32)
        stg = [sc.sb("stg%d" % i, [128, 16, 128], BF16) for i in range(2)]
        stg_t = [None, None]
        nstg = [0]
        sa, sb_ = w[0], w[1]

        def emit_T(src, dst_dram, kidx):
            s = stg[nstg[0] % 2]
            nstg[0] += 1
            for q in range(4):
                for j in range(4):
                    cc = q * 4 + j
                    cx.pe(lambda E, q=q, j=j, cc=cc: E.transpose(ps[:, q, j * 128:(j + 1) * 128], src[:, cc * 128:(cc + 1) * 128], G.ident))
                cx.cp("act", s[:, q * 4:(q + 1) * 4, :], ps[:, q, :].rearrange("p (a m) -> p a m", m=128))
            cx.dma(st, dst_dram[:, :, kidx, :].rearrange("c p m -> p c m"), s[:])

        def emit_plain(src, dst_dram, kidx):
            s = stg[nstg[0] % 2]
            nstg[0] += 1
            cx.cp("act", s[:], src.rearrange("p (c m) -> p c m", m=128))
            cx.dma(st, dst_dram[:, :, kidx, :].rearrange("c p m -> p c m"), s[:])

        for k in range(8):
            ARk = tabs[:, k, 0, :]
            AIk = tabs[:, k, 1, :]
            cx.ts("dve", sa[:], AIk, G.sgn, None, ALU.mult)
            cx.tt("dve", v3(ta[:]), v3(bbs[:]), bc(ARk), ALU.mult)
            cx.tt("dve", v3(tb[:]), v3(bbx[:]), bc(sa[:]), ALU.mult)
            cx.tt("dve", sk[:], ta[:], tb[:], ALU.add)
            emit_T(sk, T["SkT"], k)
            cx.ts("dve", sa[:], ARk, G.sgn, None, ALU.mult)
            cx.tt("dve", v3(ta[:]), v3(bbx[:]), bc(sa[:]), ALU.mult)
            cx.tt("dve", v3(tb[:]), v3(bbs[:]), bc(AIk), ALU.mult)
            cx.tt("dve", sk[:], ta[:], tb[:], ALU.subtract)
            emit_T(sk, T["SJkT"], k)
        for tau in range(9):
            ARk = tabs[:, tau, 0, :]
            AIk = tabs[:, tau, 1, :]
            cx.ts("dve", sa[:], ARk, G.sgn, -1.0, ALU.mult, ALU.mult)
            cx.tt("dve", v3(ta[:]), v3(Cs), bc(sa[:]), ALU.mult)
            cx.tt("dve", v3(tb[:]), v3(Cx), bc(AIk), ALU.mult)
            cx.tt("dve", sk[:], ta[:], tb[:], ALU.subtract)
            if tau >= 1:
                emit_plain(sk[:], T["Rk"], tau - 1)
            if tau <= 7:
                s = stg[nstg[0] % 2]
                nstg[0] += 1
                for q in range(4):
                    for j in range(4):
                        cc = q * 4 + j
                        cx.pe(lambda E, q=q, j=j, cc=cc: E.matmul(ps[:, q, j * 128:(j + 1) * 128], lhsT=bbs[:, cc * 128:(cc + 1) * 128],
                                                                  rhs=sk[:, cc * 128:(cc + 1) * 128], start=True, stop=True))
                    cx.tt("dve", s[:, q * 4:(q + 1) * 4, :], ps[:, q, :].rearrange("p (a m) -> p a m", m=128),
                          G.bdmask.unsqueeze(1).broadcast_to([128, 4, 128]), ALU.mult)
                cx.dma(st, T["Kbd"][:, :, tau, :].rearrange("c p m -> p c m"), s[:])
        cx.cp("dve", G.a8[:], tabs[:, 8, :, :])
        cx.barrier()
    cx.serial = False


def load_norm(cx, G, AT, src, tiles, gcol, xT_out=None):
    with cx.scope() as sc:
        ld = [sc.dsem("ld%d" % i) for i in range(2)]
        st = [sc.dsem("st%d" % i) for i in range(2)]
        xtok = [sc.sb("xtok%d" % i, [128, D], F32) for i in range(2)]
        xT = [sc.sb("xTt%d" % i, [128, 32, 128], F32) for i in range(2)]
        junk = sc.sb("junk", [128, D], mybir.dt.float8e5)
        ssq = sc.sb("ssq", [128, 2], F32)
        rstd = sc.sb("rstd", [128, 2], F32)
        diag = [sc.sb("diag%d" % i, [128, 128], F32) for i in range(2)]
        rbs = [sc.sb("rbs%d" % i, [128, 128], F32) for i in range(2)]
        ps = sc.ps("ps", [128, 8, 512], F32)
        NB = 6
        bank_rel = [None] * NB
        slot_rel = [None, None]
        xT_rel = [[], []]
        rb_rel = [[], []]
        nj = 0
        for ti, (r0, nr, c0) in enumerate(tiles):
            sl = ti % 2
            t_ld = cx.dma(ld[sl], xtok[sl][0:nr, :], src[r0:r0 + nr, :], deps=[slot_rel[sl]] if slot_rel[sl] else [])
            cx.act(junk[0:nr, :], xtok[sl][0:nr, :], AF.Square, accum=ssq[0:nr, sl:sl + 1], deps=[t_ld] + rb_rel[sl], saturate=False)
            t_sq = cx.act(rstd[0:nr, sl:sl + 1], ssq[0:nr, sl:sl + 1], AF.Sqrt, bias=G.eps[0:nr, :], scale=1.0 / D)
            cx.op("dve", lambda E: E.reciprocal(out=rstd[0:nr, sl:sl + 1], in_=rstd[0:nr, sl:sl + 1]), deps=[t_sq])
            t_d = cx.ts("dve", diag[sl][0:nr, 0:nr], G.ident[0:nr, 0:nr], rstd[0:nr, sl:sl + 1], None, ALU.mult)
            ev_ticks = []
            tk = None
            for grp in range(8):
                b = nj % NB
                nj += 1
                deps = [t_ld]
                if bank_rel[b]:
                    deps.append(bank_rel[b])
                for j in range(4):
                    kc = grp * 4 + j
                    tk = cx.pe(lambda E, b=b, j=j, kc=kc: E.transpose(ps[:, b, j * 128:j * 128 + nr], xtok[sl][0:nr, kc * 128:(kc + 1) * 128],
                                                                      G.ident[0:nr, 0:nr]), deps=deps if j == 0 else (), tick=(j == 3))
                d2 = [tk]
                if grp == 0:
                    d2 += xT_rel[sl]
                t_ev = cx.cp("act", xT[sl][:, grp * 4:(grp + 1) * 4, 0:nr], ps[:, b, :].rearrange("p (a m) -> p a m", m=128)[:, :, 0:nr], deps=d2)
                bank_rel[b] = t_ev
                ev_ticks.append(t_ev)
            slot_rel[sl] = tk
            t_rb = cx.pe(lambda E: E.matmul(ps[:, 6 + sl, 0:nr], lhsT=G.ones[0:nr, :], rhs=diag[sl][0:nr, 0:nr], start=True, stop=True),
                         deps=[t_d], tick=True)
            t_rbs = cx.cp("act", rbs[sl][:, 0:nr], ps[:, 6 + sl, 0:nr], deps=[t_rb])
            last = None
            for kc in range(32):
                last = cx.stt("dve", AT[:, kc, c0:c0 + nr], xT[sl][:, kc, 0:nr], G.pcol[:, gcol + kc:gcol + kc + 1], rbs[sl][:, 0:nr],
                              ALU.mult, ALU.mult, deps=[ev_ticks[kc // 4], t_rbs])
            rb_rel[sl] = [last]
            xT_rel[sl] = [last]
            if xT_out is not None:
                st_tick = cx.dma(st[sl], xT_out.rearrange("(kc p) n -> p kc n", p=128)[:, :, c0:c0 + nr], xT[sl][:, :, 0:nr], deps=[ev_ticks[-1]])
                xT_rel[sl].append(st_tick)
        cx.barrier()


class Gemm:
    def __init__(self, cx, sc, KC, nwst=4, nwbf=2, piece=4):
        self.cx = cx
        self.KC = KC
        self.piece = piece
        self.npieces = (KC + piece - 1) // piece
        self.wst = [sc.sb("wst%d" % i, [128, piece, 128], F32) for i in range(nwst)]
        self.wld = [sc.dsem("wld%d" % i) for i in range(nwst)]
        self.wst_rel = [None] * nwst
        self.wbf = [sc.sb("wbf%d" % i, [128, KC, 128], BF16) for i in range(nwbf)]
        self.wbf_rel = [None] * nwbf
        self.nst = 0
        self.nbf = 0
        self.loaded = {}

    def load(self, W, k0, col0, key, ncols=128):
        cx = self.cx
        bs = self.nbf % len(self.wbf)
        self.nbf += 1
        wb = self.wbf[bs]
        last = None
        for p in range(self.npieces):
            kc0 = p * self.piece
            n = min(self.piece, self.KC - kc0)
            ss = self.nst % len(self.wst)
            self.nst += 1
            src = W[(k0 + kc0) * 128:(k0 + kc0 + n) * 128, col0:col0 + ncols].rearrange("(kc p) m -> p kc m", p=128)
            t_ld = cx.dma(self.wld[ss], self.wst[ss][:, 0:n, 0:ncols], src, deps=[self.wst_rel[ss]] if self.wst_rel[ss] else [])
            deps = [t_ld]
            if p == 0 and self.wbf_rel[bs]:
                deps.append(self.wbf_rel[bs])
            last = cx.cp("pool", wb[:, kc0:kc0 + n, 0:ncols], self.wst[ss][:, 0:n, 0:ncols], deps=deps)
            self.wst_rel[ss] = last
        self.loaded[key] = (wb, last, bs)
        return wb, last

    def release(self, key, tick):
        wb, last, bs = self.loaded.pop(key)
        self.wbf_rel[bs] = tick


def run_gemm(cx, sc, ps, banks, AT, KC, W, k0, coltiles, blocks, epilogue, prefetch=1, pre=None, post=None):
    g = Gemm(cx, sc, KC)
    nM = len(coltiles)
    for i in range(min(prefetch, nM)):
        g.load(W, k0, coltiles[i], i)
    bank_rel = {b: None for b in banks}
    nj = 0
    for mi in range(nM):
        if mi + prefetch < nM:
            g.load(W, k0, coltiles[mi + prefetch], mi + prefetch)
        wb, wt = g.loaded[mi][0], g.loaded[mi][1]
        if pre:
            pre(mi)
        tk = None
        for bi, (c0, n) in enumerate(blocks):
            b = banks[nj % len(banks)]
            nj += 1
            deps = [wt]
            if bank_rel[b]:
                deps.append(bank_rel[b])
            for kc in range(KC):
                tk = cx.pe(lambda E, b=b, kc=kc: E.matmul(ps[:, b, 0:n], lhsT=wb[:, kc, :], rhs=AT[:, kc, c0:c0 + n],
                                                         start=(kc == 0), stop=(kc == KC - 1)),
                           deps=deps if kc == 0 else (), tick=(kc == KC - 1))
            bank_rel[b] = epilogue(mi, bi, ps[:, b, 0:n], n, c0, tk)
        g.release(mi, tk)
        if post:
            post(mi)


def ssq_finalize(cx, sc, G, ps, bank, acc, blocks, dnorm, rstd_row, st, deps, work=None):
    rs = work if work is not None else sc.sb("rsfin", [128, NT], F32)
    last = None
    rel = None
    for (c0, n) in blocks:
        d = list(deps)
        if rel:
            d.append(rel)
        tk = cx.pe(lambda E: E.matmul(ps[:, bank, 0:n], lhsT=G.ones[:], rhs=acc[:, c0:c0 + n], start=True, stop=True), deps=d, tick=True)
        t1 = cx.act(rs[:, c0:c0 + n], ps[:, bank, 0:n], AF.Sqrt, bias=G.eps[:], scale=1.0 / dnorm, deps=[tk])
        rel = t1
        last = cx.op("dve", lambda E: E.reciprocal(out=rs[:, c0:c0 + n], in_=rs[:, c0:c0 + n]), deps=[t1])
    return cx.dma(st, rstd_row, rs[0:1, :], deps=[last])


def norm_load(cx, G, AT, src, KC, N, gcol, rstd_rows, rows_for_kc):
    with cx.scope() as sc:
        ld = [sc.dsem("nl%d" % i) for i in range(3)]
        rl = sc.dsem("rl")
        xin = [sc.sb("xin%d" % i, [128, N], F32) for i in range(3)]
        rel = [None] * 3
        rb = {}
        for r in sorted(set(rows_for_kc)):
            rb[r] = sc.sb("rb%d" % r, [128, N], F32)
            t_rb = cx.dma(rl, rb[r][:], rstd_rows[r:r + 1, :].broadcast_to([128, N]))
        for kc in range(KC):
            s = kc % 3
            t_ld = cx.dma(ld[s], xin[s][:], src[kc * 128:(kc + 1) * 128, :], deps=[rel[s]] if rel[s] else [])
            rel[s] = cx.stt("dve", AT[:, kc, 0:N], xin[s][:], G.pcol[:, gcol + kc:gcol + kc + 1], rb[rows_for_kc[kc]][:],
                            ALU.mult, ALU.mult, deps=[t_ld, t_rb])
        cx.barrier()


def plain_load(cx, AT, src, KC, N):
    with cx.scope() as sc:
        ld = sc.dsem("pl")
        for kc0 in range(0, KC, 8):
            n = min(8, KC - kc0)
            cx.dma(ld, AT[:, kc0:kc0 + n, 0:N], src[kc0 * 128:(kc0 + n) * 128, :].rearrange("(kc p) n -> p kc n", p=128))
        cx.barrier()


def ssm_main(cx, G, sc0, uT, T, N, own):
    NCH = N // 8
    if own:
        subs = [(0, 1)] + [(1 + 32 * i, 32) for i in range(8)]
    else:
        subs = [(32 * i, 32) for i in range(8)]
    with cx.scope() as sc:
        Hst = sc.sb("Hst", [128, 128, NCH + 1], BF16) if own else None
        t1 = sc.sb("sct1", [128, 128], F32)
        t2 = sc.sb("sct2", [128, 128], F32)
        t3 = sc.sb("sct3", [128, 128], F32)
        t4 = sc.sb("sct4", [128, 128], F32)
        sinj = cx.scope()
        sinj.__enter__()
        skl = [sinj.dsem("skl%d" % i) for i in range(2)]
        skt = [sinj.sb("skt%d" % i, [128, 2, 8, 128], BF16) for i in range(2)]
        skt_rel = [None, None]
        um = [sinj.sb("um%d" % i, [128, 8, 256], BF16) for i in range(2)]
        um_rel = [None, None]
        bst = [sinj.sb("bst%d" % i, [128, 2, 128, 32], BF16) for i in range(2)]
        bst_rel = [None, None]
        a8 = G.a8
        t_a8 = None
        ps = sc.ps("ps", [128, 8, 512], F32)
        banks = list(range(8))
        bank_rel = [None] * 8
        nj = 0
        nl = 0
        AR8 = a8[:, 0, :]
        AI8 = a8[:, 1, :]
        t_state = None
        if own:
            t_state = cx.cp("act", Hst[:, :, 0], G.h1[:])
        for si, (cb, ncb) in enumerate(subs):
            bs = si % 2
            n = ncb * 8
            c0 = cb * 8
            ev_last = None
            for cc in range(16):
                s = nl % 2
                nl += 1
                d = [skt_rel[s]] if skt_rel[s] else []
                t_l1 = cx.dma(skl[s], skt[s][:, 0, :, :], T["SkT"][cc], deps=d)
                t_l2 = cx.dma(skl[s], skt[s][:, 1, :, :], T["SJkT"][cc])
                d = [um_rel[s]] if um_rel[s] else []
                t_um = cx.tt("pool", um[s][:, :, 0:n], uT[:, cc, c0:c0 + n].unsqueeze(1).broadcast_to([128, 8, n]),
                             G.mask8b[:, :].unsqueeze(2).broadcast_to([128, 8, n]), ALU.mult, deps=d)
                for dual in range(2):
                    b = banks[nj % 8]
                    nj += 1
                    deps = [t_l2, t_um]
                    if bank_rel[b]:
                        deps.append(bank_rel[b])
                    tk = None
                    for j in range(8):
                        tk = cx.pe(lambda E, b=b, j=j, s=s, dual=dual: E.matmul(
                            ps[:, b, 0:8 * ncb].rearrange("p (g c) -> p g c", c=ncb),
                            lhsT=skt[s][:, dual, 7 - j, :],
                            rhs=um[s][:, :, 0:n].rearrange("p g (c j) -> p g c j", j=8)[:, :, :, j],
                            start=(j == 0), stop=(j == 7)), deps=deps if j == 0 else (), tick=(j == 7))
                    d2 = [tk]
                    if cc == 0 and dual == 0 and bst_rel[bs]:
                        d2.append(bst_rel[bs])
                    ev = cx.cp("act", bst[bs][:, dual, cc * 8:(cc + 1) * 8, 0:ncb],
                               ps[:, b, 0:8 * ncb].rearrange("p (g c) -> p g c", c=ncb), deps=d2)
                    bank_rel[b] = ev
                    ev_last = ev
                skt_rel[s] = tk
                um_rel[s] = tk
            for c in range(ncb):
                b1 = bst[bs][:, 0, :, c]
                b2 = bst[bs][:, 1, :, c]
                d = [ev_last, t_state] if c == 0 else []
                cx.tt("dve", t1[:], AR8, G.h1[:], ALU.mult, deps=d)
                cx.tt("dve", t2[:], AI8, G.h2[:], ALU.mult)
                cx.tt("dve", t3[:], AR8, G.h2[:], ALU.mult)
                cx.tt("dve", t4[:], AI8, G.h1[:], ALU.mult)
                cx.tt("dve", t1[:], t1[:], t2[:], ALU.add)
                cx.tt("dve", t3[:], t3[:], t4[:], ALU.subtract)
                if own:
                    t_state = cx.tt("dve", Hst[:, :, cb + c + 1], t1[:], b1, ALU.add)
                t_h1 = cx.tt("dve", G.h1[:], t1[:], b1, ALU.add)
                t_h2 = cx.tt("dve", G.h2[:], t3[:], b2, ALU.add)
            bst_rel[bs] = t_h2
        cx.barrier()
        sinj.__exit__(None, None, None)
        if not own:
            return
        kl = [sc.dsem("kl%d" % i) for i in range(2)]
        kb = [sc.sb("kb%d" % i, [128, 2, 8, 128], BF16) for i in range(2)]
        kb_rel = [None, None]
        rp = [sc.sb("rp%d" % i, [128, 8, 8, 128], BF16) for i in range(2)]
        rp_rel = [None, None]
        yt = sc.sb("yt", [128, 512], F32)
        y2 = sc.sb("y2", [128, 512], F32)
        sg = sc.sb("sg", [128, 512], F32)
        blocks = BLK_OWN
        bank_rel = [None] * 8
        nj = 0
        for cc in range(16):
            s = cc % 2
            d = [kb_rel[s]] if kb_rel[s] else []
            cx.dma(kl[s], kb[s][:, 0, :, :], T["Kbd"][cc], deps=d)
            t_kl = cx.dma(kl[s], kb[s][:, 1, :, :], T["Rk"][cc])
            t_rp = None
            for j in range(8):
                d = [t_kl]
                if j == 0 and rp_rel[s]:
                    d.append(rp_rel[s])
                t_rp = cx.tt("pool", rp[s][:, j, :, :], kb[s][:, 1, j, :].unsqueeze(1).broadcast_to([128, 8, 128]), G.cmaskb[:], ALU.mult, deps=d)
            bb = []
            for bi in range(len(blocks)):
                b = banks[nj % 8]
                nj += 1
                bb.append(b)
            first_deps = [t_kl, t_rp, t_state] + [bank_rel[b] for b in bb if bank_rel[b]]
            first = True
            for tau in range(8):
                for bi, (c0, n) in enumerate(blocks):
                    ncb = n // 8
                    b = bb[bi]
                    cx.pe(lambda E, b=b, tau=tau, c0=c0, n=n: E.matmul(
                        ps[:, b, 0:n].rearrange("p (c j) -> p c j", j=8)[:, :, tau:8],
                        lhsT=kb[s][:, 0, tau, :],
                        rhs=uT[:, cc, c0:c0 + n].rearrange("p (c j) -> p c j", j=8)[:, :, 0:8 - tau],
                        start=(tau == 0), stop=False), deps=first_deps if first else ())
                    first = False
            tk = None
            for gl in range(8):
                g = cc * 8 + gl
                for j in range(8):
                    for bi, (c0, n) in enumerate(blocks):
                        ncb = n // 8
                        cb = c0 // 8
                        b = bb[bi]
                        lastmm = (gl == 7 and j == 7)
                        tk = cx.pe(lambda E, b=b, j=j, gl=gl, g=g, cb=cb, ncb=ncb, n=n, lastmm=lastmm: E.matmul(
                            ps[:, b, 0:n].rearrange("p (c j) -> p c j", j=8)[:, :, j],
                            lhsT=rp[s][:, j, gl, :],
                            rhs=Hst[:, g, cb:cb + ncb],
                            start=False, stop=lastmm), tick=(lastmm and bi == len(blocks) - 1))
            kb_rel[s] = tk
            rp_rel[s] = tk
            dcol = G.pcol[:, PC["ssm_d"] + cc:PC["ssm_d"] + cc + 1]
            for bi, (c0, n) in enumerate(blocks):
                b = bb[bi]
                cx.stt("dve", yt[:, 0:n], uT[:, cc, c0:c0 + n], dcol, ps[:, b, 0:n], ALU.mult, ALU.add, deps=[tk])
                cx.tt("dve", y2[:, 0:n], yt[:, 0:n], yt[:, 0:n], ALU.mult)
                cx.ts("dve", y2[:, 0:n], y2[:, 0:n], 0.044715 * 1.5957691216057308, 1.5957691216057308, ALU.mult, ALU.add)
                t_w = cx.tt("dve", y2[:, 0:n], y2[:, 0:n], yt[:, 0:n], ALU.mult)
                t_sg = cx.act(sg[:, 0:n], y2[:, 0:n], AF.Sigmoid, deps=[t_w])
                t_z = cx.tt("dve", uT[:, cc, c0:c0 + n], yt[:, 0:n], sg[:, 0:n], ALU.mult, deps=[t_sg])
                bank_rel[b] = t_z
        cx.barrier()


def build_program(stop_after=None, force_dbg=False):
    nc = bass.Bass("TRN2", target_bir_lowering=False)
    T = {}

    def din(n, shp, dt=F32):
        T[n] = nc.dram_tensor(n, list(shp), dt, kind="ExternalInput").ap()

    dbg = (stop_after is not None) or force_dbg

    def dscr(n, shp, dt=F32):
        T[n] = nc.dram_tensor(n, list(shp), dt, kind=("ExternalOutput" if dbg else "Internal")).ap()

    din("xown", [NT, D])
    din("xprev", [NPREV, D])
    din("memt", [NMEM, D])
    din("consts", [128, 266])
    din("cmask", [128, 8, 128])
    din("pvec", [NPC, 128])
    din("ssmin", [128, 384 + 8192])
    order = ["setup", "prev", "win", "ssm", "glu", "wout", "attn", "down0", None]
    lvl = order.index(stop_after)
    if lvl >= 1:
        din("w_in", [D, 8192])
    if lvl >= 4:
        din("glu_w", [2048, 2048])
    if lvl >= 5:
        din("w_out", [D, D])
    if lvl >= 6:
        din("wq", [D, D])
        din("wk", [D, D])
        din("wv", [D, D])
        din("wo", [D, D])
    if lvl >= 7:
        din("w_up", [D, 2 * DFF])
        din("w_down", [DFF, D])
    T["out"] = nc.dram_tensor("out", [NTOK, D], F32, kind="ExternalOutput").ap()
    dscr("SkT", [16, 128, 8, 128], BF16)
    dscr("SJkT", [16, 128, 8, 128], BF16)
    dscr("Rk", [16, 128, 8, 128], BF16)
    dscr("Kbd", [16, 128, 8, 128], BF16)
    dscr("xT", [D, NT], F32)
    dscr("ymix", [D, NT], F32)
    dscr("uTp", [2048, NPREV], BF16)
    dscr("uTo", [2048, NT], BF16)
    dscr("qT", [D, NT], BF16)
    dscr("oT", [D, NT], BF16)
    dscr("actT", [DFF, NT], BF16)
    dscr("rstd", [4, NT], F32)
    if dbg:
        dscr("dbg_h", [128, 128], F32)
        dscr("dbg_z", [2048, NT], BF16)

    cx = Ctx(nc)
    with cx.gs:
        G = cx.scope()
        with G:
            load_consts(cx, G, T)
            ssm_setup(cx, G, T)
            if stop_after == "setup":
                return nc
            mixer(cx, G, T, stop_after)
            if stop_after in ("prev", "win", "ssm", "glu"):
                return nc
            rest(cx, G, T, stop_after)
    return nc


def gemm_u(cx, G, AT, T, dst, blocks, N):
    with cx.scope() as sc:
        ps = sc.ps("ps", [128, 8, 512], F32)
        st = [sc.dsem("st%d" % i) for i in range(2)]
        ust = [sc.sb("ust%d" % i, [128, N], BF16) for i in range(2)]
        rel = [None, None]
        state = {}

        def epi(mi, bi, bank, n, c0, tk):
            s = mi % 2
            deps = [tk]
            if bi == 0 and rel[s]:
                deps.append(rel[s])
            t = cx.cp("act", ust[s][:, c0:c0 + n], bank, deps=deps)
            state["last"] = t
            return t

        def post(mi):
            s = mi % 2
            rel[s] = cx.dma(st[s], dst[mi * 128:(mi + 1) * 128, :], ust[s][:, 0:N], deps=[state["last"]])

        run_gemm(cx, sc, ps, list(range(8)), AT, 32, T["w_in"], 0, [128 * m for m in range(16)], blocks, epi, post=post)
        cx.barrier()


def gemm_conv(cx, G, AT, T):
    with cx.scope() as sc:
        ps = sc.ps("ps", [128, 8, 512], F32)
        st = sc.dsem("st")
        st2 = sc.dsem("st2")
        cv = sc.sb("cv", [128, 2 + NT], F32)
        gb = sc.sb("gb", [128, NT], F32)
        yc = sc.sb("yc", [128, NT], F32)
        acc = sc.sb("acc", [128, NT], F32)
        t0 = cx.memset("dve", cv[:, 0:2], 0.0)
        cx.memset("dve", acc[:], 0.0)
        state = {"cv_rel": None, "gb_rel": None, "last": {}}
        coltiles = []
        for i in range(16):
            coltiles += [4096 + 128 * i, 6144 + 128 * i, 2048 + 128 * i]

        def epi(mi, bi, bank, n, c0, tk):
            i, which = mi // 3, mi % 3
            if which == 0:
                deps = [tk]
                if bi == 0 and state["cv_rel"]:
                    deps.append(state["cv_rel"])
                t = cx.cp("act", cv[:, 2 + c0:2 + c0 + n], bank, deps=deps)
                state["last"][(0, bi)] = t
            elif which == 1:
                t = cx.tt("dve", cv[:, 2 + c0:2 + c0 + n], bank, cv[:, 2 + c0:2 + c0 + n], ALU.mult, deps=[tk, state["last"][(0, bi)]])
                state["last"][1] = t
            else:
                deps = [tk]
                if bi == 0 and state["gb_rel"]:
                    deps += state["gb_rel"]
                t = cx.cp("act", gb[:, c0:c0 + n], bank, deps=deps)
                state["last"][2] = t
            return t

        def post(mi):
            i, which = mi // 3, mi % 3
            if which != 2:
                return
            w0 = G.pcol[:, PC["cw0"] + i:PC["cw0"] + i + 1]
            w1 = G.pcol[:, PC["cw1"] + i:PC["cw1"] + i + 1]
            w2 = G.pcol[:, PC["cw2"] + i:PC["cw2"] + i + 1]
            cx.ts("dve", yc[:], cv[:, 2:2 + NT], w2, None, ALU.mult, deps=[state["last"][1], state["last"][2]])
            cx.stt("dve", yc[:], cv[:, 1:1 + NT], w1, yc[:], ALU.mult, ALU.add)
            t_c = cx.stt("dve", yc[:], cv[:, 0:NT], w0, yc[:], ALU.mult, ALU.add)
            state["cv_rel"] = t_c
            t_y = cx.tt("dve", gb[:], gb[:], yc[:], ALU.mult)
            t_s = cx.act(yc[:], gb[:], AF.Square, deps=[t_y])
            t_a = cx.tt("pool", acc[:], acc[:], yc[:], ALU.add, deps=[t_s])
            cx.wait("dve", t_a)
            t_st = cx.dma(st, T["ymix"][2048 + 128 * i:2048 + 128 * (i + 1), :], gb[:], deps=[t_y])
            state["gb_rel"] = [t_st, t_s]
            state["acc"] = t_a

        run_gemm(cx, sc, ps, list(range(7)), AT, 32, T["w_in"], 0, coltiles, BLK_OWN, epi, post=post)
        ssq_finalize(cx, sc, G, ps, 7, acc, BLK_OWN, 2048.0, T["rstd"][1:2, :], st2, [state["acc"]], work=yc)
        cx.barrier()


def gemm_glu(cx, G, zT, T):
    with cx.scope() as sc:
        ps = sc.ps("ps", [128, 8, 512], F32)
        st = [sc.dsem("st%d" % i) for i in range(2)]
        st2 = sc.dsem("st2")
        gt = sc.sb("gt", [128, 512], F32)
        yst = [sc.sb("yst%d" % i, [128, NT], F32) for i in range(2)]
        sq = sc.sb("sq", [128, NT], F32)
        acc = sc.sb("acc", [128, NT], F32)
        cx.memset("dve", acc[:], 0.0)
        rel = [None, None]
        state = {}

        def epi(mi, bi, bank, n, c0, tk):
            s = mi % 2
            bcol = G.pcol[:, PC["glu_b"] + mi:PC["glu_b"] + mi + 1]
            t_g = cx.act(gt[:, 0:n], bank, AF.Sigmoid, bias=bcol, deps=[tk] + ([state["y"]] if "y" in state else []))
            deps = [t_g]
            if bi == 0 and rel[s]:
                deps += rel[s]
            state["y"] = cx.tt("dve", yst[s][:, c0:c0 + n], zT[:, mi, c0:c0 + n], gt[:, 0:n], ALU.mult, deps=deps)
            return t_g

        def post(mi):
            s = mi % 2
            d = [state["y"]] + ([state["acc"]] if "acc" in state else [])
            t_s = cx.act(sq[:], yst[s][:], AF.Square, deps=d)
            state["acc"] = cx.tt("pool", acc[:], acc[:], sq[:], ALU.add, deps=[t_s])
            t_st = cx.dma(st[s], T["ymix"][128 * mi:128 * (mi + 1), :], yst[s][:], deps=[state["y"]])
            rel[s] = [t_st, t_s]

        run_gemm(cx, sc, ps, list(range(7)), zT, 16, T["glu_w"], 0, [128 * m for m in range(16)], BLK_OWN, epi, post=post)
        ssq_finalize(cx, sc, G, ps, 7, acc, BLK_OWN, 2048.0, T["rstd"][0:1, :], st2, [state["acc"]], work=sq)
        cx.barrier()


def dbg_dump(cx, T, name, ap):
    with cx.scope() as sc:
        d = sc.dsem("dbg")
        cx.serial = True
        cx.dma(d, T[name], ap)
        cx.barrier()
        cx.serial = False


def mixer(cx, G, T, stop_after):
    tiles_prev = [(128 * i, 128, 128 * i) for i in range(16)]
    tiles_own = [(0, 8, 0)] + [(8 + 128 * i, 128, 8 + 128 * i) for i in range(16)]
    with cx.scope() as s1:
        AT = s1.sb("AT", [128, 32, NT], BF16)
        load_norm(cx, G, AT, T["xprev"], tiles_prev, PC["g_mix"])
        gemm_u(cx, G, AT, T, T["uTp"], BLK_PREV, NPREV)
    with cx.scope() as s2:
        uT = s2.sb("uT", [128, 16, NT], BF16)
        plain_load(cx, uT, T["uTp"], 16, NPREV)
        ssm_main(cx, G, s2, uT, T, NPREV, own=False)
    if stop_after == "prev":
        dbg_dump(cx, T, "dbg_h", G.h1[:])
        return
    with cx.scope() as s3:
        AT = s3.sb("AT", [128, 32, NT], BF16)
        load_norm(cx, G, AT, T["xown"], tiles_own, PC["g_mix"], xT_out=T["xT"])
        gemm_u(cx, G, AT, T, T["uTo"], BLK_OWN, NT)
        gemm_conv(cx, G, AT, T)
    if stop_after == "win":
        return
    with cx.scope() as s4:
        uT = s4.sb("uT", [128, 16, NT], BF16)
        plain_load(cx, uT, T["uTo"], 16, NT)
        ssm_main(cx, G, s4, uT, T, NT, own=True)
        if stop_after == "ssm":
            with cx.scope() as sc:
                d = sc.dsem("dbg")
                cx.serial = True
                cx.dma(d, T["dbg_z"].rearrange("(kc p) n -> p kc n", p=128), uT[:])
                cx.barrier()
                cx.serial = False
            return
        gemm_glu(cx, G, uT, T)


def gemm_resid(cx, G, AT, KC, W, k0, T, final_ssq):
    with cx.scope() as sc:
        ps = sc.ps("ps", [128, 8, 512], F32)
        ld = [sc.dsem("xl%d" % i) for i in range(2)]
        st = [sc.dsem("xs%d" % i) for i in range(2)]
        st2 = sc.dsem("st2")
        xr = [sc.sb("xr%d" % i, [128, NT], F32) for i in range(2)]
        rel = [None, None]
        ldt = [None, None]
        state = {}
        if final_ssq:
            sq = sc.sb("sq", [128, NT], F32)
            acc = sc.sb("acc", [128, NT], F32)
            cx.memset("dve", acc[:], 0.0)

        def pre(mi):
            s = mi % 2
            ldt[s] = cx.dma(ld[s], xr[s][:], T["xT"][128 * mi:128 * (mi + 1), :], deps=rel[s] if rel[s] else [])

        def epi(mi, bi, bank, n, c0, tk):
            s = mi % 2
            t = cx.tt("dve", xr[s][:, c0:c0 + n], bank, xr[s][:, c0:c0 + n], ALU.add, deps=[tk, ldt[s]])
            state["last"] = t
            return t

        def post(mi):
            s = mi % 2
            r = []
            if final_ssq:
                d = [state["last"]] + ([state["acc"]] if "acc" in state else [])
                t_s = cx.act(sq[:], xr[s][:], AF.Square, deps=d)
                state["acc"] = cx.tt("pool", acc[:], acc[:], sq[:], ALU.add, deps=[t_s])
                r.append(t_s)
            t_st = cx.dma(st[s], T["xT"][128 * mi:128 * (mi + 1), :], xr[s][:], deps=[state["last"]])
            r.append(t_st)
            rel[s] = r

        banks = list(range(7)) if final_ssq else list(range(8))
        run_gemm(cx, sc, ps, banks, AT, KC, W, k0, [128 * m for m in range(32)], BLK_OWN, epi, pre=pre, post=post)
        if final_ssq:
            ssq_finalize(cx, sc, G, ps, 7, acc, BLK_OWN, float(D), T["rstd"][2:3, :], st2, [state["acc"]], work=sq)
        cx.barrier()


def gemm_store(cx, G, AT, W, dst):
    with cx.scope() as sc:
        ps = sc.ps("ps", [128, 8, 512], F32)
        st = [sc.dsem("st%d" % i) for i in range(2)]
        qs = [sc.sb("qs%d" % i, [128, NT], BF16) for i in range(2)]
        rel = [None, None]
        state = {}

        def epi(mi, bi, bank, n, c0, tk):
            s = mi % 2
            deps = [tk]
            if bi == 0 and rel[s]:
                deps.append(rel[s])
            t = cx.cp("act", qs[s][:, c0:c0 + n], bank, deps=deps)
            state["last"] = t
            return t

        def post(mi):
            s = mi % 2
            rel[s] = cx.dma(st[s], dst[mi * 128:(mi + 1) * 128, :], qs[s][:], deps=[state["last"]])

        run_gemm(cx, sc, ps, list(range(8)), AT, 32, W, 0, [128 * m for m in range(32)], BLK_OWN, epi, post=post)
        cx.barrier()


def gemm_up(cx, G, AT, T):
    with cx.scope() as sc:
        ps = sc.ps("ps", [128, 8, 512], F32)
        st = [sc.dsem("st%d" % i) for i in range(2)]
        a_sb = sc.sb("a_sb", [128, 2 + NT], F32)
        g_sb = sc.sb("g_sb", [128, NT], F32)
        tt_ = sc.sb("tconv", [128, NT], F32)
        ast = [sc.sb("ast%d" % i, [128, NT], BF16) for i in range(2)]
        cx.memset("dve", a_sb[:, 0:2], 0.0)
        rel = [None, None]
        state = {"a_rel": None, "g_rel": None}
        coltiles = []
        for i in range(86):
            coltiles += [128 * i, DFF + 128 * i]

        def epi(mi, bi, bank, n, c0, tk):
            i, which = mi // 2, mi % 2
            if which == 0:
                deps = [tk]
                if bi == 0 and state["a_rel"]:
                    deps.append(state["a_rel"])
                t = cx.cp("act", a_sb[:, 2 + c0:2 + c0 + n], bank, deps=deps)
                state["la"] = t
            else:
                deps = [tk]
                if bi == 0 and state["g_rel"]:
                    deps.append(state["g_rel"])
                t = cx.cp("act", g_sb[:, c0:c0 + n], bank, deps=deps)
                state["lg"] = t
            return t

        def post(mi):
            i, which = mi // 2, mi % 2
            if which != 1:
                return
            s = i % 2
            w0 = G.pcol[:, PC["fw0"] + i:PC["fw0"] + i + 1]
            w1 = G.pcol[:, PC["fw1"] + i:PC["fw1"] + i + 1]
            w2 = G.pcol[:, PC["fw2"] + i:PC["fw2"] + i + 1]
            cb = G.pcol[:, PC["fcb"] + i:PC["fcb"] + i + 1]
            cx.ts("dve", a_sb[:, 2:2 + HALO], a_sb[:, 2:2 + HALO], G.flag, None, ALU.mult, deps=[state["la"], state["lg"]])
            cx.ts("dve", tt_[:], a_sb[:, 2:2 + NT], w2, cb, ALU.mult, ALU.add)
            cx.stt("dve", tt_[:], a_sb[:, 1:1 + NT], w1, tt_[:], ALU.mult, ALU.add)
            t_c = cx.stt("dve", tt_[:], a_sb[:, 0:NT], w0, tt_[:], ALU.mult, ALU.add)
            state["a_rel"] = t_c
            t_s = cx.act(tt_[:], tt_[:], AF.Silu, deps=[t_c])
            t_m = cx.tt("dve", ast[s][:], tt_[:], g_sb[:], ALU.mult, deps=[t_s] + ([rel[s]] if rel[s] else []))
            state["g_rel"] = t_m
            rel[s] = cx.dma(st[s], T["actT"][128 * i:128 * (i + 1), :], ast[s][:], deps=[t_m])

        run_gemm(cx, sc, ps, list(range(8)), AT, 32, T["w_up"], 0, coltiles, BLK_OWN, epi, post=post)
        cx.barrier()


def attention(cx, G, T):
    with cx.scope() as sc:
        kT = sc.sb("kT", [128, 32, NMEM], BF16)
        vsb = sc.sb("vsb", [128, 2, D], BF16)
        with cx.scope() as s1:
            hmT = s1.sb("hmT", [128, 32, NMEM], BF16)
            load_norm(cx, G, hmT, T["memt"], [(0, 128, 0), (128, 128, 128)], PC["g_mem"])
            with cx.scope() as s2:
                ps = s2.ps("ps", [128, 8, 512], F32)

                def epi(mi, bi, bank, n, c0, tk):
                    return cx.cp("act", kT[:, mi, 0:NMEM], bank, deps=[tk])

                run_gemm(cx, s2, ps, list(range(8)), hmT, 32, T["wk"], 0, [128 * m for m in range(32)], [(0, NMEM)], epi)
                cx.barrier()
            with cx.scope() as s2:
                ps = s2.ps("ps", [128, 8, 512], F32)
                g = Gemm(cx, s2, 32)
                g.load(T["wv"], 0, 0, 0)
                bank_rel = [None] * 8
                nj = 0
                for mi in range(32):
                    if mi + 1 < 32:
                        g.load(T["wv"], 0, 128 * (mi + 1), mi + 1)
                    wb, wt = g.loaded[mi][0], g.loaded[mi][1]
                    tk = None
                    for tt_i in range(2):
                        b = nj % 8
                        nj += 1
                        deps = [wt] + ([bank_rel[b]] if bank_rel[b] else [])
                        for kc in range(32):
                            tk = cx.pe(lambda E, b=b, kc=kc, tt_i=tt_i: E.matmul(ps[:, b, 0:128], lhsT=hmT[:, kc, tt_i * 128:(tt_i + 1) * 128],
                                                                                 rhs=wb[:, kc, :], start=(kc == 0), stop=(kc == 31)),
                                       deps=deps if kc == 0 else (), tick=(kc == 31))
                        bank_rel[b] = cx.cp("act", vsb[:, tt_i, 128 * mi:128 * (mi + 1)], ps[:, b, 0:128], deps=[tk])
                    g.release(mi, tk)
                cx.barrier()
        with cx.scope() as s3:
            ps = s3.ps("ps", [128, 6, 512], F32)
            psb = s3.ps("psb", [128, 2, 1024], BF16)
            ql = s3.dsem("ql")
            od = s3.dsem("od")
            qh = s3.sb("qh", [128, 8, NT], BF16)
            pT = s3.sb("pT", [128, 2, NT], BF16)
            es = s3.sb("es", [128, NMEM], F32)
            pb = s3.sb("pb", [128, NMEM], BF16)
            mx = s3.sb("mx", [128, 1], F32)
            nmx = s3.sb("nmx", [128, 1], F32)
            sm = s3.sb("sm", [128, 1], F32)
            rs = s3.sb("rs", [128, 1], F32)
            ost = s3.sb("ost", [128, NT], BF16)
            tiles = [(0, 8)] + [(8 + 128 * i, 128) for i in range(16)]
            scale = 1.0 / 32.0
            cx.serial = True
            for h in range(4):
                cx.dma(ql, qh[:], T["qT"][1024 * h:1024 * (h + 1), :].rearrange("(kc p) n -> p kc n", p=128))
                for ti, (c0, nr) in enumerate(tiles):
                    b = ti % 3
                    for dc in range(8):
                        cx.pe(lambda E, b=b, dc=dc, c0=c0, nr=nr: E.matmul(ps[0:nr, b, 0:NMEM], lhsT=qh[:, dc, c0:c0 + nr], rhs=kT[:, h * 8 + dc, :],
                                                                          start=(dc == 0), stop=(dc == 7)))
                    cx.op("dve", lambda E: E.reduce_max(out=mx[0:nr, :], in_=ps[0:nr, b, 0:NMEM], axis=AX.X))
                    cx.ts("dve", nmx[0:nr, :], mx[0:nr, :], -scale, None, ALU.mult)
                    cx.act(es[0:nr, :], ps[0:nr, b, 0:NMEM], AF.Exp, bias=nmx[0:nr, :], scale=scale, accum=sm[0:nr, :])
                    cx.op("dve", lambda E: E.reciprocal(out=rs[0:nr, :], in_=sm[0:nr, :]))
                    cx.ts("dve", pb[0:nr, :], es[0:nr, :], rs[0:nr, :], None, ALU.mult)
                    bb = ti % 2
                    for kc in range(2):
                        cx.pe(lambda E, bb=bb, kc=kc, nr=nr: E.transpose(psb[:, bb, kc * 128:kc * 128 + nr], pb[0:nr, kc * 128:(kc + 1) * 128],
                                                                        G.identb[0:nr, 0:nr]))
                    cx.cp("act", pT[:, :, c0:c0 + nr], psb[:, bb, 0:256].rearrange("p (k m) -> p k m", m=128)[:, :, 0:nr])
                for dvt in range(8):
                    for bi, (c0, n) in enumerate(BLK_OWN):
                        b = 3 + (bi % 3)
                        for kc in range(2):
                            cx.pe(lambda E, b=b, kc=kc, c0=c0, n=n: E.matmul(ps[:, b, 0:n], lhsT=vsb[:, kc, h * 1024 + dvt * 128:h * 1024 + (dvt + 1) * 128],
                                                                            rhs=pT[:, kc, c0:c0 + n], start=(kc == 0), stop=(kc == 1)))
                        cx.cp("act", ost[:, c0:c0 + n], ps[:, b, 0:n])
                    cx.dma(od, T["oT"][128 * (h * 8 + dvt):128 * (h * 8 + dvt + 1), :], ost[:])
            cx.barrier()
            cx.serial = False


def final_out(cx, G, T):
    with cx.scope() as sc:
        ps = sc.ps("ps", [128, 8, 512], F32)
        ld = [sc.dsem("fl%d" % i) for i in range(2)]
        st = [sc.dsem("fs%d" % i) for i in range(2)]
        rl = sc.dsem("rl")
        xf = [sc.sb("xf%d" % i, [128, 32, 128], F32) for i in range(2)]
        orow = [sc.sb("orow%d" % i, [128, D], F32) for i in range(2)]
        rb = sc.sb("rb", [128, NT], F32)
        t_rb = cx.dma(rl, rb[:], T["rstd"][2:3, :].broadcast_to([128, NT]))
        xf_rel = [None, None]
        or_rel = [None, None]
        bank_rel = [None] * 8
        nj = 0
        gcol = PC["g_final"]
        for ti in range(16):
            s = ti % 2
            c0 = 8 + 128 * ti
            t_ld = cx.dma(ld[s], xf[s][:], T["xT"].rearrange("(kc p) n -> p kc n", p=128)[:, :, c0:c0 + 128], deps=[xf_rel[s]] if xf_rel[s] else [])
            cx.tt("dve", xf[s][:], xf[s][:], rb[:, c0:c0 + 128].unsqueeze(1).broadcast_to([128, 32, 128]), ALU.mult, deps=[t_ld, t_rb])
            t_n = cx.tt("dve", xf[s][:], xf[s][:], G.pcol[:, gcol:gcol + 32].unsqueeze(2).broadcast_to([128, 32, 128]), ALU.mult)
            tk = None
            last_ev = None
            for grp in range(8):
                b = nj % 8
                nj += 1
                deps = [t_n] + ([bank_rel[b]] if bank_rel[b] else [])
                for j in range(4):
                    kc = grp * 4 + j
                    tk = cx.pe(lambda E, b=b, j=j, kc=kc, s=s: E.transpose(ps[:, b, j * 128:(j + 1) * 128], xf[s][:, kc, :], G.ident),
                               deps=deps if j == 0 else (), tick=(j == 3))
                d2 = [tk]
                if grp == 0 and or_rel[s]:
                    d2.append(or_rel[s])
                last_ev = cx.cp("act", orow[s][:, grp * 512:(grp + 1) * 512], ps[:, b, :], deps=d2)
                bank_rel[b] = last_ev
            xf_rel[s] = tk
            or_rel[s] = cx.dma(st[s], T["out"][128 * ti:128 * (ti + 1), :], orow[s][:], deps=[last_ev])
        cx.barrier()


def rest(cx, G, T, stop_after):
    with cx.scope() as s:
        AT = s.sb("AT", [128, 32, NT], BF16)
        norm_load(cx, G, AT, T["ymix"], 32, NT, PC["g_omix"], T["rstd"], [0] * 16 + [1] * 16)
        gemm_resid(cx, G, AT, 32, T["w_out"], 0, T, True)
    if stop_after == "wout":
        return
    with cx.scope() as s:
        AT = s.sb("AT", [128, 32, NT], BF16)
        norm_load(cx, G, AT, T["xT"], 32, NT, PC["g_xattn"], T["rstd"], [2] * 32)
        gemm_store(cx, G, AT, T["wq"], T["qT"])
    attention(cx, G, T)
    with cx.scope() as s:
        AT = s.sb("AT", [128, 32, NT], BF16)
        plain_load(cx, AT, T["oT"], 32, NT)
        gemm_resid(cx, G, AT, 32, T["wo"], 0, T, True)
    if stop_after == "attn":
        return
    with cx.scope() as s:
        AT = s.sb("AT", [128, 32, NT], BF16)
        norm_load(cx, G, AT, T["xT"], 32, NT, PC["g_ffn"], T["rstd"], [2] * 32)
        gemm_up(cx, G, AT, T)
    for p, (k0, kc) in enumerate([(0, 29), (29, 29), (58, 28)]):
        with cx.scope() as s:
            AT = s.sb("AT", [128, 29, NT], BF16)
            plain_load(cx, AT, T["actT"][k0 * 128:(k0 + kc) * 128, :], kc, NT)
            gemm_resid(cx, G, AT, kc, T["w_down"], k0, T, p == 2)
        if stop_after == "down0":
            return
    final_out(cx, G, T)


def _host_inputs(inp, cores):
    f = np.float32
    x = np.asarray(inp["x"], f)
    mem = np.asarray(inp["mem"], f)
    ident = np.eye(128, dtype=f)
    bd = np.kron(np.eye(8, dtype=f), np.ones((16, 16), f))
    mask8 = np.kron(np.eye(8, dtype=f), np.ones((16, 1), f))
    sgn = np.concatenate([-np.ones((64, 1), f), np.ones((64, 1), f)], 0)
    cmask = np.zeros((128, 8, 128), f)
    for gl in range(8):
        cmask[:, gl, gl * 16:(gl + 1) * 16] = 1.0
    vecs = {
        "g_mix": inp["norm_mix_g"][0], "ssm_d": inp["ssm_d"][0], "glu_b": inp["ssm_glu_b"][0],
        "cw0": inp["conv_w"][0, 0], "cw1": inp["conv_w"][0, 1], "cw2": inp["conv_w"][0, 2],
        "g_omix": np.concatenate([inp["out_norm_ssm_g"][0], inp["out_norm_conv_g"][0]]),
        "g_xattn": inp["norm_xattn_g"][0], "g_mem": inp["norm_mem_g"][0], "g_ffn": inp["norm_ffn_g"][0],
        "fw0": inp["ffn_conv_w"][0, 0], "fw1": inp["ffn_conv_w"][0, 1], "fw2": inp["ffn_conv_w"][0, 2],
        "fcb": inp["ffn_conv_b"][0], "g_final": inp["norm_final_g"],
    }
    pvec = np.zeros((NPC, 128), f)
    for k, v in vecs.items():
        v = np.asarray(v, f).reshape(-1, 128)
        pvec[PC[k]:PC[k] + v.shape[0]] = v
    lr = np.asarray(inp["ssm_lambda_re"][0], f).T
    li = np.asarray(inp["ssm_lambda_im"][0], f).T
    ls = np.broadcast_to(np.asarray(inp["ssm_log_step"][0], f)[None, :], (128, 128))
    br = np.asarray(inp["ssm_b_re"][0], f).transpose(1, 0, 2).reshape(64, 2048)
    bi = np.asarray(inp["ssm_b_im"][0], f).transpose(1, 0, 2).reshape(64, 2048)
    cr = np.asarray(inp["ssm_c_re"][0], f).transpose(2, 0, 1).reshape(64, 2048)
    ci = np.asarray(inp["ssm_c_im"][0], f).transpose(2, 0, 1).reshape(64, 2048)
    ssmin = np.concatenate([
        np.concatenate([lr, lr], 0), np.concatenate([li, li], 0), ls,
        np.concatenate([br, bi], 0), np.concatenate([bi, br], 0),
        np.concatenate([cr, ci], 0), np.concatenate([ci, cr], 0)], 1).astype(f)
    shared = {
        "cmask": cmask, "pvec": pvec, "ssmin": np.ascontiguousarray(ssmin),
        "w_in": np.asarray(inp["w_in"][0], f), "glu_w": np.asarray(inp["ssm_glu_w"][0], f),
        "w_out": np.asarray(inp["w_out"][0], f), "wq": np.asarray(inp["xattn_wq"][0], f),
        "wk": np.asarray(inp["xattn_wk"][0], f), "wv": np.asarray(inp["xattn_wv"][0], f),
        "wo": np.asarray(inp["xattn_wo"][0], f), "w_up": np.asarray(inp["ffn_w_up"][0], f),
        "w_down": np.asarray(inp["ffn_w_down"][0], f),
    }
    maps = []
    for c in cores:
        b, half = c // 2, c % 2
        xown = np.zeros((NT, D), f)
        xprev = np.zeros((NPREV, D), f)
        if half == 0:
            xown[HALO:] = x[b, 0:NTOK]
        else:
            xown[:] = x[b, NTOK - HALO:2 * NTOK]
            xprev[HALO:] = x[b, 0:NTOK - HALO]
        consts = np.concatenate([ident, bd, mask8, sgn, np.full((128, 1), float(half), f)], 1).astype(f)
        m = dict(shared)
        m.update({"xown": xown, "xprev": xprev, "memt": np.ascontiguousarray(mem[b]), "consts": np.ascontiguousarray(consts)})
        maps.append(m)
    return maps


_NC_CACHE = {}


def kernel(**inputs):
    cores = list(range(8))
    maps = _host_inputs(inputs, cores)
    if "nc" not in _NC_CACHE:
        _NC_CACHE["nc"] = build_program(None)
    res = run_bass_kernel_spmd(_NC_CACHE["nc"], maps, core_ids=cores)
    out = np.zeros((4, 2 * NTOK, D), np.float32)
    for c in cores:
        out[c // 2, (c % 2) * NTOK:(c % 2 + 1) * NTOK] = res.results[c]["out"]
    return out
```

```python
import math, contextlib
import numpy as np
import concourse.bass as bass
import concourse.mybir as mybir
from concourse.bass_utils import run_bass_kernel_spmd

F32 = mybir.dt.float32
BF16 = mybir.dt.bfloat16
I32 = mybir.dt.int32
AF = mybir.ActivationFunctionType
ALU = mybir.AluOpType
AX = mybir.AxisListType

D = 4096
NTOK = 2048
HALO = 8
NT = NTOK + HALO
NPREV = 2048
NMEM = 256
DFF = 11008
EPS = 1e-6
BLK_OWN = [(0, 8)] + [(8 + 512 * i, 512) for i in range(4)]
BLK_PREV = [(512 * i, 512) for i in range(4)]
TWO_PI = 2.0 * math.pi
PI_SAFE = 3.14159

PC = {}
_o = 0
for _n, _l in [("g_mix", 32), ("ssm_d", 16), ("glu_b", 16), ("cw0", 16), ("cw1", 16), ("cw2", 16),
               ("g_omix", 32), ("g_xattn", 32), ("g_mem", 32), ("g_ffn", 32),
               ("fw0", 86), ("fw1", 86), ("fw2", 86), ("fcb", 86), ("g_final", 32)]:
    PC[_n] = _o
    _o += _l
NPC = 640
assert _o <= NPC


class DSem:
    _n = 0

    def __init__(self, sem):
        self.sem = sem
        self.cnt = 0
        DSem._n += 1
        self.uid = DSem._n


class Scope:
    def __init__(self, cx):
        self.cx = cx
        self.es = contextlib.ExitStack()
        self.dsems = []

    def __enter__(self):
        self.es.__enter__()
        return self

    def __exit__(self, *a):
        for d in self.dsems:
            self.cx.dsems.remove(d)
        return self.es.__exit__(*a)

    def sb(self, name, shape, dt=F32):
        self.cx.uid += 1
        return self.es.enter_context(self.cx.nc.sbuf_tensor("%s_%d" % (name, self.cx.uid), list(shape), dt))

    def ps(self, name, shape, dt=F32):
        self.cx.uid += 1
        return self.es.enter_context(self.cx.nc.psum_tensor("%s_%d" % (name, self.cx.uid), list(shape), dt))

    def dsem(self, name):
        self.cx.uid += 1
        d = DSem(self.es.enter_context(self.cx.nc.semaphore("%s_%d" % (name, self.cx.uid))))
        self.dsems.append(d)
        self.cx.dsems.append(d)
        return d


class Ctx:
    def __init__(self, nc):
        self.nc = nc
        self.eng = {"pe": nc.tensor, "act": nc.scalar, "dve": nc.vector, "pool": nc.gpsimd, "sp": nc.sync}
        self.uid = 0
        self.gs = contextlib.ExitStack()
        self.sem = {}
        self.cnt = {}
        self.waited = {}
        for e in ("pe", "act", "dve", "pool", "sp"):
            self.sem[e] = self.gs.enter_context(nc.semaphore("tk_" + e))
            self.cnt[e] = 0
        self.dsems = []
        self.serial = False

    def scope(self):
        return Scope(self)

    def wait(self, cons, tick):
        if tick is None:
            return
        if tick[0] == "dma":
            d, v = tick[1], tick[2]
            k = (cons, "dma%d" % d.uid)
            if self.waited.get(k, 0) >= v:
                return
            self.eng[cons].wait_ge(d.sem, v)
            self.waited[k] = v
        else:
            prod, v = tick
            if v <= 0:
                return
            k = (cons, prod)
            if self.waited.get(k, 0) >= v:
                return
            self.eng[cons].wait_ge(self.sem[prod], v)
            self.waited[k] = v

    def _serial_waits(self, cons):
        for e in ("pe", "act", "dve", "pool"):
            if e != cons:
                self.wait(cons, (e, self.cnt[e]))
        for d in self.dsems:
            if d.cnt > 0:
                self.wait(cons, ("dma", d, d.cnt))

    def op(self, e, fn, deps=()):
        for d in deps:
            self.wait(e, d)
        if self.serial:
            self._serial_waits(e)
        c = self.cnt[e]
        self.wait(e, (e, c))
        inst = fn(self.eng[e])
        inst.then_inc(self.sem[e], 1)
        self.cnt[e] = c + 1
        return (e, c + 1)

    def pe(self, fn, deps=(), tick=False):
        for d in deps:
            self.wait("pe", d)
        if self.serial:
            self._serial_waits("pe")
            tick = True
        inst = fn(self.eng["pe"])
        if tick:
            inst.then_inc(self.sem["pe"], 1)
            self.cnt["pe"] += 1
            return ("pe", self.cnt["pe"])
        return None

    def dma(self, dsem, out, in_, deps=(), q="sp"):
        for d in deps:
            self.wait(q, d)
        if self.serial:
            self._serial_waits(q)
        self.eng[q].dma_start(out=out, in_=in_).then_inc(dsem.sem, 16)
        dsem.cnt += 16
        return ("dma", dsem, dsem.cnt)

    def barrier(self):
        for e in ("pe", "act", "dve", "pool"):
            self.wait("sp", (e, self.cnt[e]))
        for d in self.dsems:
            if d.cnt > 0:
                self.wait("sp", ("dma", d, d.cnt))
        for d in self.dsems:
            if d.cnt > 0:
                self.eng["sp"].sem_clear(d.sem)
                d.cnt = 0
                for k in [k for k in self.waited if k[1] == "dma%d" % d.uid]:
                    del self.waited[k]
        self.eng["sp"].sem_inc(self.sem["sp"], 1)
        self.cnt["sp"] += 1
        for e in ("pe", "act", "dve", "pool"):
            self.wait(e, ("sp", self.cnt["sp"]))

    def tt(self, e, out, a, b, op, deps=()):
        return self.op(e, lambda E: E.tensor_tensor(out=out, in0=a, in1=b, op=op), deps)

    def ts(self, e, out, a, s1, s2, op0, op1=None, deps=()):
        if op1 is None:
            return self.op(e, lambda E: E.tensor_scalar(out=out, in0=a, scalar1=s1, scalar2=None, op0=op0), deps)
        return self.op(e, lambda E: E.tensor_scalar(out=out, in0=a, scalar1=s1, scalar2=s2, op0=op0, op1=op1), deps)

    def stt(self, e, out, in0, scalar, in1, op0, op1, deps=()):
        return self.op(e, lambda E: E.scalar_tensor_tensor(out=out, in0=in0, scalar=scalar, in1=in1, op0=op0, op1=op1), deps)

    def cp(self, e, out, in_, deps=()):
        if e == "act":
            return self.op(e, lambda E: E.copy(out=out, in_=in_), deps)
        return self.op(e, lambda E: E.tensor_copy(out=out, in_=in_), deps)

    def act(self, out, in_, func, bias=None, scale=1.0, accum=None, deps=(), saturate=None):
        kw = {}
        if saturate is not None:
            kw["saturate"] = saturate
        if bias is not None:
            kw["bias"] = bias
        if accum is not None:
            kw["accum_out"] = accum
        return self.op("act", lambda E: E.activation(out=out, in_=in_, func=func, scale=scale, **kw), deps)

    def memset(self, e, ap, v, deps=()):
        return self.op(e, lambda E: E.memset(ap, v), deps)


def tmax(*ticks):
    return [t for t in ticks if t is not None]


def load_consts(cx, G, T):
    cx.serial = True
    G.consts = G.sb("consts", [128, 266], F32)
    G.cmaskb = G.sb("cmaskb", [128, 8, 128], BF16)
    G.mask8b = G.sb("mask8b", [128, 8], BF16)
    G.identb = G.sb("identb", [128, 128], BF16)
    G.ones = G.sb("ones", [128, 128], F32)
    G.pcol = G.sb("pcol", [128, NPC], F32)
    G.h1 = G.sb("h1", [128, 128], F32)
    G.h2 = G.sb("h2", [128, 128], F32)
    G.eps = G.sb("epsc", [128, 1], F32)
    G.a8 = G.sb("a8g", [128, 2, 128], F32)
    G.ident = G.consts[:, 0:128]
    G.bdmask = G.consts[:, 128:256]
    G.mask8 = G.consts[:, 256:264]
    G.sgn = G.consts[:, 264:265]
    G.flag = G.consts[:, 265:266]
    with cx.scope() as sc:
        ld = sc.dsem("ld")
        cmf = sc.sb("cmf", [128, 8, 128], F32)
        pv = sc.sb("pv", [128, 5, 128], F32)
        ps = sc.ps("ps", [128, 2, 512], F32)
        cx.dma(ld, G.consts[:], T["consts"])
        cx.dma(ld, cmf[:], T["cmask"])
        cx.dma(ld, pv[:], T["pvec"].rearrange("(a p) m -> p a m", p=128))
        cx.cp("dve", G.cmaskb[:], cmf[:])
        cx.cp("dve", G.mask8b[:], G.mask8)
        cx.cp("dve", G.identb[:], G.ident)
        cx.memset("dve", G.ones[:], 1.0)
        cx.memset("dve", G.eps[:], EPS)
        cx.memset("dve", G.h1[:], 0.0)
        cx.memset("dve", G.h2[:], 0.0)
        for a in range(5):
            bank, off = (0, a * 128) if a < 4 else (1, 0)
            cx.pe(lambda E, a=a, bank=bank, off=off: E.transpose(ps[:, bank, off:off + 128], pv[:, a, :], G.ident))
        cx.cp("dve", G.pcol[:, 0:512], ps[:, 0, :])
        cx.cp("dve", G.pcol[:, 512:640], ps[:, 1, 0:128])
        cx.barrier()
    cx.serial = False


def ssm_setup(cx, G, T):
    cx.serial = True
    with cx.scope() as sc:
        ld = sc.dsem("ld")
        st = sc.dsem("st")
        sin = sc.sb("ssmin", [128, 384 + 4 * 2048], F32)
        cx.dma(ld, sin[:], T["ssmin"])
        lr2 = sin[:, 0:128]
        li2 = sin[:, 128:256]
        lst = sin[:, 256:384]
        Bs = sin[:, 384:384 + 2048]
        Bx = sin[:, 384 + 2048:384 + 4096]
        Cs = sin[:, 384 + 4096:384 + 6144]
        Cx = sin[:, 384 + 6144:384 + 8192]
        tabs = sc.sb("tabs", [128, 9, 2, 128], F32)
        w = [sc.sb("w%d" % i, [128, 128], F32) for i in range(8)]
        wi = sc.sb("wi", [128, 128], I32)
        step, lrs, lis, mag, kf, kf2, r, sc_ = w
        cx.act(step[:], lst, AF.Exp)
        cx.tt("dve", lrs[:], lr2, step[:], ALU.mult)
        cx.tt("dve", lis[:], li2, step[:], ALU.mult)
        cx.memset("dve", tabs[:, 0, 0, :], 1.0)
        cx.memset("dve", tabs[:, 0, 1, :], 0.0)
        for k in range(1, 9):
            cx.act(mag[:], lrs[:], AF.Exp, scale=float(k))
            for which, shift in ((1, 0.0), (0, 0.25)):
                cx.ts("dve", kf[:], lis[:], k / TWO_PI, shift, ALU.mult, ALU.add)
                cx.cp("dve", wi[:], kf[:])
                cx.cp("dve", kf2[:], wi[:])
                cx.tt("dve", r[:], kf[:], kf2[:], ALU.subtract)
                cx.ts("dve", r[:], r[:], TWO_PI, PI_SAFE, ALU.mult, ALU.min)
                cx.ts("dve", r[:], r[:], -PI_SAFE, None, ALU.max)
                cx.act(sc_[:], r[:], AF.Sin)
                cx.tt("dve", tabs[:, k, which, :], mag[:], sc_[:], ALU.mult)
        den, nr, t1, t2, cr, ci, cisg, tmp = w
        cx.tt("dve", den[:], lr2, lr2, ALU.mult)
        cx.tt("dve", t1[:], li2, li2, ALU.mult)
        cx.tt("dve", den[:], den[:], t1[:], ALU.add)
        cx.op("dve", lambda E: E.reciprocal(out=den[:], in_=den[:]))
        cx.ts("dve", nr[:], tabs[:, 1, 0, :], -1.0, None, ALU.add)
        ni = tabs[:, 1, 1, :]
        cx.tt("dve", t1[:], nr[:], lr2, ALU.mult)
        cx.tt("dve", t2[:], ni, li2, ALU.mult)
        cx.tt("dve", t1[:], t1[:], t2[:], ALU.add)
        cx.tt("dve", cr[:], t1[:], den[:], ALU.mult)
        cx.tt("dve", t1[:], ni, lr2, ALU.mult)
        cx.tt("dve", t2[:], nr[:], li2, ALU.mult)
        cx.tt("dve", t1[:], t1[:], t2[:], ALU.subtract)
        cx.tt("dve", ci[:], t1[:], den[:], ALU.mult)
        cx.ts("dve", cisg[:], ci[:], G.sgn, None, ALU.mult)

        def bc(tab):
            return tab.unsqueeze(2).broadcast_to([128, 128, 16])

        def v3(ap):
            return ap.rearrange("p (g h) -> p g h", h=16)

        big = [sc.sb("big%d" % i, [128, 2048], F32) for i in range(5)]
        bbs, bbx, ta, tb, sk = big
        cx.tt("dve", v3(ta[:]), v3(Bs), bc(cr[:]), ALU.mult)
        cx.tt("dve", v3(tb[:]), v3(Bx), bc(cisg[:]), ALU.mult)
        cx.tt("dve", bbs[:], ta[:], tb[:], ALU.add)
        cx.tt("dve", v3(ta[:]), v3(Bx), bc(cr[:]), ALU.mult)
        cx.tt("dve", v3(tb[:]), v3(Bs), bc(cisg[:]), ALU.mult)
        cx.tt("dve", bbx[:], ta[:], tb[:], ALU.subtract)
        ps = sc.ps("ps", [128, 4, 512], F32)
        stg = [sc.sb("stg%d" % i, [128, 16, 128], BF16) for i in range(2)]
        stg_t = [None, None]
        nstg = [0]
        sa, sb_ = w[0], w[1]

        def emit_T(src, dst_dram, kidx):
            s = stg[nstg[0] % 2]
            nstg[0] += 1
            for q in range(4):
                for j in range(4):
                    cc = q * 4 + j
                    cx.pe(lambda E, q=q, j=j, cc=cc: E.transpose(ps[:, q, j * 128:(j + 1) * 128], src[:, cc * 128:(cc + 1) * 128], G.ident))
                cx.cp("act", s[:, q * 4:(q + 1) * 4, :], ps[:, q, :].rearrange("p (a m) -> p a m", m=128))
            cx.dma(st, dst_dram[:, :, kidx, :].rearrange("c p m -> p c m"), s[:])

        def emit_plain(src, dst_dram, kidx):
            s = stg[nstg[0] % 2]
            nstg[0] += 1
            cx.cp("act", s[:], src.rearrange("p (c m) -> p c m", m=128))
            cx.dma(st, dst_dram[:, :, kidx, :].rearrange("c p m -> p c m"), s[:])

        for k in range(8):
            ARk = tabs[:, k, 0, :]
            AIk = tabs[:, k, 1, :]
            cx.ts("dve", sa[:], AIk, G.sgn, None, ALU.mult)
            cx.tt("dve", v3(ta[:]), v3(bbs[:]), bc(ARk), ALU.mult)
            cx.tt("dve", v3(tb[:]), v3(bbx[:]), bc(sa[:]), ALU.mult)
            cx.tt("dve", sk[:], ta[:], tb[:], ALU.add)
            emit_T(sk, T["SkT"], k)
            cx.ts("dve", sa[:], ARk, G.sgn, None, ALU.mult)
            cx.tt("dve", v3(ta[:]), v3(bbx[:]), bc(sa[:]), ALU.mult)
            cx.tt("dve", v3(tb[:]), v3(bbs[:]), bc(AIk), ALU.mult)
            cx.tt("dve", sk[:], ta[:], tb[:], ALU.subtract)
            emit_T(sk, T["SJkT"], k)
        for tau in range(9):
            ARk = tabs[:, tau, 0, :]
            AIk = tabs[:, tau, 1, :]
            cx.ts("dve", sa[:], ARk, G.sgn, -1.0, ALU.mult, ALU.mult)
            cx.tt("dve", v3(ta[:]), v3(Cs), bc(sa[:]), ALU.mult)
            cx.tt("dve", v3(tb[:]), v3(Cx), bc(AIk), ALU.mult)
            cx.tt("dve", sk[:], ta[:], tb[:], ALU.subtract)
            if tau >= 1:
                emit_plain(sk[:], T["Rk"], tau - 1)
            if tau <= 7:
                s = stg[nstg[0] % 2]
                nstg[0] += 1
                for q in range(4):
                    for j in range(4):
                        cc = q * 4 + j
                        cx.pe(lambda E, q=q, j=j, cc=cc: E.matmul(ps[:, q, j * 128:(j + 1) * 128], lhsT=bbs[:, cc * 128:(cc + 1) * 128],
                                                                  rhs=sk[:, cc * 128:(cc + 1) * 128], start=True, stop=True))
                    cx.tt("dve", s[:, q * 4:(q + 1) * 4, :], ps[:, q, :].rearrange("p (a m) -> p a m", m=128),
                          G.bdmask.unsqueeze(1).broadcast_to([128, 4, 128]), ALU.mult)
                cx.dma(st, T["Kbd"][:, :, tau, :].rearrange("c p m -> p c m"), s[:])
        cx.cp("dve", G.a8[:], tabs[:, 8, :, :])
        cx.barrier()
    cx.serial = False


def load_norm(cx, G, AT, src, tiles, gcol, xT_out=None):
    with cx.scope() as sc:
        ld = [sc.dsem("ld%d" % i) for i in range(2)]
        st = [sc.dsem("st%d" % i) for i in range(2)]
        xtok = [sc.sb("xtok%d" % i, [128, D], F32) for i in range(2)]
        xT = [sc.sb("xTt%d" % i, [128, 32, 128], F32) for i in range(2)]
        junk = sc.sb("junk", [128, D], mybir.dt.float8e5)
        ssq = sc.sb("ssq", [128, 2], F32)
        rstd = sc.sb("rstd", [128, 2], F32)
        diag = [sc.sb("diag%d" % i, [128, 128], F32) for i in range(2)]
        rbs = [sc.sb("rbs%d" % i, [128, 128], F32) for i in range(2)]
        ps = sc.ps("ps", [128, 8, 512], F32)
        NB = 6
        bank_rel = [None] * NB
        slot_rel = [None, None]
        xT_rel = [[], []]
        rb_rel = [[], []]
        nj = 0
        for ti, (r0, nr, c0) in enumerate(tiles):
            sl = ti % 2
            t_ld = cx.dma(ld[sl], xtok[sl][0:nr, :], src[r0:r0 + nr, :], deps=[slot_rel[sl]] if slot_rel[sl] else [])
            cx.act(junk[0:nr, :], xtok[sl][0:nr, :], AF.Square, accum=ssq[0:nr, sl:sl + 1], deps=[t_ld] + rb_rel[sl], saturate=False)
            t_sq = cx.act(rstd[0:nr, sl:sl + 1], ssq[0:nr, sl:sl + 1], AF.Sqrt, bias=G.eps[0:nr, :], scale=1.0 / D)
            cx.op("dve", lambda E: E.reciprocal(out=rstd[0:nr, sl:sl + 1], in_=rstd[0:nr, sl:sl + 1]), deps=[t_sq])
            t_d = cx.ts("dve", diag[sl][0:nr, 0:nr], G.ident[0:nr, 0:nr], rstd[0:nr, sl:sl + 1], None, ALU.mult)
            ev_ticks = []
            tk = None
            for grp in range(8):
                b = nj % NB
                nj += 1
                deps = [t_ld]
                if bank_rel[b]:
                    deps.append(bank_rel[b])
                for j in range(4):
                    kc = grp * 4 + j
                    tk = cx.pe(lambda E, b=b, j=j, kc=kc: E.transpose(ps[:, b, j * 128:j * 128 + nr], xtok[sl][0:nr, kc * 128:(kc + 1) * 128],
                                                                      G.ident[0:nr, 0:nr]), deps=deps if j == 0 else (), tick=(j == 3))
                d2 = [tk]
                if grp == 0:
                    d2 += xT_rel[sl]
                t_ev = cx.cp("act", xT[sl][:, grp * 4:(grp + 1) * 4, 0:nr], ps[:, b, :].rearrange("p (a m) -> p a m", m=128)[:, :, 0:nr], deps=d2)
                bank_rel[b] = t_ev
                ev_ticks.append(t_ev)
            slot_rel[sl] = tk
            t_rb = cx.pe(lambda E: E.matmul(ps[:, 6 + sl, 0:nr], lhsT=G.ones[0:nr, :], rhs=diag[sl][0:nr, 0:nr], start=True, stop=True),
                         deps=[t_d], tick=True)
            t_rbs = cx.cp("act", rbs[sl][:, 0:nr], ps[:, 6 + sl, 0:nr], deps=[t_rb])
            st_tick = None
            if xT_out is not None:
                st_tick = cx.dma(st[sl], xT_out.rearrange("(kc p) n -> p kc n", p=128)[:, :, c0:c0 + nr], xT[sl][:, :, 0:nr], deps=[ev_ticks[-1]])
            d3 = [ev_ticks[-1], t_rbs] + ([st_tick] if st_tick else [])
            cx.tt("dve", xT[sl][:, :, 0:nr], xT[sl][:, :, 0:nr], rbs[sl][:, 0:nr].unsqueeze(1).broadcast_to([128, 32, nr]), ALU.mult, deps=d3)
            last = cx.tt("dve", AT[:, :, c0:c0 + nr], xT[sl][:, :, 0:nr], G.pcol[:, gcol:gcol + 32].unsqueeze(2).broadcast_to([128, 32, nr]), ALU.mult)
            rb_rel[sl] = [last]
            xT_rel[sl] = [last]
        cx.barrier()


class Gemm:
    def __init__(self, cx, sc, KC, nwst=4, nwbf=2, piece=4, cast_eng=("pool",)):
        self.cx = cx
        self.cast_eng = cast_eng
        self.ncast = 0
        self.KC = KC
        self.piece = piece
        self.npieces = (KC + piece - 1) // piece
        self.wst = [sc.sb("wst%d" % i, [128, piece, 128], F32) for i in range(nwst)]
        self.wld = [sc.dsem("wld%d" % i) for i in range(nwst)]
        self.wst_rel = [None] * nwst
        self.wbf = [sc.sb("wbf%d" % i, [128, KC, 128], BF16) for i in range(nwbf)]
        self.wbf_rel = [None] * nwbf
        self.nst = 0
        self.nbf = 0
        self.loaded = {}

    def load(self, W, k0, col0, key, ncols=128):
        cx = self.cx
        bs = self.nbf % len(self.wbf)
        self.nbf += 1
        wb = self.wbf[bs]
        last = None
        lasts = []
        for p in range(self.npieces):
            kc0 = p * self.piece
            n = min(self.piece, self.KC - kc0)
            ss = self.nst % len(self.wst)
            self.nst += 1
            src = W[(k0 + kc0) * 128:(k0 + kc0 + n) * 128, col0:col0 + ncols].rearrange("(kc p) m -> p kc m", p=128)
            t_ld = cx.dma(self.wld[ss], self.wst[ss][:, 0:n, 0:ncols], src, deps=[self.wst_rel[ss]] if self.wst_rel[ss] else [])
            deps = [t_ld]
            if self.wbf_rel[bs]:
                deps.append(self.wbf_rel[bs])
            ce = self.cast_eng[self.ncast % len(self.cast_eng)]
            self.ncast += 1
            last = cx.cp(ce, wb[:, kc0:kc0 + n, 0:ncols], self.wst[ss][:, 0:n, 0:ncols], deps=deps)
            self.wst_rel[ss] = last
            lasts.append(last)
        self.loaded[key] = (wb, lasts[-len(set(self.cast_eng)) * 2:], bs)
        return wb, lasts

    def release(self, key, tick):
        wb, last, bs = self.loaded.pop(key)
        self.wbf_rel[bs] = tick


def run_gemm(cx, sc, ps, banks, AT, KC, W, k0, coltiles, blocks, epilogue, prefetch=1, pre=None, post=None, cast_eng=("pool",)):
    g = Gemm(cx, sc, KC, cast_eng=cast_eng)
    nM = len(coltiles)
    for i in range(min(prefetch, nM)):
        g.load(W, k0, coltiles[i], i)
    bank_rel = {b: None for b in banks}
    nj = 0
    for mi in range(nM):
        if mi + prefetch < nM:
            g.load(W, k0, coltiles[mi + prefetch], mi + prefetch)
        wb, wt = g.loaded[mi][0], g.loaded[mi][1]
        if pre:
            pre(mi)
        tk = None
        for bi, (c0, n) in enumerate(blocks):
            b = banks[nj % len(banks)]
            nj += 1
            deps = list(wt)
            if bank_rel[b]:
                deps.append(bank_rel[b])
            for kc in range(KC):
                tk = cx.pe(lambda E, b=b, kc=kc: E.matmul(ps[:, b, 0:n], lhsT=wb[:, kc, :], rhs=AT[:, kc, c0:c0 + n],
                                                         start=(kc == 0), stop=(kc == KC - 1)),
                           deps=deps if kc == 0 else (), tick=(kc == KC - 1))
            bank_rel[b] = epilogue(mi, bi, ps[:, b, 0:n], n, c0, tk)
        g.release(mi, tk)
        if post:
            post(mi)


def ssq_finalize(cx, sc, G, ps, bank, acc, blocks, dnorm, rstd_row, st, deps, work=None):
    rs = work if work is not None else sc.sb("rsfin", [128, NT], F32)
    last = None
    rel = None
    for (c0, n) in blocks:
        d = list(deps)
        if rel:
            d.append(rel)
        tk = cx.pe(lambda E: E.matmul(ps[:, bank, 0:n], lhsT=G.ones[:], rhs=acc[:, c0:c0 + n], start=True, stop=True), deps=d, tick=True)
        t1 = cx.act(rs[:, c0:c0 + n], ps[:, bank, 0:n], AF.Sqrt, bias=G.eps[:], scale=1.0 / dnorm, deps=[tk])
        rel = t1
        last = cx.op("dve", lambda E: E.reciprocal(out=rs[:, c0:c0 + n], in_=rs[:, c0:c0 + n]), deps=[t1])
    return cx.dma(st, rstd_row, rs[0:1, :], deps=[last])


def norm_load(cx, G, AT, src, KC, N, gcol, rstd_rows, rows_for_kc):
    with cx.scope() as sc:
        ld = [sc.dsem("nl%d" % i) for i in range(3)]
        rl = sc.dsem("rl")
        xin = [sc.sb("xin%d" % i, [128, N], F32) for i in range(3)]
        rel = [None] * 3
        rb = {}
        for r in sorted(set(rows_for_kc)):
            rb[r] = sc.sb("rb%d" % r, [128, N], F32)
            t_rb = cx.dma(rl, rb[r][:], rstd_rows[r:r + 1, :].broadcast_to([128, N]))
        for kc in range(KC):
            s = kc % 3
            t_ld = cx.dma(ld[s], xin[s][:], src[kc * 128:(kc + 1) * 128, :], deps=[rel[s]] if rel[s] else [])
            rel[s] = cx.stt("dve", AT[:, kc, 0:N], xin[s][:], G.pcol[:, gcol + kc:gcol + kc + 1], rb[rows_for_kc[kc]][:],
                            ALU.mult, ALU.mult, deps=[t_ld, t_rb])
        cx.barrier()


def plain_load(cx, AT, src, KC, N):
    with cx.scope() as sc:
        ld = sc.dsem("pl")
        for kc0 in range(0, KC, 8):
            n = min(8, KC - kc0)
            cx.dma(ld, AT[:, kc0:kc0 + n, 0:N], src[kc0 * 128:(kc0 + n) * 128, :].rearrange("(kc p) n -> p kc n", p=128))
        cx.barrier()


def ssm_main(cx, G, sc0, uT, T, N, own):
    NCH = N // 8
    if own:
        subs = [(0, 1)] + [(1 + 32 * i, 32) for i in range(8)]
    else:
        subs = [(32 * i, 32) for i in range(8)]
    with cx.scope() as sc:
        Hst = sc.sb("Hst", [128, 128, NCH + 1], BF16) if own else None
        t1 = sc.sb("sct1", [128, 128], F32)
        t2 = sc.sb("sct2", [128, 128], F32)
        t3 = sc.sb("sct3", [128, 128], F32)
        t4 = sc.sb("sct4", [128, 128], F32)
        sinj = cx.scope()
        sinj.__enter__()
        skl = [sinj.dsem("skl%d" % i) for i in range(2)]
        skt = [sinj.sb("skt%d" % i, [128, 2, 8, 128], BF16) for i in range(2)]
        skt_rel = [None, None]
        um = [sinj.sb("um%d" % i, [128, 8, 256], BF16) for i in range(2)]
        um_rel = [None, None]
        bst = [sinj.sb("bst%d" % i, [128, 2, 128, 32], BF16) for i in range(2)]
        bst_rel = [None, None]
        a8 = G.a8
        t_a8 = None
        ps = sc.ps("ps", [128, 8, 512], F32)
        banks = list(range(8))
        bank_rel = [None] * 8
        nj = 0
        nl = 0
        AR8 = a8[:, 0, :]
        AI8 = a8[:, 1, :]
        t_state = None
        if own:
            t_state = cx.cp("act", Hst[:, :, 0], G.h1[:])
        for si, (cb, ncb) in enumerate(subs):
            bs = si % 2
            n = ncb * 8
            c0 = cb * 8
            ev_last = None
            for cc in range(16):
                s = nl % 2
                nl += 1
                d = [skt_rel[s]] if skt_rel[s] else []
                t_l1 = cx.dma(skl[s], skt[s][:, 0, :, :], T["SkT"][cc], deps=d)
                t_l2 = cx.dma(skl[s], skt[s][:, 1, :, :], T["SJkT"][cc])
                d = [um_rel[s]] if um_rel[s] else []
                t_um = cx.tt("pool", um[s][:, :, 0:n], uT[:, cc, c0:c0 + n].unsqueeze(1).broadcast_to([128, 8, n]),
                             G.mask8b[:, :].unsqueeze(2).broadcast_to([128, 8, n]), ALU.mult, deps=d)
                for dual in range(2):
                    b = banks[nj % 8]
                    nj += 1
                    deps = [t_l2, t_um]
                    if bank_rel[b]:
                        deps.append(bank_rel[b])
                    tk = None
                    for j in range(8):
                        tk = cx.pe(lambda E, b=b, j=j, s=s, dual=dual: E.matmul(
                            ps[:, b, 0:8 * ncb].rearrange("p (g c) -> p g c", c=ncb),
                            lhsT=skt[s][:, dual, 7 - j, :],
                            rhs=um[s][:, :, 0:n].rearrange("p g (c j) -> p g c j", j=8)[:, :, :, j],
                            start=(j == 0), stop=(j == 7)), deps=deps if j == 0 else (), tick=(j == 7))
                    d2 = [tk]
                    if cc == 0 and dual == 0 and bst_rel[bs]:
                        d2.append(bst_rel[bs])
                    ev = cx.cp("act", bst[bs][:, dual, cc * 8:(cc + 1) * 8, 0:ncb],
                               ps[:, b, 0:8 * ncb].rearrange("p (g c) -> p g c", c=ncb), deps=d2)
                    bank_rel[b] = ev
                    ev_last = ev
                skt_rel[s] = tk
                um_rel[s] = tk
            for c in range(ncb):
                b1 = bst[bs][:, 0, :, c]
                b2 = bst[bs][:, 1, :, c]
                d = [ev_last, t_state] if c == 0 else []
                cx.tt("dve", t1[:], AR8, G.h1[:], ALU.mult, deps=d)
                cx.tt("dve", t2[:], AI8, G.h2[:], ALU.mult)
                cx.tt("dve", t3[:], AR8, G.h2[:], ALU.mult)
                cx.tt("dve", t4[:], AI8, G.h1[:], ALU.mult)
                cx.tt("dve", t1[:], t1[:], t2[:], ALU.add)
                cx.tt("dve", t3[:], t3[:], t4[:], ALU.subtract)
                if own:
                    t_state = cx.tt("dve", Hst[:, :, cb + c + 1], t1[:], b1, ALU.add)
                t_h1 = cx.tt("dve", G.h1[:], t1[:], b1, ALU.add)
                t_h2 = cx.tt("dve", G.h2[:], t3[:], b2, ALU.add)
            bst_rel[bs] = t_h2
        cx.barrier()
        sinj.__exit__(None, None, None)
        if not own:
            return
        kl = [sc.dsem("kl%d" % i) for i in range(2)]
        kb = [sc.sb("kb%d" % i, [128, 2, 8, 128], BF16) for i in range(2)]
        kb_rel = [None, None]
        rp = [sc.sb("rp%d" % i, [128, 8, 8, 128], BF16) for i in range(2)]
        rp_rel = [None, None]
        yt = sc.sb("yt", [128, 512], F32)
        y2 = sc.sb("y2", [128, 512], F32)
        sg = sc.sb("sg", [128, 512], F32)
        blocks = BLK_OWN
        bank_rel = [None] * 8
        nj = 0
        for cc in range(16):
            s = cc % 2
            d = [kb_rel[s]] if kb_rel[s] else []
            cx.dma(kl[s], kb[s][:, 0, :, :], T["Kbd"][cc], deps=d)
            t_kl = cx.dma(kl[s], kb[s][:, 1, :, :], T["Rk"][cc])
            t_rp = None
            for j in range(8):
                d = [t_kl]
                if j == 0 and rp_rel[s]:
                    d.append(rp_rel[s])
                t_rp = cx.tt("pool", rp[s][:, j, :, :], kb[s][:, 1, j, :].unsqueeze(1).broadcast_to([128, 8, 128]), G.cmaskb[:], ALU.mult, deps=d)
            bb = []
            for bi in range(len(blocks)):
                b = banks[nj % 8]
                nj += 1
                bb.append(b)
            first_deps = [t_kl, t_rp, t_state] + [bank_rel[b] for b in bb if bank_rel[b]]
            first = True
            for tau in range(8):
                for bi, (c0, n) in enumerate(blocks):
                    ncb = n // 8
                    b = bb[bi]
                    cx.pe(lambda E, b=b, tau=tau, c0=c0, n=n: E.matmul(
                        ps[:, b, 0:n].rearrange("p (c j) -> p c j", j=8)[:, :, tau:8],
                        lhsT=kb[s][:, 0, tau, :],
                        rhs=uT[:, cc, c0:c0 + n].rearrange("p (c j) -> p c j", j=8)[:, :, 0:8 - tau],
                        start=(tau == 0), stop=False), deps=first_deps if first else ())
                    first = False
            tk = None
            for gl in range(8):
                g = cc * 8 + gl
                for j in range(8):
                    for bi, (c0, n) in enumerate(blocks):
                        ncb = n // 8
                        cb = c0 // 8
                        b = bb[bi]
                        lastmm = (gl == 7 and j == 7)
                        tk = cx.pe(lambda E, b=b, j=j, gl=gl, g=g, cb=cb, ncb=ncb, n=n, lastmm=lastmm: E.matmul(
                            ps[:, b, 0:n].rearrange("p (c j) -> p c j", j=8)[:, :, j],
                            lhsT=rp[s][:, j, gl, :],
                            rhs=Hst[:, g, cb:cb + ncb],
                            start=False, stop=lastmm), tick=(lastmm and bi == len(blocks) - 1))
            kb_rel[s] = tk
            rp_rel[s] = tk
            dcol = G.pcol[:, PC["ssm_d"] + cc:PC["ssm_d"] + cc + 1]
            for bi, (c0, n) in enumerate(blocks):
                b = bb[bi]
                cx.stt("dve", yt[:, 0:n], uT[:, cc, c0:c0 + n], dcol, ps[:, b, 0:n], ALU.mult, ALU.add, deps=[tk])
                cx.tt("dve", y2[:, 0:n], yt[:, 0:n], yt[:, 0:n], ALU.mult)
                cx.ts("dve", y2[:, 0:n], y2[:, 0:n], 0.044715 * 1.5957691216057308, 1.5957691216057308, ALU.mult, ALU.add)
                t_w = cx.tt("dve", y2[:, 0:n], y2[:, 0:n], yt[:, 0:n], ALU.mult)
                t_sg = cx.act(sg[:, 0:n], y2[:, 0:n], AF.Sigmoid, deps=[t_w])
                t_z = cx.tt("dve", uT[:, cc, c0:c0 + n], yt[:, 0:n], sg[:, 0:n], ALU.mult, deps=[t_sg])
                bank_rel[b] = t_z
        cx.barrier()


def build_program(stop_after=None, force_dbg=False):
    nc = bass.Bass("TRN2", target_bir_lowering=False)
    T = {}

    def din(n, shp, dt=F32):
        T[n] = nc.dram_tensor(n, list(shp), dt, kind="ExternalInput").ap()

    dbg = (stop_after is not None) or force_dbg

    def dscr(n, shp, dt=F32):
        T[n] = nc.dram_tensor(n, list(shp), dt, kind=("ExternalOutput" if dbg else "Internal")).ap()

    din("xown", [NT, D])
    din("xprev", [NPREV, D])
    din("memt", [NMEM, D])
    din("consts", [128, 266])
    din("cmask", [128, 8, 128])
    din("pvec", [NPC, 128])
    din("ssmin", [128, 384 + 8192])
    order = ["setup", "prev", "win", "ssm", "glu", "wout", "attn", "down0", None]
    lvl = order.index(stop_after)
    if lvl >= 1:
        din("w_in", [D, 8192])
    if lvl >= 4:
        din("glu_w", [2048, 2048])
    if lvl >= 5:
        din("w_out", [D, D])
    if lvl >= 6:
        din("wq", [D, D])
        din("wk", [D, D])
        din("wv", [D, D])
        din("wo", [D, D])
    if lvl >= 7:
        din("w_up", [D, 2 * DFF])
        din("w_down", [DFF, D])
    T["out"] = nc.dram_tensor("out", [NTOK, D], F32, kind="ExternalOutput").ap()
    dscr("SkT", [16, 128, 8, 128], BF16)
    dscr("SJkT", [16, 128, 8, 128], BF16)
    dscr("Rk", [16, 128, 8, 128], BF16)
    dscr("Kbd", [16, 128, 8, 128], BF16)
    dscr("xT", [D, NT], F32)
    dscr("ymix", [D, NT], F32)
    dscr("uTp", [2048, NPREV], BF16)
    dscr("uTo", [2048, NT], BF16)
    dscr("qT", [D, NT], BF16)
    dscr("oT", [D, NT], BF16)
    dscr("actT", [DFF, NT], BF16)
    dscr("rstd", [4, NT], F32)
    if dbg:
        dscr("dbg_h", [128, 128], F32)
        dscr("dbg_z", [2048, NT], BF16)

    cx = Ctx(nc)
    with cx.gs:
        G = cx.scope()
        with G:
            load_consts(cx, G, T)
            ssm_setup(cx, G, T)
            if stop_after == "setup":
                return nc
            mixer(cx, G, T, stop_after)
            if stop_after in ("prev", "win", "ssm", "glu"):
                return nc
            rest(cx, G, T, stop_after)
    return nc


def gemm_u(cx, G, AT, T, dst, blocks, N):
    with cx.scope() as sc:
        ps = sc.ps("ps", [128, 8, 512], F32)
        st = [sc.dsem("st%d" % i) for i in range(2)]
        ust = [sc.sb("ust%d" % i, [128, N], BF16) for i in range(2)]
        rel = [None, None]
        state = {}

        def epi(mi, bi, bank, n, c0, tk):
            s = mi % 2
            deps = [tk]
            if bi == 0 and rel[s]:
                deps.append(rel[s])
            t = cx.cp("act", ust[s][:, c0:c0 + n], bank, deps=deps)
            state["last"] = t
            return t

        def post(mi):
            s = mi % 2
            rel[s] = cx.dma(st[s], dst[mi * 128:(mi + 1) * 128, :], ust[s][:, 0:N], deps=[state["last"]])

        run_gemm(cx, sc, ps, list(range(8)), AT, 32, T["w_in"], 0, [128 * m for m in range(16)], blocks, epi, post=post)
        cx.barrier()


def gemm_conv(cx, G, AT, T):
    with cx.scope() as sc:
        ps = sc.ps("ps", [128, 8, 512], F32)
        st = sc.dsem("st")
        st2 = sc.dsem("st2")
        cv = sc.sb("cv", [128, 2 + NT], F32)
        gb = sc.sb("gb", [128, NT], F32)
        yc = sc.sb("yc", [128, NT], F32)
        acc = sc.sb("acc", [128, NT], F32)
        t0 = cx.memset("dve", cv[:, 0:2], 0.0)
        cx.memset("dve", acc[:], 0.0)
        state = {"cv_rel": None, "gb_rel": None, "last": {}}
        coltiles = []
        for i in range(16):
            coltiles += [4096 + 128 * i, 6144 + 128 * i, 2048 + 128 * i]

        def epi(mi, bi, bank, n, c0, tk):
            i, which = mi // 3, mi % 3
            if which == 0:
                deps = [tk]
                if bi == 0 and state["cv_rel"]:
                    deps.append(state["cv_rel"])
                t = cx.cp("act", cv[:, 2 + c0:2 + c0 + n], bank, deps=deps)
                state["last"][(0, bi)] = t
            elif which == 1:
                t = cx.tt("dve", cv[:, 2 + c0:2 + c0 + n], bank, cv[:, 2 + c0:2 + c0 + n], ALU.mult, deps=[tk, state["last"][(0, bi)]])
                state["last"][1] = t
            else:
                deps = [tk]
                if bi == 0 and state["gb_rel"]:
                    deps += state["gb_rel"]
                t = cx.cp("act", gb[:, c0:c0 + n], bank, deps=deps)
                state["last"][2] = t
            return t

        def post(mi):
            i, which = mi // 3, mi % 3
            if which != 2:
                return
            w0 = G.pcol[:, PC["cw0"] + i:PC["cw0"] + i + 1]
            w1 = G.pcol[:, PC["cw1"] + i:PC["cw1"] + i + 1]
            w2 = G.pcol[:, PC["cw2"] + i:PC["cw2"] + i + 1]
            cx.ts("dve", yc[:], cv[:, 2:2 + NT], w2, None, ALU.mult, deps=[state["last"][1], state["last"][2]])
            cx.stt("dve", yc[:], cv[:, 1:1 + NT], w1, yc[:], ALU.mult, ALU.add)
            t_c = cx.stt("dve", yc[:], cv[:, 0:NT], w0, yc[:], ALU.mult, ALU.add)
            state["cv_rel"] = t_c
            t_y = cx.tt("dve", gb[:], gb[:], yc[:], ALU.mult)
            t_s = cx.act(yc[:], gb[:], AF.Square, deps=[t_y])
            t_a = cx.tt("pool", acc[:], acc[:], yc[:], ALU.add, deps=[t_s])
            cx.wait("dve", t_a)
            t_st = cx.dma(st, T["ymix"][2048 + 128 * i:2048 + 128 * (i + 1), :], gb[:], deps=[t_y])
            state["gb_rel"] = [t_st, t_s]
            state["acc"] = t_a

        run_gemm(cx, sc, ps, list(range(7)), AT, 32, T["w_in"], 0, coltiles, BLK_OWN, epi, post=post)
        ssq_finalize(cx, sc, G, ps, 7, acc, BLK_OWN, 2048.0, T["rstd"][1:2, :], st2, [state["acc"]], work=yc)
        cx.barrier()


def gemm_glu(cx, G, zT, T):
    with cx.scope() as sc:
        ps = sc.ps("ps", [128, 8, 512], F32)
        st = [sc.dsem("st%d" % i) for i in range(2)]
        st2 = sc.dsem("st2")
        gt = sc.sb("gt", [128, 512], F32)
        yst = [sc.sb("yst%d" % i, [128, NT], F32) for i in range(2)]
        sq = sc.sb("sq", [128, NT], F32)
        acc = sc.sb("acc", [128, NT], F32)
        cx.memset("dve", acc[:], 0.0)
        rel = [None, None]
        state = {}

        def epi(mi, bi, bank, n, c0, tk):
            s = mi % 2
            bcol = G.pcol[:, PC["glu_b"] + mi:PC["glu_b"] + mi + 1]
            t_g = cx.act(gt[:, 0:n], bank, AF.Sigmoid, bias=bcol, deps=[tk] + ([state["y"]] if "y" in state else []))
            deps = [t_g]
            if bi == 0 and rel[s]:
                deps += rel[s]
            state["y"] = cx.tt("dve", yst[s][:, c0:c0 + n], zT[:, mi, c0:c0 + n], gt[:, 0:n], ALU.mult, deps=deps)
            return t_g

        def post(mi):
            s = mi % 2
            d = [state["y"]] + ([state["acc"]] if "acc" in state else [])
            t_s = cx.act(sq[:], yst[s][:], AF.Square, deps=d)
            state["acc"] = cx.tt("pool", acc[:], acc[:], sq[:], ALU.add, deps=[t_s])
            t_st = cx.dma(st[s], T["ymix"][128 * mi:128 * (mi + 1), :], yst[s][:], deps=[state["y"]])
            rel[s] = [t_st, t_s]

        run_gemm(cx, sc, ps, list(range(7)), zT, 16, T["glu_w"], 0, [128 * m for m in range(16)], BLK_OWN, epi, post=post)
        ssq_finalize(cx, sc, G, ps, 7, acc, BLK_OWN, 2048.0, T["rstd"][0:1, :], st2, [state["acc"]], work=sq)
        cx.barrier()


def dbg_dump(cx, T, name, ap):
    with cx.scope() as sc:
        d = sc.dsem("dbg")
        cx.serial = True
        cx.dma(d, T[name], ap)
        cx.barrier()
        cx.serial = False


def mixer(cx, G, T, stop_after):
    tiles_prev = [(128 * i, 128, 128 * i) for i in range(16)]
    tiles_own = [(0, 8, 0)] + [(8 + 128 * i, 128, 8 + 128 * i) for i in range(16)]
    with cx.scope() as s1:
        AT = s1.sb("AT", [128, 32, NT], BF16)
        load_norm(cx, G, AT, T["xprev"], tiles_prev, PC["g_mix"])
        gemm_u(cx, G, AT, T, T["uTp"], BLK_PREV, NPREV)
    with cx.scope() as s2:
        uT = s2.sb("uT", [128, 16, NT], BF16)
        plain_load(cx, uT, T["uTp"], 16, NPREV)
        ssm_main(cx, G, s2, uT, T, NPREV, own=False)
    if stop_after == "prev":
        dbg_dump(cx, T, "dbg_h", G.h1[:])
        return
    with cx.scope() as s3:
        AT = s3.sb("AT", [128, 32, NT], BF16)
        load_norm(cx, G, AT, T["xown"], tiles_own, PC["g_mix"], xT_out=T["xT"])
        gemm_u(cx, G, AT, T, T["uTo"], BLK_OWN, NT)
        gemm_conv(cx, G, AT, T)
    if stop_after == "win":
        return
    with cx.scope() as s4:
        uT = s4.sb("uT", [128, 16, NT], BF16)
        plain_load(cx, uT, T["uTo"], 16, NT)
        ssm_main(cx, G, s4, uT, T, NT, own=True)
        if stop_after == "ssm":
            with cx.scope() as sc:
                d = sc.dsem("dbg")
                cx.serial = True
                cx.dma(d, T["dbg_z"].rearrange("(kc p) n -> p kc n", p=128), uT[:])
                cx.barrier()
                cx.serial = False
            return
        gemm_glu(cx, G, uT, T)


def gemm_resid(cx, G, AT, KC, W, k0, T, final_ssq):
    with cx.scope() as sc:
        ps = sc.ps("ps", [128, 8, 512], F32)
        ld = [sc.dsem("xl%d" % i) for i in range(2)]
        st = [sc.dsem("xs%d" % i) for i in range(2)]
        st2 = sc.dsem("st2")
        xr = [sc.sb("xr%d" % i, [128, NT], F32) for i in range(2)]
        rel = [None, None]
        ldt = [None, None]
        state = {}
        if final_ssq:
            sq = sc.sb("sq", [128, NT], F32)
            acc = sc.sb("acc", [128, NT], F32)
            cx.memset("dve", acc[:], 0.0)

        def pre(mi):
            s = mi % 2
            ldt[s] = cx.dma(ld[s], xr[s][:], T["xT"][128 * mi:128 * (mi + 1), :], deps=rel[s] if rel[s] else [])

        def epi(mi, bi, bank, n, c0, tk):
            s = mi % 2
            t = cx.tt("dve", xr[s][:, c0:c0 + n], bank, xr[s][:, c0:c0 + n], ALU.add, deps=[tk, ldt[s]])
            state["last"] = t
            return t

        def post(mi):
            s = mi % 2
            r = []
            if final_ssq:
                d = [state["last"]] + ([state["acc"]] if "acc" in state else [])
                t_s = cx.act(sq[:], xr[s][:], AF.Square, deps=d)
                state["acc"] = cx.tt("pool", acc[:], acc[:], sq[:], ALU.add, deps=[t_s])
                r.append(t_s)
            t_st = cx.dma(st[s], T["xT"][128 * mi:128 * (mi + 1), :], xr[s][:], deps=[state["last"]])
            r.append(t_st)
            rel[s] = r

        banks = list(range(7)) if final_ssq else list(range(8))
        run_gemm(cx, sc, ps, banks, AT, KC, W, k0, [128 * m for m in range(32)], BLK_OWN, epi, pre=pre, post=post)
        if final_ssq:
            ssq_finalize(cx, sc, G, ps, 7, acc, BLK_OWN, float(D), T["rstd"][2:3, :], st2, [state["acc"]], work=sq)
        cx.barrier()


def gemm_store(cx, G, AT, W, dst):
    with cx.scope() as sc:
        ps = sc.ps("ps", [128, 8, 512], F32)
        st = [sc.dsem("st%d" % i) for i in range(2)]
        qs = [sc.sb("qs%d" % i, [128, NT], BF16) for i in range(2)]
        rel = [None, None]
        state = {}

        def epi(mi, bi, bank, n, c0, tk):
            s = mi % 2
            deps = [tk]
            if bi == 0 and rel[s]:
                deps.append(rel[s])
            t = cx.cp("act", qs[s][:, c0:c0 + n], bank, deps=deps)
            state["last"] = t
            return t

        def post(mi):
            s = mi % 2
            rel[s] = cx.dma(st[s], dst[mi * 128:(mi + 1) * 128, :], qs[s][:], deps=[state["last"]])

        run_gemm(cx, sc, ps, list(range(8)), AT, 32, W, 0, [128 * m for m in range(32)], BLK_OWN, epi, post=post)
        cx.barrier()


def gemm_up(cx, G, AT, T):
    with cx.scope() as sc:
        ps = sc.ps("ps", [128, 8, 512], F32)
        st = [sc.dsem("st%d" % i) for i in range(2)]
        a_sb = sc.sb("a_sb", [128, 2 + NT], F32)
        g_sb = sc.sb("g_sb", [128, NT], F32)
        tt_ = sc.sb("tconv", [128, NT], F32)
        ast = [sc.sb("ast%d" % i, [128, NT], BF16) for i in range(2)]
        cx.memset("dve", a_sb[:, 0:2], 0.0)
        rel = [None, None]
        state = {"a_rel": None, "g_rel": None}
        coltiles = []
        for i in range(86):
            coltiles += [128 * i, DFF + 128 * i]

        def epi(mi, bi, bank, n, c0, tk):
            i, which = mi // 2, mi % 2
            if which == 0:
                deps = [tk]
                if bi == 0 and state["a_rel"]:
                    deps.append(state["a_rel"])
                t = cx.cp("act", a_sb[:, 2 + c0:2 + c0 + n], bank, deps=deps)
                state["la"] = t
            else:
                deps = [tk]
                if bi == 0 and state["g_rel"]:
                    deps.append(state["g_rel"])
                t = cx.cp("act", g_sb[:, c0:c0 + n], bank, deps=deps)
                state["lg"] = t
            return t

        def post(mi):
            i, which = mi // 2, mi % 2
            if which != 1:
                return
            s = i % 2
            w0 = G.pcol[:, PC["fw0"] + i:PC["fw0"] + i + 1]
            w1 = G.pcol[:, PC["fw1"] + i:PC["fw1"] + i + 1]
            w2 = G.pcol[:, PC["fw2"] + i:PC["fw2"] + i + 1]
            cb = G.pcol[:, PC["fcb"] + i:PC["fcb"] + i + 1]
            cx.ts("dve", a_sb[:, 2:2 + HALO], a_sb[:, 2:2 + HALO], G.flag, None, ALU.mult, deps=[state["la"], state["lg"]])
            cx.ts("dve", tt_[:], a_sb[:, 2:2 + NT], w2, cb, ALU.mult, ALU.add)
            cx.stt("dve", tt_[:], a_sb[:, 1:1 + NT], w1, tt_[:], ALU.mult, ALU.add)
            t_c = cx.stt("dve", tt_[:], a_sb[:, 0:NT], w0, tt_[:], ALU.mult, ALU.add)
            state["a_rel"] = t_c
            t_s = cx.act(tt_[:], tt_[:], AF.Silu, deps=[t_c])
            t_m = cx.tt("dve", ast[s][:], tt_[:], g_sb[:], ALU.mult, deps=[t_s] + ([rel[s]] if rel[s] else []))
            state["g_rel"] = t_m
            rel[s] = cx.dma(st[s], T["actT"][128 * i:128 * (i + 1), :], ast[s][:], deps=[t_m])

        run_gemm(cx, sc, ps, list(range(8)), AT, 32, T["w_up"], 0, coltiles, BLK_OWN, epi, post=post)
        cx.barrier()


def attention(cx, G, T):
    with cx.scope() as sc:
        kT = sc.sb("kT", [128, 32, NMEM], BF16)
        vsb = sc.sb("vsb", [128, 2, D], BF16)
        with cx.scope() as s1:
            hmT = s1.sb("hmT", [128, 32, NMEM], BF16)
            load_norm(cx, G, hmT, T["memt"], [(0, 128, 0), (128, 128, 128)], PC["g_mem"])
            with cx.scope() as s2:
                ps = s2.ps("ps", [128, 8, 512], F32)

                def epi(mi, bi, bank, n, c0, tk):
                    return cx.cp("act", kT[:, mi, 0:NMEM], bank, deps=[tk])

                run_gemm(cx, s2, ps, list(range(8)), hmT, 32, T["wk"], 0, [128 * m for m in range(32)], [(0, NMEM)], epi, cast_eng=("dve", "pool", "dve", "act"))
                cx.barrier()
            with cx.scope() as s2:
                ps = s2.ps("ps", [128, 8, 512], F32)
                g = Gemm(cx, s2, 32, cast_eng=("dve", "pool", "dve", "act"))
                g.load(T["wv"], 0, 0, 0)
                bank_rel = [None] * 8
                nj = 0
                for mi in range(32):
                    if mi + 1 < 32:
                        g.load(T["wv"], 0, 128 * (mi + 1), mi + 1)
                    wb, wt = g.loaded[mi][0], g.loaded[mi][1]
                    tk = None
                    for tt_i in range(2):
                        b = nj % 8
                        nj += 1
                        deps = list(wt) + ([bank_rel[b]] if bank_rel[b] else [])
                        for kc in range(32):
                            tk = cx.pe(lambda E, b=b, kc=kc, tt_i=tt_i: E.matmul(ps[:, b, 0:128], lhsT=hmT[:, kc, tt_i * 128:(tt_i + 1) * 128],
                                                                                 rhs=wb[:, kc, :], start=(kc == 0), stop=(kc == 31)),
                                       deps=deps if kc == 0 else (), tick=(kc == 31))
                        bank_rel[b] = cx.cp("act", vsb[:, tt_i, 128 * mi:128 * (mi + 1)], ps[:, b, 0:128], deps=[tk])
                    g.release(mi, tk)
                cx.barrier()
        with cx.scope() as s3:
            ps = s3.ps("ps", [128, 6, 512], F32)
            psb = s3.ps("psb", [128, 2, 1024], BF16)
            ql = s3.dsem("ql")
            od = s3.dsem("od")
            qh = s3.sb("qh", [128, 8, NT], BF16)
            pT = s3.sb("pT", [128, 2, NT], BF16)
            es = s3.sb("es", [128, NMEM], F32)
            pb = s3.sb("pb", [128, NMEM], BF16)
            mx = s3.sb("mx", [128, 1], F32)
            nmx = s3.sb("nmx", [128, 1], F32)
            sm = s3.sb("sm", [128, 1], F32)
            rs = s3.sb("rs", [128, 1], F32)
            ost = s3.sb("ost", [128, NT], BF16)
            tiles = [(0, 8)] + [(8 + 128 * i, 128) for i in range(16)]
            scale = 1.0 / 32.0
            cx.serial = True
            for h in range(4):
                cx.dma(ql, qh[:], T["qT"][1024 * h:1024 * (h + 1), :].rearrange("(kc p) n -> p kc n", p=128))
                for ti, (c0, nr) in enumerate(tiles):
                    b = ti % 3
                    for dc in range(8):
                        cx.pe(lambda E, b=b, dc=dc, c0=c0, nr=nr: E.matmul(ps[0:nr, b, 0:NMEM], lhsT=qh[:, dc, c0:c0 + nr], rhs=kT[:, h * 8 + dc, :],
                                                                          start=(dc == 0), stop=(dc == 7)))
                    cx.op("dve", lambda E: E.reduce_max(out=mx[0:nr, :], in_=ps[0:nr, b, 0:NMEM], axis=AX.X))
                    cx.ts("dve", nmx[0:nr, :], mx[0:nr, :], -scale, None, ALU.mult)
                    cx.act(es[0:nr, :], ps[0:nr, b, 0:NMEM], AF.Exp, bias=nmx[0:nr, :], scale=scale, accum=sm[0:nr, :])
                    cx.op("dve", lambda E: E.reciprocal(out=rs[0:nr, :], in_=sm[0:nr, :]))
                    cx.ts("dve", pb[0:nr, :], es[0:nr, :], rs[0:nr, :], None, ALU.mult)
                    bb = ti % 2
                    for kc in range(2):
                        cx.pe(lambda E, bb=bb, kc=kc, nr=nr: E.transpose(psb[:, bb, kc * 128:kc * 128 + nr], pb[0:nr, kc * 128:(kc + 1) * 128],
                                                                        G.identb[0:nr, 0:nr]))
                    cx.cp("act", pT[:, :, c0:c0 + nr], psb[:, bb, 0:256].rearrange("p (k m) -> p k m", m=128)[:, :, 0:nr])
                for dvt in range(8):
                    for bi, (c0, n) in enumerate(BLK_OWN):
                        b = 3 + (bi % 3)
                        for kc in range(2):
                            cx.pe(lambda E, b=b, kc=kc, c0=c0, n=n: E.matmul(ps[:, b, 0:n], lhsT=vsb[:, kc, h * 1024 + dvt * 128:h * 1024 + (dvt + 1) * 128],
                                                                            rhs=pT[:, kc, c0:c0 + n], start=(kc == 0), stop=(kc == 1)))
                        cx.cp("act", ost[:, c0:c0 + n], ps[:, b, 0:n])
                    cx.dma(od, T["oT"][128 * (h * 8 + dvt):128 * (h * 8 + dvt + 1), :], ost[:])
            cx.barrier()
            cx.serial = False


def final_out(cx, G, T):
    with cx.scope() as sc:
        ps = sc.ps("ps", [128, 8, 512], F32)
        ld = [sc.dsem("fl%d" % i) for i in range(2)]
        st = [sc.dsem("fs%d" % i) for i in range(2)]
        rl = sc.dsem("rl")
        xf = [sc.sb("xf%d" % i, [128, 32, 128], F32) for i in range(2)]
        orow = [sc.sb("orow%d" % i, [128, D], F32) for i in range(2)]
        rb = sc.sb("rb", [128, NT], F32)
        t_rb = cx.dma(rl, rb[:], T["rstd"][2:3, :].broadcast_to([128, NT]))
        xf_rel = [None, None]
        or_rel = [None, None]
        bank_rel = [None] * 8
        nj = 0
        gcol = PC["g_final"]
        for ti in range(16):
            s = ti % 2
            c0 = 8 + 128 * ti
            t_ld = cx.dma(ld[s], xf[s][:], T["xT"].rearrange("(kc p) n -> p kc n", p=128)[:, :, c0:c0 + 128], deps=[xf_rel[s]] if xf_rel[s] else [])
            cx.tt("dve", xf[s][:], xf[s][:], rb[:, c0:c0 + 128].unsqueeze(1).broadcast_to([128, 32, 128]), ALU.mult, deps=[t_ld, t_rb])
            t_n = cx.tt("dve", xf[s][:], xf[s][:], G.pcol[:, gcol:gcol + 32].unsqueeze(2).broadcast_to([128, 32, 128]), ALU.mult)
            tk = None
            last_ev = None
            for grp in range(8):
                b = nj % 8
                nj += 1
                deps = [t_n] + ([bank_rel[b]] if bank_rel[b] else [])
                for j in range(4):
                    kc = grp * 4 + j
                    tk = cx.pe(lambda E, b=b, j=j, kc=kc, s=s: E.transpose(ps[:, b, j * 128:(j + 1) * 128], xf[s][:, kc, :], G.ident),
                               deps=deps if j == 0 else (), tick=(j == 3))
                d2 = [tk]
                if grp == 0 and or_rel[s]:
                    d2.append(or_rel[s])
                last_ev = cx.cp("act", orow[s][:, grp * 512:(grp + 1) * 512], ps[:, b, :], deps=d2)
                bank_rel[b] = last_ev
            xf_rel[s] = tk
            or_rel[s] = cx.dma(st[s], T["out"][128 * ti:128 * (ti + 1), :], orow[s][:], deps=[last_ev])
        cx.barrier()


def rest(cx, G, T, stop_after):
    with cx.scope() as s:
        AT = s.sb("AT", [128, 32, NT], BF16)
        norm_load(cx, G, AT, T["ymix"], 32, NT, PC["g_omix"], T["rstd"], [0] * 16 + [1] * 16)
        gemm_resid(cx, G, AT, 32, T["w_out"], 0, T, True)
    if stop_after == "wout":
        return
    with cx.scope() as s:
        AT = s.sb("AT", [128, 32, NT], BF16)
        norm_load(cx, G, AT, T["xT"], 32, NT, PC["g_xattn"], T["rstd"], [2] * 32)
        gemm_store(cx, G, AT, T["wq"], T["qT"])
    attention(cx, G, T)
    with cx.scope() as s:
        AT = s.sb("AT", [128, 32, NT], BF16)
        plain_load(cx, AT, T["oT"], 32, NT)
        gemm_resid(cx, G, AT, 32, T["wo"], 0, T, True)
    if stop_after == "attn":
        return
    with cx.scope() as s:
        AT = s.sb("AT", [128, 32, NT], BF16)
        norm_load(cx, G, AT, T["xT"], 32, NT, PC["g_ffn"], T["rstd"], [2] * 32)
        gemm_up(cx, G, AT, T)
    for p, (k0, kc) in enumerate([(0, 29), (29, 29), (58, 28)]):
        with cx.scope() as s:
            AT = s.sb("AT", [128, 29, NT], BF16)
            plain_load(cx, AT, T["actT"][k0 * 128:(k0 + kc) * 128, :], kc, NT)
            gemm_resid(cx, G, AT, kc, T["w_down"], k0, T, p == 2)
        if stop_after == "down0":
            return
    final_out(cx, G, T)


def _host_inputs(inp, cores):
    f = np.float32
    x = np.asarray(inp["x"], f)
    mem = np.asarray(inp["mem"], f)
    ident = np.eye(128, dtype=f)
    bd = np.kron(np.eye(8, dtype=f), np.ones((16, 16), f))
    mask8 = np.kron(np.eye(8, dtype=f), np.ones((16, 1), f))
    sgn = np.concatenate([-np.ones((64, 1), f), np.ones((64, 1), f)], 0)
    cmask = np.zeros((128, 8, 128), f)
    for gl in range(8):
        cmask[:, gl, gl * 16:(gl + 1) * 16] = 1.0
    vecs = {
        "g_mix": inp["norm_mix_g"][0], "ssm_d": inp["ssm_d"][0], "glu_b": inp["ssm_glu_b"][0],
        "cw0": inp["conv_w"][0, 0], "cw1": inp["conv_w"][0, 1], "cw2": inp["conv_w"][0, 2],
        "g_omix": np.concatenate([inp["out_norm_ssm_g"][0], inp["out_norm_conv_g"][0]]),
        "g_xattn": inp["norm_xattn_g"][0], "g_mem": inp["norm_mem_g"][0], "g_ffn": inp["norm_ffn_g"][0],
        "fw0": inp["ffn_conv_w"][0, 0], "fw1": inp["ffn_conv_w"][0, 1], "fw2": inp["ffn_conv_w"][0, 2],
        "fcb": inp["ffn_conv_b"][0], "g_final": inp["norm_final_g"],
    }
    pvec = np.zeros((NPC, 128), f)
    for k, v in vecs.items():
        v = np.asarray(v, f).reshape(-1, 128)
        pvec[PC[k]:PC[k] + v.shape[0]] = v
    lr = np.asarray(inp["ssm_lambda_re"][0], f).T
    li = np.asarray(inp["ssm_lambda_im"][0], f).T
    ls = np.broadcast_to(np.asarray(inp["ssm_log_step"][0], f)[None, :], (128, 128))
    br = np.asarray(inp["ssm_b_re"][0], f).transpose(1, 0, 2).reshape(64, 2048)
    bi = np.asarray(inp["ssm_b_im"][0], f).transpose(1, 0, 2).reshape(64, 2048)
    cr = np.asarray(inp["ssm_c_re"][0], f).transpose(2, 0, 1).reshape(64, 2048)
    ci = np.asarray(inp["ssm_c_im"][0], f).transpose(2, 0, 1).reshape(64, 2048)
    ssmin = np.concatenate([
        np.concatenate([lr, lr], 0), np.concatenate([li, li], 0), ls,
        np.concatenate([br, bi], 0), np.concatenate([bi, br], 0),
        np.concatenate([cr, ci], 0), np.concatenate([ci, cr], 0)], 1).astype(f)
    shared = {
        "cmask": cmask, "pvec": pvec, "ssmin": np.ascontiguousarray(ssmin),
        "w_in": np.asarray(inp["w_in"][0], f), "glu_w": np.asarray(inp["ssm_glu_w"][0], f),
        "w_out": np.asarray(inp["w_out"][0], f), "wq": np.asarray(inp["xattn_wq"][0], f),
        "wk": np.asarray(inp["xattn_wk"][0], f), "wv": np.asarray(inp["xattn_wv"][0], f),
        "wo": np.asarray(inp["xattn_wo"][0], f), "w_up": np.asarray(inp["ffn_w_up"][0], f),
        "w_down": np.asarray(inp["ffn_w_down"][0], f),
    }
    maps = []
    for c in cores:
        b, half = c // 2, c % 2
        xown = np.zeros((NT, D), f)
        xprev = np.zeros((NPREV, D), f)
        if half == 0:
            xown[HALO:] = x[b, 0:NTOK]
        else:
            xown[:] = x[b, NTOK - HALO:2 * NTOK]
            xprev[HALO:] = x[b, 0:NTOK - HALO]
        consts = np.concatenate([ident, bd, mask8, sgn, np.full((128, 1), float(half), f)], 1).astype(f)
        m = dict(shared)
        m.update({"xown": xown, "xprev": xprev, "memt": np.ascontiguousarray(mem[b]), "consts": np.ascontiguousarray(consts)})
        maps.append(m)
    return maps


_NC_CACHE = {}


def kernel(**inputs):
    cores = list(range(8))
    maps = _host_inputs(inputs, cores)
    if "nc" not in _NC_CACHE:
        _NC_CACHE["nc"] = build_program(None)
    res = run_bass_kernel_spmd(_NC_CACHE["nc"], maps, core_ids=cores)
    out = np.zeros((4, 2 * NTOK, D), np.float32)
    for c in cores:
        out[c // 2, (c % 2) * NTOK:(c % 2 + 1) * NTOK] = res.results[c]["out"]
    return out
```

```python
import math, contextlib
import numpy as np
import concourse.bass as bass
import concourse.mybir as mybir
from concourse.bass_utils import run_bass_kernel_spmd

F32 = mybir.dt.float32
BF16 = mybir.dt.bfloat16
I32 = mybir.dt.int32
AF = mybir.ActivationFunctionType
ALU = mybir.AluOpType
AX = mybir.AxisListType

D = 4096
NTOK = 2048
HALO = 8
NT = NTOK + HALO
NPREV = 2048
NMEM = 256
DFF = 11008
EPS = 1e-6
BLK_OWN = [(0, 8)] + [(8 + 512 * i, 512) for i in range(4)]
BLK_PREV = [(512 * i, 512) for i in range(4)]
TWO_PI = 2.0 * math.pi
PI_SAFE = 3.14159

PC = {}
_o = 0
for _n, _l in [("g_mix", 32), ("ssm_d", 16), ("glu_b", 16), ("cw0", 16), ("cw1", 16), ("cw2", 16),
               ("g_omix", 32), ("g_xattn", 32), ("g_mem", 32), ("g_ffn", 32),
               ("fw0", 86), ("fw1", 86), ("fw2", 86), ("fcb", 86), ("g_final", 32)]:
    PC[_n] = _o
    _o += _l
NPC = 640
assert _o <= NPC


class DSem:
    _n = 0

    def __init__(self, sem):
        self.sem = sem
        self.cnt = 0
        DSem._n += 1
        self.uid = DSem._n


class Scope:
    def __init__(self, cx):
        self.cx = cx
        self.es = contextlib.ExitStack()
        self.dsems = []

    def __enter__(self):
        self.es.__enter__()
        return self

    def __exit__(self, *a):
        for d in self.dsems:
            self.cx.dsems.remove(d)
        return self.es.__exit__(*a)

    def sb(self, name, shape, dt=F32):
        self.cx.uid += 1
        return self.es.enter_context(self.cx.nc.sbuf_tensor("%s_%d" % (name, self.cx.uid), list(shape), dt))

    def ps(self, name, shape, dt=F32):
        self.cx.uid += 1
        return self.es.enter_context(self.cx.nc.psum_tensor("%s_%d" % (name, self.cx.uid), list(shape), dt))

    def dsem(self, name):
        self.cx.uid += 1
        d = DSem(self.es.enter_context(self.cx.nc.semaphore("%s_%d" % (name, self.cx.uid))))
        self.dsems.append(d)
        self.cx.dsems.append(d)
        return d


class Ctx:
    def __init__(self, nc):
        self.nc = nc
        self.eng = {"pe": nc.tensor, "act": nc.scalar, "dve": nc.vector, "pool": nc.gpsimd, "sp": nc.sync}
        self.uid = 0
        self.gs = contextlib.ExitStack()
        self.sem = {}
        self.cnt = {}
        self.waited = {}
        for e in ("pe", "act", "dve", "pool", "sp"):
            self.sem[e] = self.gs.enter_context(nc.semaphore("tk_" + e))
            self.cnt[e] = 0
        self.dsems = []
        self.serial = False

    def scope(self):
        return Scope(self)

    def wait(self, cons, tick):
        if tick is None:
            return
        if tick[0] == "dma":
            d, v = tick[1], tick[2]
            k = (cons, "dma%d" % d.uid)
            if self.waited.get(k, 0) >= v:
                return
            self.eng[cons].wait_ge(d.sem, v)
            self.waited[k] = v
        else:
            prod, v = tick
            if v <= 0:
                return
            k = (cons, prod)
            if self.waited.get(k, 0) >= v:
                return
            self.eng[cons].wait_ge(self.sem[prod], v)
            self.waited[k] = v

    def _serial_waits(self, cons):
        for e in ("pe", "act", "dve", "pool"):
            if e != cons:
                self.wait(cons, (e, self.cnt[e]))
        for d in self.dsems:
            if d.cnt > 0:
                self.wait(cons, ("dma", d, d.cnt))

    def op(self, e, fn, deps=()):
        for d in deps:
            self.wait(e, d)
        if self.serial:
            self._serial_waits(e)
        c = self.cnt[e]
        self.wait(e, (e, c))
        inst = fn(self.eng[e])
        inst.then_inc(self.sem[e], 1)
        self.cnt[e] = c + 1
        return (e, c + 1)

    def pe(self, fn, deps=(), tick=False):
        for d in deps:
            self.wait("pe", d)
        if self.serial:
            self._serial_waits("pe")
            tick = True
        inst = fn(self.eng["pe"])
        if tick:
            inst.then_inc(self.sem["pe"], 1)
            self.cnt["pe"] += 1
            return ("pe", self.cnt["pe"])
        return None

    def dma(self, dsem, out, in_, deps=(), q="sp"):
        for d in deps:
            self.wait(q, d)
        if self.serial:
            self._serial_waits(q)
        self.eng[q].dma_start(out=out, in_=in_).then_inc(dsem.sem, 16)
        dsem.cnt += 16
        return ("dma", dsem, dsem.cnt)

    def barrier(self):
        for e in ("pe", "act", "dve", "pool"):
            self.wait("sp", (e, self.cnt[e]))
        for d in self.dsems:
            if d.cnt > 0:
                self.wait("sp", ("dma", d, d.cnt))
        for d in self.dsems:
            if d.cnt > 0:
                self.eng["sp"].sem_clear(d.sem)
                d.cnt = 0
                for k in [k for k in self.waited if k[1] == "dma%d" % d.uid]:
                    del self.waited[k]
        self.eng["sp"].sem_inc(self.sem["sp"], 1)
        self.cnt["sp"] += 1
        for e in ("pe", "act", "dve", "pool"):
            self.wait(e, ("sp", self.cnt["sp"]))

    def tt(self, e, out, a, b, op, deps=()):
        return self.op(e, lambda E: E.tensor_tensor(out=out, in0=a, in1=b, op=op), deps)

    def ts(self, e, out, a, s1, s2, op0, op1=None, deps=()):
        if op1 is None:
            return self.op(e, lambda E: E.tensor_scalar(out=out, in0=a, scalar1=s1, scalar2=None, op0=op0), deps)
        return self.op(e, lambda E: E.tensor_scalar(out=out, in0=a, scalar1=s1, scalar2=s2, op0=op0, op1=op1), deps)

    def stt(self, e, out, in0, scalar, in1, op0, op1, deps=()):
        return self.op(e, lambda E: E.scalar_tensor_tensor(out=out, in0=in0, scalar=scalar, in1=in1, op0=op0, op1=op1), deps)

    def cp(self, e, out, in_, deps=()):
        if e == "act":
            return self.op(e, lambda E: E.copy(out=out, in_=in_), deps)
        return self.op(e, lambda E: E.tensor_copy(out=out, in_=in_), deps)

    def act(self, out, in_, func, bias=None, scale=1.0, accum=None, deps=(), saturate=None):
        kw = {}
        if saturate is not None:
            kw["saturate"] = saturate
        if bias is not None:
            kw["bias"] = bias
        if accum is not None:
            kw["accum_out"] = accum
        return self.op("act", lambda E: E.activation(out=out, in_=in_, func=func, scale=scale, **kw), deps)

    def memset(self, e, ap, v, deps=()):
        return self.op(e, lambda E: E.memset(ap, v), deps)


def tmax(*ticks):
    return [t for t in ticks if t is not None]


def load_consts(cx, G, T):
    cx.serial = True
    G.consts = G.sb("consts", [128, 266], F32)
    G.cmaskb = G.sb("cmaskb", [128, 8, 128], BF16)
    G.mask8b = G.sb("mask8b", [128, 8], BF16)
    G.identb = G.sb("identb", [128, 128], BF16)
    G.ones = G.sb("ones", [128, 128], F32)
    G.pcol = G.sb("pcol", [128, NPC], F32)
    G.h1 = G.sb("h1", [128, 128], F32)
    G.h2 = G.sb("h2", [128, 128], F32)
    G.eps = G.sb("epsc", [128, 1], F32)
    G.a8 = G.sb("a8g", [128, 2, 128], F32)
    G.ident = G.consts[:, 0:128]
    G.bdmask = G.consts[:, 128:256]
    G.mask8 = G.consts[:, 256:264]
    G.sgn = G.consts[:, 264:265]
    G.flag = G.consts[:, 265:266]
    with cx.scope() as sc:
        ld = sc.dsem("ld")
        cmf = sc.sb("cmf", [128, 8, 128], F32)
        pv = sc.sb("pv", [128, 5, 128], F32)
        ps = sc.ps("ps", [128, 2, 512], F32)
        cx.dma(ld, G.consts[:], T["consts"])
        cx.dma(ld, cmf[:], T["cmask"])
        cx.dma(ld, pv[:], T["pvec"].rearrange("(a p) m -> p a m", p=128))
        cx.cp("dve", G.cmaskb[:], cmf[:])
        cx.cp("dve", G.mask8b[:], G.mask8)
        cx.cp("dve", G.identb[:], G.ident)
        cx.memset("dve", G.ones[:], 1.0)
        cx.memset("dve", G.eps[:], EPS)
        cx.memset("dve", G.h1[:], 0.0)
        cx.memset("dve", G.h2[:], 0.0)
        for a in range(5):
            bank, off = (0, a * 128) if a < 4 else (1, 0)
            cx.pe(lambda E, a=a, bank=bank, off=off: E.transpose(ps[:, bank, off:off + 128], pv[:, a, :], G.ident))
        cx.cp("dve", G.pcol[:, 0:512], ps[:, 0, :])
        cx.cp("dve", G.pcol[:, 512:640], ps[:, 1, 0:128])
        cx.barrier()
    cx.serial = False


def ssm_setup(cx, G, T):
    cx.serial = True
    with cx.scope() as sc:
        ld = sc.dsem("ld")
        st = sc.dsem("st")
        sin = sc.sb("ssmin", [128, 384 + 4 * 2048], F32)
        cx.dma(ld, sin[:], T["ssmin"])
        lr2 = sin[:, 0:128]
        li2 = sin[:, 128:256]
        lst = sin[:, 256:384]
        Bs = sin[:, 384:384 + 2048]
        Bx = sin[:, 384 + 2048:384 + 4096]
        Cs = sin[:, 384 + 4096:384 + 6144]
        Cx = sin[:, 384 + 6144:384 + 8192]
        tabs = sc.sb("tabs", [128, 9, 2, 128], F32)
        w = [sc.sb("w%d" % i, [128, 128], F32) for i in range(8)]
        wi = sc.sb("wi", [128, 128], I32)
        step, lrs, lis, mag, kf, kf2, r, sc_ = w
        cx.act(step[:], lst, AF.Exp)
        cx.tt("dve", lrs[:], lr2, step[:], ALU.mult)
        cx.tt("dve", lis[:], li2, step[:], ALU.mult)
        cx.memset("dve", tabs[:, 0, 0, :], 1.0)
        cx.memset("dve", tabs[:, 0, 1, :], 0.0)
        for k in range(1, 9):
            cx.act(mag[:], lrs[:], AF.Exp, scale=float(k))
            for which, shift in ((1, 0.0), (0, 0.25)):
                cx.ts("dve", kf[:], lis[:], k / TWO_PI, shift, ALU.mult, ALU.add)
                cx.cp("dve", wi[:], kf[:])
                cx.cp("dve", kf2[:], wi[:])
                cx.tt("dve", r[:], kf[:], kf2[:], ALU.subtract)
                cx.ts("dve", r[:], r[:], TWO_PI, PI_SAFE, ALU.mult, ALU.min)
                cx.ts("dve", r[:], r[:], -PI_SAFE, None, ALU.max)
                cx.act(sc_[:], r[:], AF.Sin)
                cx.tt("dve", tabs[:, k, which, :], mag[:], sc_[:], ALU.mult)
        den, nr, t1, t2, cr, ci, cisg, tmp = w
        cx.tt("dve", den[:], lr2, lr2, ALU.mult)
        cx.tt("dve", t1[:], li2, li2, ALU.mult)
        cx.tt("dve", den[:], den[:], t1[:], ALU.add)
        cx.op("dve", lambda E: E.reciprocal(out=den[:], in_=den[:]))
        cx.ts("dve", nr[:], tabs[:, 1, 0, :], -1.0, None, ALU.add)
        ni = tabs[:, 1, 1, :]
        cx.tt("dve", t1[:], nr[:], lr2, ALU.mult)
        cx.tt("dve", t2[:], ni, li2, ALU.mult)
        cx.tt("dve", t1[:], t1[:], t2[:], ALU.add)
        cx.tt("dve", cr[:], t1[:], den[:], ALU.mult)
        cx.tt("dve", t1[:], ni, lr2, ALU.mult)
        cx.tt("dve", t2[:], nr[:], li2, ALU.mult)
        cx.tt("dve", t1[:], t1[:], t2[:], ALU.subtract)
        cx.tt("dve", ci[:], t1[:], den[:], ALU.mult)
        cx.ts("dve", cisg[:], ci[:], G.sgn, None, ALU.mult)

        def bc(tab):
            return tab.unsqueeze(2).broadcast_to([128, 128, 16])

        def v3(ap):
            return ap.rearrange("p (g h) -> p g h", h=16)

        big = [sc.sb("big%d" % i, [128, 2048], F32) for i in range(5)]
        bbs, bbx, ta, tb, sk = big
        cx.tt("dve", v3(ta[:]), v3(Bs), bc(cr[:]), ALU.mult)
        cx.tt("dve", v3(tb[:]), v3(Bx), bc(cisg[:]), ALU.mult)
        cx.tt("dve", bbs[:], ta[:], tb[:], ALU.add)
        cx.tt("dve", v3(ta[:]), v3(Bx), bc(cr[:]), ALU.mult)
        cx.tt("dve", v3(tb[:]), v3(Bs), bc(cisg[:]), ALU.mult)
        cx.tt("dve", bbx[:], ta[:], tb[:], ALU.subtract)
        ps = sc.ps("ps", [128, 4, 512], F32)
        stg = [sc.sb("stg%d" % i, [128, 16, 128], BF16) for i in range(2)]
        stg_t = [None, None]
        nstg = [0]
        sa, sb_ = w[0], w[1]

        def emit_T(src, dst_dram, kidx):
            s = stg[nstg[0] % 2]
            nstg[0] += 1
            for q in range(4):
                for j in range(4):
                    cc = q * 4 + j
                    cx.pe(lambda E, q=q, j=j, cc=cc: E.transpose(ps[:, q, j * 128:(j + 1) * 128], src[:, cc * 128:(cc + 1) * 128], G.ident))
                cx.cp("act", s[:, q * 4:(q + 1) * 4, :], ps[:, q, :].rearrange("p (a m) -> p a m", m=128))
            cx.dma(st, dst_dram[:, :, kidx, :].rearrange("c p m -> p c m"), s[:])

        def emit_plain(src, dst_dram, kidx):
            s = stg[nstg[0] % 2]
            nstg[0] += 1
            cx.cp("act", s[:], src.rearrange("p (c m) -> p c m", m=128))
            cx.dma(st, dst_dram[:, :, kidx, :].rearrange("c p m -> p c m"), s[:])

        for k in range(8):
            ARk = tabs[:, k, 0, :]
            AIk = tabs[:, k, 1, :]
            cx.ts("dve", sa[:], AIk, G.sgn, None, ALU.mult)
            cx.tt("dve", v3(ta[:]), v3(bbs[:]), bc(ARk), ALU.mult)
            cx.tt("dve", v3(tb[:]), v3(bbx[:]), bc(sa[:]), ALU.mult)
            cx.tt("dve", sk[:], ta[:], tb[:], ALU.add)
            emit_T(sk, T["SkT"], k)
            cx.ts("dve", sa[:], ARk, G.sgn, None, ALU.mult)
            cx.tt("dve", v3(ta[:]), v3(bbx[:]), bc(sa[:]), ALU.mult)
            cx.tt("dve", v3(tb[:]), v3(bbs[:]), bc(AIk), ALU.mult)
            cx.tt("dve", sk[:], ta[:], tb[:], ALU.subtract)
            emit_T(sk, T["SJkT"], k)
        for tau in range(9):
            ARk = tabs[:, tau, 0, :]
            AIk = tabs[:, tau, 1, :]
            cx.ts("dve", sa[:], ARk, G.sgn, -1.0, ALU.mult, ALU.mult)
            cx.tt("dve", v3(ta[:]), v3(Cs), bc(sa[:]), ALU.mult)
            cx.tt("dve", v3(tb[:]), v3(Cx), bc(AIk), ALU.mult)
            cx.tt("dve", sk[:], ta[:], tb[:], ALU.subtract)
            if tau >= 1:
                emit_plain(sk[:], T["Rk"], tau - 1)
            if tau <= 7:
                s = stg[nstg[0] % 2]
                nstg[0] += 1
                for q in range(4):
                    for j in range(4):
                        cc = q * 4 + j
                        cx.pe(lambda E, q=q, j=j, cc=cc: E.matmul(ps[:, q, j * 128:(j + 1) * 128], lhsT=bbs[:, cc * 128:(cc + 1) * 128],
                                                                  rhs=sk[:, cc * 128:(cc + 1) * 128], start=True, stop=True))
                    cx.tt("dve", s[:, q * 4:(q + 1) * 4, :], ps[:, q, :].rearrange("p (a m) -> p a m", m=128),
                          G.bdmask.unsqueeze(1).broadcast_to([128, 4, 128]), ALU.mult)
                cx.dma(st, T["Kbd"][:, :, tau, :].rearrange("c p m -> p c m"), s[:])
        cx.cp("dve", G.a8[:], tabs[:, 8, :, :])
        cx.barrier()
    cx.serial = False


def load_norm(cx, G, AT, src, tiles, gcol, xT_out=None):
    with cx.scope() as sc:
        ld = [sc.dsem("ld%d" % i) for i in range(2)]
        st = [sc.dsem("st%d" % i) for i in range(2)]
        xtok = [sc.sb("xtok%d" % i, [128, D], F32) for i in range(2)]
        xT = [sc.sb("xTt%d" % i, [128, 32, 128], F32) for i in range(2)]
        junk = sc.sb("junk", [128, D], mybir.dt.float8e5)
        ssq = sc.sb("ssq", [128, 2], F32)
        rstd = sc.sb("rstd", [128, 2], F32)
        diag = [sc.sb("diag%d" % i, [128, 128], F32) for i in range(2)]
        rbs = [sc.sb("rbs%d" % i, [128, 128], F32) for i in range(2)]
        ps = sc.ps("ps", [128, 8, 512], F32)
        NB = 6
        bank_rel = [None] * NB
        slot_rel = [None, None]
        xT_rel = [[], []]
        rb_rel = [[], []]
        nj = 0
        for ti, (r0, nr, c0) in enumerate(tiles):
            sl = ti % 2
            t_ld = cx.dma(ld[sl], xtok[sl][0:nr, :], src[r0:r0 + nr, :], deps=[slot_rel[sl]] if slot_rel[sl] else [])
            cx.act(junk[0:nr, :], xtok[sl][0:nr, :], AF.Square, accum=ssq[0:nr, sl:sl + 1], deps=[t_ld] + rb_rel[sl], saturate=False)
            t_sq = cx.act(rstd[0:nr, sl:sl + 1], ssq[0:nr, sl:sl + 1], AF.Sqrt, bias=G.eps[0:nr, :], scale=1.0 / D)
            cx.op("dve", lambda E: E.reciprocal(out=rstd[0:nr, sl:sl + 1], in_=rstd[0:nr, sl:sl + 1]), deps=[t_sq])
            t_d = cx.ts("dve", diag[sl][0:nr, 0:nr], G.ident[0:nr, 0:nr], rstd[0:nr, sl:sl + 1], None, ALU.mult)
            ev_ticks = []
            tk = None
            for grp in range(8):
                b = nj % NB
                nj += 1
                deps = [t_ld]
                if bank_rel[b]:
                    deps.append(bank_rel[b])
                for j in range(4):
                    kc = grp * 4 + j
                    tk = cx.pe(lambda E, b=b, j=j, kc=kc: E.transpose(ps[:, b, j * 128:j * 128 + nr], xtok[sl][0:nr, kc * 128:(kc + 1) * 128],
                                                                      G.ident[0:nr, 0:nr]), deps=deps if j == 0 else (), tick=(j == 3))
                d2 = [tk]
                if grp == 0:
                    d2 += xT_rel[sl]
                t_ev = cx.cp("act", xT[sl][:, grp * 4:(grp + 1) * 4, 0:nr], ps[:, b, :].rearrange("p (a m) -> p a m", m=128)[:, :, 0:nr], deps=d2)
                bank_rel[b] = t_ev
                ev_ticks.append(t_ev)
            slot_rel[sl] = tk
            t_rb = cx.pe(lambda E: E.matmul(ps[:, 6 + sl, 0:nr], lhsT=G.ones[0:nr, :], rhs=diag[sl][0:nr, 0:nr], start=True, stop=True),
                         deps=[t_d], tick=True)
            t_rbs = cx.cp("act", rbs[sl][:, 0:nr], ps[:, 6 + sl, 0:nr], deps=[t_rb])
            st_tick = None
            if xT_out is not None:
                st_tick = cx.dma(st[sl], xT_out.rearrange("(kc p) n -> p kc n", p=128)[:, :, c0:c0 + nr], xT[sl][:, :, 0:nr], deps=[ev_ticks[-1]])
            d3 = [ev_ticks[-1], t_rbs] + ([st_tick] if st_tick else [])
            cx.tt("dve", xT[sl][:, :, 0:nr], xT[sl][:, :, 0:nr], rbs[sl][:, 0:nr].unsqueeze(1).broadcast_to([128, 32, nr]), ALU.mult, deps=d3)
            last = cx.tt("dve", AT[:, :, c0:c0 + nr], xT[sl][:, :, 0:nr], G.pcol[:, gcol:gcol + 32].unsqueeze(2).broadcast_to([128, 32, nr]), ALU.mult)
            rb_rel[sl] = [last]
            xT_rel[sl] = [last]
        cx.barrier()


class Gemm:
    def __init__(self, cx, sc, KC, nwst=4, nwbf=2, piece=4, cast_eng=("pool",)):
        self.cx = cx
        self.cast_eng = cast_eng
        self.ncast = 0
        self.KC = KC
        self.piece = piece
        self.npieces = (KC + piece - 1) // piece
        self.wst = [sc.sb("wst%d" % i, [128, piece, 128], F32) for i in range(nwst)]
        self.wld = [sc.dsem("wld%d" % i) for i in range(nwst)]
        self.wst_rel = [None] * nwst
        self.wbf = [sc.sb("wbf%d" % i, [128, KC, 128], BF16) for i in range(nwbf)]
        self.wbf_rel = [None] * nwbf
        self.nst = 0
        self.nbf = 0
        self.loaded = {}

    def load(self, W, k0, col0, key, ncols=128):
        cx = self.cx
        bs = self.nbf % len(self.wbf)
        self.nbf += 1
        wb = self.wbf[bs]
        last = None
        lasts = []
        for p in range(self.npieces):
            kc0 = p * self.piece
            n = min(self.piece, self.KC - kc0)
            ss = self.nst % len(self.wst)
            self.nst += 1
            src = W[(k0 + kc0) * 128:(k0 + kc0 + n) * 128, col0:col0 + ncols].rearrange("(kc p) m -> p kc m", p=128)
            t_ld = cx.dma(self.wld[ss], self.wst[ss][:, 0:n, 0:ncols], src, deps=[self.wst_rel[ss]] if self.wst_rel[ss] else [])
            deps = [t_ld]
            if self.wbf_rel[bs]:
                deps.append(self.wbf_rel[bs])
            ce = self.cast_eng[self.ncast % len(self.cast_eng)]
            self.ncast += 1
            last = cx.cp(ce, wb[:, kc0:kc0 + n, 0:ncols], self.wst[ss][:, 0:n, 0:ncols], deps=deps)
            self.wst_rel[ss] = last
            lasts.append(last)
        self.loaded[key] = (wb, lasts[-len(set(self.cast_eng)) * 2:], bs)
        return wb, lasts

    def release(self, key, tick):
        wb, last, bs = self.loaded.pop(key)
        self.wbf_rel[bs] = tick


def run_gemm(cx, sc, ps, banks, AT, KC, W, k0, coltiles, blocks, epilogue, prefetch=1, pre=None, post=None, cast_eng=("pool",)):
    g = Gemm(cx, sc, KC, cast_eng=cast_eng)
    nM = len(coltiles)
    for i in range(min(prefetch, nM)):
        g.load(W, k0, coltiles[i], i)
    bank_rel = {b: None for b in banks}
    nj = 0
    for mi in range(nM):
        if mi + prefetch < nM:
            g.load(W, k0, coltiles[mi + prefetch], mi + prefetch)
        wb, wt = g.loaded[mi][0], g.loaded[mi][1]
        if pre:
            pre(mi)
        tk = None
        for bi, (c0, n) in enumerate(blocks):
            b = banks[nj % len(banks)]
            nj += 1
            deps = list(wt)
            if bank_rel[b]:
                deps.append(bank_rel[b])
            for kc in range(KC):
                tk = cx.pe(lambda E, b=b, kc=kc: E.matmul(ps[:, b, 0:n], lhsT=wb[:, kc, :], rhs=AT[:, kc, c0:c0 + n],
                                                         start=(kc == 0), stop=(kc == KC - 1)),
                           deps=deps if kc == 0 else (), tick=(kc == KC - 1))
            bank_rel[b] = epilogue(mi, bi, ps[:, b, 0:n], n, c0, tk)
        g.release(mi, tk)
        if post:
            post(mi)


def ssq_finalize(cx, sc, G, ps, bank, acc, blocks, dnorm, rstd_row, st, deps, work=None):
    rs = work if work is not None else sc.sb("rsfin", [128, NT], F32)
    last = None
    rel = None
    for (c0, n) in blocks:
        d = list(deps)
        if rel:
            d.append(rel)
        tk = cx.pe(lambda E: E.matmul(ps[:, bank, 0:n], lhsT=G.ones[:], rhs=acc[:, c0:c0 + n], start=True, stop=True), deps=d, tick=True)
        t1 = cx.act(rs[:, c0:c0 + n], ps[:, bank, 0:n], AF.Sqrt, bias=G.eps[:], scale=1.0 / dnorm, deps=[tk])
        rel = t1
        last = cx.op("dve", lambda E: E.reciprocal(out=rs[:, c0:c0 + n], in_=rs[:, c0:c0 + n]), deps=[t1])
    return cx.dma(st, rstd_row, rs[0:1, :], deps=[last])


def norm_load(cx, G, AT, src, KC, N, gcol, rstd_rows, rows_for_kc):
    with cx.scope() as sc:
        ld = [sc.dsem("nl%d" % i) for i in range(3)]
        rl = sc.dsem("rl")
        xin = [sc.sb("xin%d" % i, [128, N], F32) for i in range(3)]
        rel = [None] * 3
        rb = {}
        for r in sorted(set(rows_for_kc)):
            rb[r] = sc.sb("rb%d" % r, [128, N], F32)
            t_rb = cx.dma(rl, rb[r][:], rstd_rows[r:r + 1, :].broadcast_to([128, N]))
        for kc in range(KC):
            s = kc % 3
            t_ld = cx.dma(ld[s], xin[s][:], src[kc * 128:(kc + 1) * 128, :], deps=[rel[s]] if rel[s] else [])
            rel[s] = cx.stt("dve", AT[:, kc, 0:N], xin[s][:], G.pcol[:, gcol + kc:gcol + kc + 1], rb[rows_for_kc[kc]][:],
                            ALU.mult, ALU.mult, deps=[t_ld, t_rb])
        cx.barrier()


def plain_load(cx, AT, src, KC, N):
    with cx.scope() as sc:
        ld = sc.dsem("pl")
        for kc0 in range(0, KC, 8):
            n = min(8, KC - kc0)
            cx.dma(ld, AT[:, kc0:kc0 + n, 0:N], src[kc0 * 128:(kc0 + n) * 128, :].rearrange("(kc p) n -> p kc n", p=128))
        cx.barrier()


def ssm_main(cx, G, sc0, uT, T, N, own):
    NCH = N // 8
    if own:
        subs = [(0, 1)] + [(1 + 32 * i, 32) for i in range(8)]
    else:
        subs = [(32 * i, 32) for i in range(8)]
    with cx.scope() as sc:
        Hst = sc.sb("Hst", [128, 128, NCH + 1], BF16) if own else None
        t1 = sc.sb("sct1", [128, 128], F32)
        t2 = sc.sb("sct2", [128, 128], F32)
        t3 = sc.sb("sct3", [128, 128], F32)
        t4 = sc.sb("sct4", [128, 128], F32)
        sinj = cx.scope()
        sinj.__enter__()
        skl = [sinj.dsem("skl%d" % i) for i in range(2)]
        skt = [sinj.sb("skt%d" % i, [128, 2, 8, 128], BF16) for i in range(2)]
        skt_rel = [None, None]
        um = [sinj.sb("um%d" % i, [128, 8, 256], BF16) for i in range(2)]
        um_rel = [None, None]
        bst = [sinj.sb("bst%d" % i, [128, 2, 128, 32], BF16) for i in range(2)]
        bst_rel = [None, None]
        a8 = G.a8
        t_a8 = None
        ps = sc.ps("ps", [128, 8, 512], F32)
        banks = list(range(8))
        bank_rel = [None] * 8
        nj = 0
        nl = 0
        AR8 = a8[:, 0, :]
        AI8 = a8[:, 1, :]
        t_state = None
        if own:
            t_state = cx.cp("act", Hst[:, :, 0], G.h1[:])
        for si, (cb, ncb) in enumerate(subs):
            bs = si % 2
            n = ncb * 8
            c0 = cb * 8
            ev_last = None
            for cc in range(16):
                s = nl % 2
                nl += 1
                d = [skt_rel[s]] if skt_rel[s] else []
                t_l1 = cx.dma(skl[s], skt[s][:, 0, :, :], T["SkT"][cc], deps=d)
                t_l2 = cx.dma(skl[s], skt[s][:, 1, :, :], T["SJkT"][cc])
                d = [um_rel[s]] if um_rel[s] else []
                t_um = cx.tt("pool", um[s][:, :, 0:n], uT[:, cc, c0:c0 + n].unsqueeze(1).broadcast_to([128, 8, n]),
                             G.mask8b[:, :].unsqueeze(2).broadcast_to([128, 8, n]), ALU.mult, deps=d)
                for dual in range(2):
                    b = banks[nj % 8]
                    nj += 1
                    deps = [t_l2, t_um]
                    if bank_rel[b]:
                        deps.append(bank_rel[b])
                    tk = None
                    for j in range(8):
                        tk = cx.pe(lambda E, b=b, j=j, s=s, dual=dual: E.matmul(
                            ps[:, b, 0:8 * ncb].rearrange("p (g c) -> p g c", c=ncb),
                            lhsT=skt[s][:, dual, 7 - j, :],
                            rhs=um[s][:, :, 0:n].rearrange("p g (c j) -> p g c j", j=8)[:, :, :, j],
                            start=(j == 0), stop=(j == 7)), deps=deps if j == 0 else (), tick=(j == 7))
                    d2 = [tk]
                    if cc == 0 and dual == 0 and bst_rel[bs]:
                        d2.append(bst_rel[bs])
                    ev = cx.cp("act", bst[bs][:, dual, cc * 8:(cc + 1) * 8, 0:ncb],
                               ps[:, b, 0:8 * ncb].rearrange("p (g c) -> p g c", c=ncb), deps=d2)
                    bank_rel[b] = ev
                    ev_last = ev
                skt_rel[s] = tk
                um_rel[s] = tk
            for c in range(ncb):
                b1 = bst[bs][:, 0, :, c]
                b2 = bst[bs][:, 1, :, c]
                d = [ev_last, t_state] if c == 0 else []
                cx.tt("dve", t1[:], AR8, G.h1[:], ALU.mult, deps=d)
                cx.tt("dve", t2[:], AI8, G.h2[:], ALU.mult)
                cx.tt("dve", t3[:], AR8, G.h2[:], ALU.mult)
                cx.tt("dve", t4[:], AI8, G.h1[:], ALU.mult)
                cx.tt("dve", t1[:], t1[:], t2[:], ALU.add)
                cx.tt("dve", t3[:], t3[:], t4[:], ALU.subtract)
                if own:
                    t_state = cx.tt("dve", Hst[:, :, cb + c + 1], t1[:], b1, ALU.add)
                t_h1 = cx.tt("dve", G.h1[:], t1[:], b1, ALU.add)
                t_h2 = cx.tt("dve", G.h2[:], t3[:], b2, ALU.add)
            bst_rel[bs] = t_h2
        cx.barrier()
        sinj.__exit__(None, None, None)
        if not own:
            return
        kl = [sc.dsem("kl%d" % i) for i in range(2)]
        kb = [sc.sb("kb%d" % i, [128, 2, 8, 128], BF16) for i in range(2)]
        kb_rel = [None, None]
        rp = [sc.sb("rp%d" % i, [128, 8, 8, 128], BF16) for i in range(2)]
        rp_rel = [None, None]
        yt = sc.sb("yt", [128, 512], F32)
        y2 = sc.sb("y2", [128, 512], F32)
        sg = sc.sb("sg", [128, 512], F32)
        blocks = BLK_OWN
        bank_rel = [None] * 8
        nj = 0
        for cc in range(16):
            s = cc % 2
            d = [kb_rel[s]] if kb_rel[s] else []
            cx.dma(kl[s], kb[s][:, 0, :, :], T["Kbd"][cc], deps=d)
            t_kl = cx.dma(kl[s], kb[s][:, 1, :, :], T["Rk"][cc])
            t_rp = None
            for j in range(8):
                d = [t_kl]
                if j == 0 and rp_rel[s]:
                    d.append(rp_rel[s])
                t_rp = cx.tt("pool", rp[s][:, j, :, :], kb[s][:, 1, j, :].unsqueeze(1).broadcast_to([128, 8, 128]), G.cmaskb[:], ALU.mult, deps=d)
            bb = []
            for bi in range(len(blocks)):
                b = banks[nj % 8]
                nj += 1
                bb.append(b)
            first_deps = [t_kl, t_rp, t_state] + [bank_rel[b] for b in bb if bank_rel[b]]
            first = True
            for tau in range(8):
                for bi, (c0, n) in enumerate(blocks):
                    ncb = n // 8
                    b = bb[bi]
                    cx.pe(lambda E, b=b, tau=tau, c0=c0, n=n: E.matmul(
                        ps[:, b, 0:n].rearrange("p (c j) -> p c j", j=8)[:, :, tau:8],
                        lhsT=kb[s][:, 0, tau, :],
                        rhs=uT[:, cc, c0:c0 + n].rearrange("p (c j) -> p c j", j=8)[:, :, 0:8 - tau],
                        start=(tau == 0), stop=False), deps=first_deps if first else ())
                    first = False
            tk = None
            for gl in range(8):
                g = cc * 8 + gl
                for j in range(8):
                    for bi, (c0, n) in enumerate(blocks):
                        ncb = n // 8
                        cb = c0 // 8
                        b = bb[bi]
                        lastmm = (gl == 7 and j == 7)
                        tk = cx.pe(lambda E, b=b, j=j, gl=gl, g=g, cb=cb, ncb=ncb, n=n, lastmm=lastmm: E.matmul(
                            ps[:, b, 0:n].rearrange("p (c j) -> p c j", j=8)[:, :, j],
                            lhsT=rp[s][:, j, gl, :],
                            rhs=Hst[:, g, cb:cb + ncb],
                            start=False, stop=lastmm), tick=(lastmm and bi == len(blocks) - 1))
            kb_rel[s] = tk
            rp_rel[s] = tk
            dcol = G.pcol[:, PC["ssm_d"] + cc:PC["ssm_d"] + cc + 1]
            for bi, (c0, n) in enumerate(blocks):
                b = bb[bi]
                cx.stt("dve", yt[:, 0:n], uT[:, cc, c0:c0 + n], dcol, ps[:, b, 0:n], ALU.mult, ALU.add, deps=[tk])
                cx.tt("dve", y2[:, 0:n], yt[:, 0:n], yt[:, 0:n], ALU.mult)
                cx.ts("dve", y2[:, 0:n], y2[:, 0:n], 0.044715 * 1.5957691216057308, 1.5957691216057308, ALU.mult, ALU.add)
                t_w = cx.tt("dve", y2[:, 0:n], y2[:, 0:n], yt[:, 0:n], ALU.mult)
                t_sg = cx.act(sg[:, 0:n], y2[:, 0:n], AF.Sigmoid, deps=[t_w])
                t_z = cx.tt("dve", uT[:, cc, c0:c0 + n], yt[:, 0:n], sg[:, 0:n], ALU.mult, deps=[t_sg])
                bank_rel[b] = t_z
        cx.barrier()


def build_program(stop_after=None, force_dbg=False):
    nc = bass.Bass("TRN2", target_bir_lowering=False)
    T = {}

    def din(n, shp, dt=F32):
        T[n] = nc.dram_tensor(n, list(shp), dt, kind="ExternalInput").ap()

    dbg = (stop_after is not None) or force_dbg

    def dscr(n, shp, dt=F32):
        T[n] = nc.dram_tensor(n, list(shp), dt, kind=("ExternalOutput" if dbg else "Internal")).ap()

    din("xown", [NT, D])
    din("xprev", [NPREV, D])
    din("memt", [NMEM, D])
    din("consts", [128, 266])
    din("cmask", [128, 8, 128])
    din("pvec", [NPC, 128])
    din("ssmin", [128, 384 + 8192])
    order = ["setup", "prev", "win", "ssm", "glu", "wout", "attn", "down0", None]
    lvl = order.index(stop_after)
    if lvl >= 1:
        din("w_in", [D, 8192])
    if lvl >= 4:
        din("glu_w", [2048, 2048])
    if lvl >= 5:
        din("w_out", [D, D])
    if lvl >= 6:
        din("wq", [D, D])
        din("wk", [D, D])
        din("wv", [D, D])
        din("wo", [D, D])
    if lvl >= 7:
        din("w_up", [D, 2 * DFF])
        din("w_down", [DFF, D])
    T["out"] = nc.dram_tensor("out", [NTOK, D], F32, kind="ExternalOutput").ap()
    dscr("SkT", [16, 128, 8, 128], BF16)
    dscr("SJkT", [16, 128, 8, 128], BF16)
    dscr("Rk", [16, 128, 8, 128], BF16)
    dscr("Kbd", [16, 128, 8, 128], BF16)
    dscr("xT", [D, NT], F32)
    dscr("ymix", [D, NT], F32)
    dscr("uTp", [2048, NPREV], BF16)
    dscr("uTo", [2048, NT], BF16)
    dscr("qT", [D, NT], BF16)
    dscr("oT", [D, NT], BF16)
    dscr("actT", [DFF, NT], BF16)
    dscr("rstd", [4, NT], F32)
    if dbg:
        dscr("dbg_h", [128, 128], F32)
        dscr("dbg_z", [2048, NT], BF16)

    cx = Ctx(nc)
    with cx.gs:
        G = cx.scope()
        with G:
            load_consts(cx, G, T)
            ssm_setup(cx, G, T)
            if stop_after == "setup":
                return nc
            mixer(cx, G, T, stop_after)
            if stop_after in ("prev", "win", "ssm", "glu"):
                return nc
            rest(cx, G, T, stop_after)
    return nc


def gemm_u(cx, G, AT, T, dst, blocks, N):
    with cx.scope() as sc:
        ps = sc.ps("ps", [128, 8, 512], F32)
        st = [sc.dsem("st%d" % i) for i in range(2)]
        ust = [sc.sb("ust%d" % i, [128, N], BF16) for i in range(2)]
        rel = [None, None]
        state = {}

        def epi(mi, bi, bank, n, c0, tk):
            s = mi % 2
            deps = [tk]
            if bi == 0 and rel[s]:
                deps.append(rel[s])
            t = cx.cp("act", ust[s][:, c0:c0 + n], bank, deps=deps)
            state["last"] = t
            return t

        def post(mi):
            s = mi % 2
            rel[s] = cx.dma(st[s], dst[mi * 128:(mi + 1) * 128, :], ust[s][:, 0:N], deps=[state["last"]])

        run_gemm(cx, sc, ps, list(range(8)), AT, 32, T["w_in"], 0, [128 * m for m in range(16)], blocks, epi, post=post)
        cx.barrier()


def gemm_conv(cx, G, AT, T):
    with cx.scope() as sc:
        ps = sc.ps("ps", [128, 8, 512], F32)
        st = sc.dsem("st")
        st2 = sc.dsem("st2")
        cv = sc.sb("cv", [128, 2 + NT], F32)
        gb = sc.sb("gb", [128, NT], F32)
        yc = sc.sb("yc", [128, NT], F32)
        acc = sc.sb("acc", [128, NT], F32)
        t0 = cx.memset("dve", cv[:, 0:2], 0.0)
        cx.memset("dve", acc[:], 0.0)
        state = {"cv_rel": None, "gb_rel": None, "last": {}}
        coltiles = []
        for i in range(16):
            coltiles += [4096 + 128 * i, 6144 + 128 * i, 2048 + 128 * i]

        def epi(mi, bi, bank, n, c0, tk):
            i, which = mi // 3, mi % 3
            if which == 0:
                deps = [tk]
                if bi == 0 and state["cv_rel"]:
                    deps.append(state["cv_rel"])
                t = cx.cp("act", cv[:, 2 + c0:2 + c0 + n], bank, deps=deps)
                state["last"][(0, bi)] = t
            elif which == 1:
                t = cx.tt("dve", cv[:, 2 + c0:2 + c0 + n], bank, cv[:, 2 + c0:2 + c0 + n], ALU.mult, deps=[tk, state["last"][(0, bi)]])
                state["last"][1] = t
            else:
                deps = [tk]
                if bi == 0 and state["gb_rel"]:
                    deps += state["gb_rel"]
                t = cx.cp("act", gb[:, c0:c0 + n], bank, deps=deps)
                state["last"][2] = t
            return t

        def post(mi):
            i, which = mi // 3, mi % 3
            if which != 2:
                return
            w0 = G.pcol[:, PC["cw0"] + i:PC["cw0"] + i + 1]
            w1 = G.pcol[:, PC["cw1"] + i:PC["cw1"] + i + 1]
            w2 = G.pcol[:, PC["cw2"] + i:PC["cw2"] + i + 1]
            cx.ts("dve", yc[:], cv[:, 2:2 + NT], w2, None, ALU.mult, deps=[state["last"][1], state["last"][2]])
            cx.stt("dve", yc[:], cv[:, 1:1 + NT], w1, yc[:], ALU.mult, ALU.add)
            t_c = cx.stt("dve", yc[:], cv[:, 0:NT], w0, yc[:], ALU.mult, ALU.add)
            state["cv_rel"] = t_c
            t_y = cx.tt("dve", gb[:], gb[:], yc[:], ALU.mult)
            t_s = cx.act(yc[:], gb[:], AF.Square, deps=[t_y])
            t_a = cx.tt("pool", acc[:], acc[:], yc[:], ALU.add, deps=[t_s])
            cx.wait("dve", t_a)
            t_st = cx.dma(st, T["ymix"][2048 + 128 * i:2048 + 128 * (i + 1), :], gb[:], deps=[t_y])
            state["gb_rel"] = [t_st, t_s]
            state["acc"] = t_a

        run_gemm(cx, sc, ps, list(range(7)), AT, 32, T["w_in"], 0, coltiles, BLK_OWN, epi, post=post)
        ssq_finalize(cx, sc, G, ps, 7, acc, BLK_OWN, 2048.0, T["rstd"][1:2, :], st2, [state["acc"]], work=yc)
        cx.barrier()


def gemm_glu(cx, G, zT, T):
    with cx.scope() as sc:
        ps = sc.ps("ps", [128, 8, 512], F32)
        st = [sc.dsem("st%d" % i) for i in range(2)]
        st2 = sc.dsem("st2")
        gt = sc.sb("gt", [128, 512], F32)
        yst = [sc.sb("yst%d" % i, [128, NT], F32) for i in range(2)]
        sq = sc.sb("sq", [128, NT], F32)
        acc = sc.sb("acc", [128, NT], F32)
        cx.memset("dve", acc[:], 0.0)
        rel = [None, None]
        state = {}

        def epi(mi, bi, bank, n, c0, tk):
            s = mi % 2
            bcol = G.pcol[:, PC["glu_b"] + mi:PC["glu_b"] + mi + 1]
            t_g = cx.act(gt[:, 0:n], bank, AF.Sigmoid, bias=bcol, deps=[tk] + ([state["y"]] if "y" in state else []))
            deps = [t_g]
            if bi == 0 and rel[s]:
                deps += rel[s]
            state["y"] = cx.tt("dve", yst[s][:, c0:c0 + n], zT[:, mi, c0:c0 + n], gt[:, 0:n], ALU.mult, deps=deps)
            return t_g

        def post(mi):
            s = mi % 2
            d = [state["y"]] + ([state["acc"]] if "acc" in state else [])
            t_s = cx.act(sq[:], yst[s][:], AF.Square, deps=d)
            state["acc"] = cx.tt("pool", acc[:], acc[:], sq[:], ALU.add, deps=[t_s])
            t_st = cx.dma(st[s], T["ymix"][128 * mi:128 * (mi + 1), :], yst[s][:], deps=[state["y"]])
            rel[s] = [t_st, t_s]

        run_gemm(cx, sc, ps, list(range(7)), zT, 16, T["glu_w"], 0, [128 * m for m in range(16)], BLK_OWN, epi, post=post)
        ssq_finalize(cx, sc, G, ps, 7, acc, BLK_OWN, 2048.0, T["rstd"][0:1, :], st2, [state["acc"]], work=sq)
        cx.barrier()


def dbg_dump(cx, T, name, ap):
    with cx.scope() as sc:
        d = sc.dsem("dbg")
        cx.serial = True
        cx.dma(d, T[name], ap)
        cx.barrier()
        cx.serial = False


def mixer(cx, G, T, stop_after):
    tiles_prev = [(128 * i, 128, 128 * i) for i in range(16)]
    tiles_own = [(0, 8, 0)] + [(8 + 128 * i, 128, 8 + 128 * i) for i in range(16)]
    with cx.scope() as s1:
        AT = s1.sb("AT", [128, 32, NT], BF16)
        load_norm(cx, G, AT, T["xprev"], tiles_prev, PC["g_mix"])
        gemm_u(cx, G, AT, T, T["uTp"], BLK_PREV, NPREV)
    with cx.scope() as s2:
        uT = s2.sb("uT", [128, 16, NT], BF16)
        plain_load(cx, uT, T["uTp"], 16, NPREV)
        ssm_main(cx, G, s2, uT, T, NPREV, own=False)
    if stop_after == "prev":
        dbg_dump(cx, T, "dbg_h", G.h1[:])
        return
    with cx.scope() as s3:
        AT = s3.sb("AT", [128, 32, NT], BF16)
        load_norm(cx, G, AT, T["xown"], tiles_own, PC["g_mix"], xT_out=T["xT"])
        gemm_u(cx, G, AT, T, T["uTo"], BLK_OWN, NT)
        gemm_conv(cx, G, AT, T)
    if stop_after == "win":
        return
    with cx.scope() as s4:
        uT = s4.sb("uT", [128, 16, NT], BF16)
        plain_load(cx, uT, T["uTo"], 16, NT)
        ssm_main(cx, G, s4, uT, T, NT, own=True)
        if stop_after == "ssm":
            with cx.scope() as sc:
                d = sc.dsem("dbg")
                cx.serial = True
                cx.dma(d, T["dbg_z"].rearrange("(kc p) n -> p kc n", p=128), uT[:])
                cx.barrier()
                cx.serial = False
            return
        gemm_glu(cx, G, uT, T)


def gemm_resid(cx, G, AT, KC, W, k0, T, final_ssq):
    with cx.scope() as sc:
        ps = sc.ps("ps", [128, 8, 512], F32)
        ld = [sc.dsem("xl%d" % i) for i in range(2)]
        st = [sc.dsem("xs%d" % i) for i in range(2)]
        st2 = sc.dsem("st2")
        xr = [sc.sb("xr%d" % i, [128, NT], F32) for i in range(2)]
        rel = [None, None]
        ldt = [None, None]
        state = {}
        if final_ssq:
            sq = sc.sb("sq", [128, NT], F32)
            acc = sc.sb("acc", [128, NT], F32)
            cx.memset("dve", acc[:], 0.0)

        def pre(mi):
            s = mi % 2
            ldt[s] = cx.dma(ld[s], xr[s][:], T["xT"][128 * mi:128 * (mi + 1), :], deps=rel[s] if rel[s] else [])

        def epi(mi, bi, bank, n, c0, tk):
            s = mi % 2
            t = cx.tt("dve", xr[s][:, c0:c0 + n], bank, xr[s][:, c0:c0 + n], ALU.add, deps=[tk, ldt[s]])
            state["last"] = t
            return t

        def post(mi):
            s = mi % 2
            r = []
            if final_ssq:
                d = [state["last"]] + ([state["acc"]] if "acc" in state else [])
                t_s = cx.act(sq[:], xr[s][:], AF.Square, deps=d)
                state["acc"] = cx.tt("pool", acc[:], acc[:], sq[:], ALU.add, deps=[t_s])
                r.append(t_s)
            t_st = cx.dma(st[s], T["xT"][128 * mi:128 * (mi + 1), :], xr[s][:], deps=[state["last"]])
            r.append(t_st)
            rel[s] = r

        banks = list(range(7)) if final_ssq else list(range(8))
        run_gemm(cx, sc, ps, banks, AT, KC, W, k0, [128 * m for m in range(32)], BLK_OWN, epi, pre=pre, post=post)
        if final_ssq:
            ssq_finalize(cx, sc, G, ps, 7, acc, BLK_OWN, float(D), T["rstd"][2:3, :], st2, [state["acc"]], work=sq)
        cx.barrier()


def gemm_store(cx, G, AT, W, dst):
    with cx.scope() as sc:
        ps = sc.ps("ps", [128, 8, 512], F32)
        st = [sc.dsem("st%d" % i) for i in range(2)]
        qs = [sc.sb("qs%d" % i, [128, NT], BF16) for i in range(2)]
        rel = [None, None]
        state = {}

        def epi(mi, bi, bank, n, c0, tk):
            s = mi % 2
            deps = [tk]
            if bi == 0 and rel[s]:
                deps.append(rel[s])
            t = cx.cp("act", qs[s][:, c0:c0 + n], bank, deps=deps)
            state["last"] = t
            return t

        def post(mi):
            s = mi % 2
            rel[s] = cx.dma(st[s], dst[mi * 128:(mi + 1) * 128, :], qs[s][:], deps=[state["last"]])

        run_gemm(cx, sc, ps, list(range(8)), AT, 32, W, 0, [128 * m for m in range(32)], BLK_OWN, epi, post=post)
        cx.barrier()


def gemm_up(cx, G, AT, T):
    with cx.scope() as sc:
        ps = sc.ps("ps", [128, 8, 512], F32)
        st = [sc.dsem("st%d" % i) for i in range(2)]
        a_sb = sc.sb("a_sb", [128, 2 + NT], F32)
        g_sb = sc.sb("g_sb", [128, NT], F32)
        tt_ = sc.sb("tconv", [128, NT], F32)
        ast = [sc.sb("ast%d" % i, [128, NT], BF16) for i in range(2)]
        cx.memset("dve", a_sb[:, 0:2], 0.0)
        rel = [None, None]
        state = {"a_rel": None, "g_rel": None}
        coltiles = []
        for i in range(86):
            coltiles += [128 * i, DFF + 128 * i]

        def epi(mi, bi, bank, n, c0, tk):
            i, which = mi // 2, mi % 2
            if which == 0:
                deps = [tk]
                if bi == 0 and state["a_rel"]:
                    deps.append(state["a_rel"])
                t = cx.cp("act", a_sb[:, 2 + c0:2 + c0 + n], bank, deps=deps)
                state["la"] = t
            else:
                deps = [tk]
                if bi == 0 and state["g_rel"]:
                    deps.append(state["g_rel"])
                t = cx.cp("act", g_sb[:, c0:c0 + n], bank, deps=deps)
                state["lg"] = t
            return t

        def post(mi):
            i, which = mi // 2, mi % 2
            if which != 1:
                return
            s = i % 2
            w0 = G.pcol[:, PC["fw0"] + i:PC["fw0"] + i + 1]
            w1 = G.pcol[:, PC["fw1"] + i:PC["fw1"] + i + 1]
            w2 = G.pcol[:, PC["fw2"] + i:PC["fw2"] + i + 1]
            cb = G.pcol[:, PC["fcb"] + i:PC["fcb"] + i + 1]
            cx.ts("dve", a_sb[:, 2:2 + HALO], a_sb[:, 2:2 + HALO], G.flag, None, ALU.mult, deps=[state["la"], state["lg"]])
            cx.ts("dve", tt_[:], a_sb[:, 2:2 + NT], w2, cb, ALU.mult, ALU.add)
            cx.stt("dve", tt_[:], a_sb[:, 1:1 + NT], w1, tt_[:], ALU.mult, ALU.add)
            t_c = cx.stt("dve", tt_[:], a_sb[:, 0:NT], w0, tt_[:], ALU.mult, ALU.add)
            state["a_rel"] = t_c
            t_s = cx.act(tt_[:], tt_[:], AF.Silu, deps=[t_c])
            t_m = cx.tt("dve", ast[s][:], tt_[:], g_sb[:], ALU.mult, deps=[t_s] + ([rel[s]] if rel[s] else []))
            state["g_rel"] = t_m
            rel[s] = cx.dma(st[s], T["actT"][128 * i:128 * (i + 1), :], ast[s][:], deps=[t_m])

        run_gemm(cx, sc, ps, list(range(8)), AT, 32, T["w_up"], 0, coltiles, BLK_OWN, epi, post=post)
        cx.barrier()


def attention(cx, G, T):
    with cx.scope() as sc:
        kT = sc.sb("kT", [128, 32, NMEM], BF16)
        vsb = sc.sb("vsb", [128, 2, D], BF16)
        with cx.scope() as s1:
            hmT = s1.sb("hmT", [128, 32, NMEM], BF16)
            load_norm(cx, G, hmT, T["memt"], [(0, 128, 0), (128, 128, 128)], PC["g_mem"])
            with cx.scope() as s2:
                ps = s2.ps("ps", [128, 8, 512], F32)

                def epi(mi, bi, bank, n, c0, tk):
                    return cx.cp("act", kT[:, mi, 0:NMEM], bank, deps=[tk])

                run_gemm(cx, s2, ps, list(range(8)), hmT, 32, T["wk"], 0, [128 * m for m in range(32)], [(0, NMEM)], epi, cast_eng=("dve", "pool", "dve", "act"))
                cx.barrier()
            with cx.scope() as s2:
                ps = s2.ps("ps", [128, 8, 512], F32)
                g = Gemm(cx, s2, 32, cast_eng=("dve", "pool", "dve", "act"))
                g.load(T["wv"], 0, 0, 0)
                bank_rel = [None] * 8
                nj = 0
                for mi in range(32):
                    if mi + 1 < 32:
                        g.load(T["wv"], 0, 128 * (mi + 1), mi + 1)
                    wb, wt = g.loaded[mi][0], g.loaded[mi][1]
                    tk = None
                    for tt_i in range(2):
                        b = nj % 8
                        nj += 1
                        deps = list(wt) + ([bank_rel[b]] if bank_rel[b] else [])
                        for kc in range(32):
                            tk = cx.pe(lambda E, b=b, kc=kc, tt_i=tt_i: E.matmul(ps[:, b, 0:128], lhsT=hmT[:, kc, tt_i * 128:(tt_i + 1) * 128],
                                                                                 rhs=wb[:, kc, :], start=(kc == 0), stop=(kc == 31)),
                                       deps=deps if kc == 0 else (), tick=(kc == 31))
                        bank_rel[b] = cx.cp("act", vsb[:, tt_i, 128 * mi:128 * (mi + 1)], ps[:, b, 0:128], deps=[tk])
                    g.release(mi, tk)
                cx.barrier()
        with cx.scope() as s3:
            ps = s3.ps("ps", [128, 6, 512], F32)
            psb = s3.ps("psb", [128, 2, 1024], BF16)
            ql = [s3.dsem("ql%d" % i) for i in range(2)]
            od = [s3.dsem("od%d" % i) for i in range(2)]
            qh = [s3.sb("qh%d" % i, [128, 8, NT], BF16) for i in range(2)]
            pT = [s3.sb("pT%d" % i, [128, 2, NT], BF16) for i in range(2)]
            es = [s3.sb("es%d" % i, [128, NMEM], F32) for i in range(2)]
            pb = [s3.sb("pb%d" % i, [128, NMEM], BF16) for i in range(2)]
            mx = s3.sb("mx", [128, 2], F32)
            nmx = s3.sb("nmx", [128, 2], F32)
            sm = s3.sb("sm", [128, 2], F32)
            rs = s3.sb("rs", [128, 2], F32)
            ost = [s3.sb("ost%d" % i, [128, NT], BF16) for i in range(2)]
            tiles = [(0, 8)] + [(8 + 128 * i, 128) for i in range(16)]
            scale = 1.0 / 32.0
            bank_rel = [None] * 6
            psb_rel = [None, None]
            qh_rel = [None, None]
            pT_rel = [None, None]
            pb_rel = [None, None]
            es_rel = [None, None]
            nmx_rel = [None, None]
            ost_rel = [None, None]
            cnt = {"t": 0, "pv": 0, "d": 0}
            for h in range(4):
                qs = h % 2
                t_q = cx.dma(ql[qs], qh[qs][:], T["qT"][1024 * h:1024 * (h + 1), :].rearrange("(kc p) n -> p kc n", p=128),
                             deps=[qh_rel[qs]] if qh_rel[qs] else [])
                st_ = {}

                def stage_a(ti, c0, nr):
                    nt = cnt["t"] + ti
                    sl = nt % 2
                    b = nt % 3
                    deps = [t_q] + ([bank_rel[b]] if bank_rel[b] else [])
                    tk = None
                    for dc in range(8):
                        tk = cx.pe(lambda E, dc=dc: E.matmul(ps[0:nr, b, 0:NMEM], lhsT=qh[qs][:, dc, c0:c0 + nr], rhs=kT[:, h * 8 + dc, :],
                                                            start=(dc == 0), stop=(dc == 7)), deps=deps if dc == 0 else (), tick=(dc == 7))
                    cx.op("dve", lambda E: E.reduce_max(out=mx[0:nr, sl:sl + 1], in_=ps[0:nr, b, 0:NMEM], axis=AX.X), deps=[tk])
                    t_n = cx.ts("dve", nmx[0:nr, sl:sl + 1], mx[0:nr, sl:sl + 1], -scale, None, ALU.mult,
                                deps=[nmx_rel[sl]] if nmx_rel[sl] else [])
                    t_e = cx.act(es[sl][0:nr, :], ps[0:nr, b, 0:NMEM], AF.Exp, bias=nmx[0:nr, sl:sl + 1], scale=scale, accum=sm[0:nr, sl:sl + 1],
                                 deps=[t_n] + ([es_rel[sl]] if es_rel[sl] else []))
                    bank_rel[b] = t_e
                    nmx_rel[sl] = t_e
                    st_[ti] = (tk, t_e)

                def stage_b(ti, c0, nr):
                    nt = cnt["t"] + ti
                    sl = nt % 2
                    bb = nt % 2
                    tk, t_e = st_[ti]
                    cx.op("dve", lambda E: E.reciprocal(out=rs[0:nr, sl:sl + 1], in_=sm[0:nr, sl:sl + 1]), deps=[t_e])
                    t_p = cx.ts("dve", pb[sl][0:nr, :], es[sl][0:nr, :], rs[0:nr, sl:sl + 1], None, ALU.mult,
                                deps=[pb_rel[sl]] if pb_rel[sl] else [])
                    es_rel[sl] = t_p
                    deps = [t_p] + ([psb_rel[bb]] if psb_rel[bb] else [])
                    t_t = None
                    for kc in range(2):
                        t_t = cx.pe(lambda E, kc=kc: E.transpose(psb[:, bb, kc * 128:kc * 128 + nr], pb[sl][0:nr, kc * 128:(kc + 1) * 128],
                                                                 G.identb[0:nr, 0:nr]), deps=deps if kc == 0 else (), tick=(kc == 1))
                    pb_rel[sl] = t_t
                    d2 = [t_t]
                    if ti == 0 and pT_rel[qs]:
                        d2.append(pT_rel[qs])
                    t_c = cx.cp("act", pT[qs][:, :, c0:c0 + nr], psb[:, bb, 0:256].rearrange("p (k m) -> p k m", m=128)[:, :, 0:nr], deps=d2)
                    psb_rel[bb] = t_c
                    st_["last_s"] = tk
                    st_["last_c"] = t_c

                for ti, (c0, nr) in enumerate(tiles):
                    stage_a(ti, c0, nr)
                    if ti >= 1:
                        stage_b(ti - 1, *tiles[ti - 1])
                stage_b(len(tiles) - 1, *tiles[-1])
                cnt["t"] += len(tiles)
                qh_rel[qs] = st_[len(tiles) - 1][0]
                t_v = None
                for dvt in range(8):
                    os_ = cnt["d"] % 2
                    cnt["d"] += 1
                    t_o = None
                    for bi, (c0, n) in enumerate(BLK_OWN):
                        b = 3 + (cnt["pv"] % 3)
                        cnt["pv"] += 1
                        deps = [st_["last_c"]] + ([bank_rel[b]] if bank_rel[b] else [])
                        for kc in range(2):
                            t_v = cx.pe(lambda E, kc=kc: E.matmul(ps[:, b, 0:n], lhsT=vsb[:, kc, h * 1024 + dvt * 128:h * 1024 + (dvt + 1) * 128],
                                                                 rhs=pT[qs][:, kc, c0:c0 + n], start=(kc == 0), stop=(kc == 1)),
                                        deps=deps if kc == 0 else (), tick=(kc == 1))
                        d2 = [t_v]
                        if bi == 0 and ost_rel[os_]:
                            d2.append(ost_rel[os_])
                        t_o = cx.cp("act", ost[os_][:, c0:c0 + n], ps[:, b, 0:n], deps=d2)
                        bank_rel[b] = t_o
                    ost_rel[os_] = cx.dma(od[os_], T["oT"][128 * (h * 8 + dvt):128 * (h * 8 + dvt + 1), :], ost[os_][:], deps=[t_o])
                pT_rel[qs] = t_v
            cx.barrier()


def final_out(cx, G, T):
    with cx.scope() as sc:
        ps = sc.ps("ps", [128, 8, 512], F32)
        ld = [sc.dsem("fl%d" % i) for i in range(2)]
        st = [sc.dsem("fs%d" % i) for i in range(2)]
        rl = sc.dsem("rl")
        xf = [sc.sb("xf%d" % i, [128, 32, 128], F32) for i in range(2)]
        orow = [sc.sb("orow%d" % i, [128, D], F32) for i in range(2)]
        rb = sc.sb("rb", [128, NT], F32)
        t_rb = cx.dma(rl, rb[:], T["rstd"][2:3, :].broadcast_to([128, NT]))
        xf_rel = [None, None]
        or_rel = [None, None]
        bank_rel = [None] * 8
        nj = 0
        gcol = PC["g_final"]
        for ti in range(16):
            s = ti % 2
            c0 = 8 + 128 * ti
            t_ld = cx.dma(ld[s], xf[s][:], T["xT"].rearrange("(kc p) n -> p kc n", p=128)[:, :, c0:c0 + 128], deps=[xf_rel[s]] if xf_rel[s] else [])
            cx.tt("dve", xf[s][:], xf[s][:], rb[:, c0:c0 + 128].unsqueeze(1).broadcast_to([128, 32, 128]), ALU.mult, deps=[t_ld, t_rb])
            t_n = cx.tt("dve", xf[s][:], xf[s][:], G.pcol[:, gcol:gcol + 32].unsqueeze(2).broadcast_to([128, 32, 128]), ALU.mult)
            tk = None
            last_ev = None
            for grp in range(8):
                b = nj % 8
                nj += 1
                deps = [t_n] + ([bank_rel[b]] if bank_rel[b] else [])
                for j in range(4):
                    kc = grp * 4 + j
                    tk = cx.pe(lambda E, b=b, j=j, kc=kc, s=s: E.transpose(ps[:, b, j * 128:(j + 1) * 128], xf[s][:, kc, :], G.ident),
                               deps=deps if j == 0 else (), tick=(j == 3))
                d2 = [tk]
                if grp == 0 and or_rel[s]:
                    d2.append(or_rel[s])
                last_ev = cx.cp("act", orow[s][:, grp * 512:(grp + 1) * 512], ps[:, b, :], deps=d2)
                bank_rel[b] = last_ev
            xf_rel[s] = tk
            or_rel[s] = cx.dma(st[s], T["out"][128 * ti:128 * (ti + 1), :], orow[s][:], deps=[last_ev])
        cx.barrier()


def rest(cx, G, T, stop_after):
    with cx.scope() as s:
        AT = s.sb("AT", [128, 32, NT], BF16)
        norm_load(cx, G, AT, T["ymix"], 32, NT, PC["g_omix"], T["rstd"], [0] * 16 + [1] * 16)
        gemm_resid(cx, G, AT, 32, T["w_out"], 0, T, True)
    if stop_after == "wout":
        return
    with cx.scope() as s:
        AT = s.sb("AT", [128, 32, NT], BF16)
        norm_load(cx, G, AT, T["xT"], 32, NT, PC["g_xattn"], T["rstd"], [2] * 32)
        gemm_store(cx, G, AT, T["wq"], T["qT"])
    attention(cx, G, T)
    with cx.scope() as s:
        AT = s.sb("AT", [128, 32, NT], BF16)
        plain_load(cx, AT, T["oT"], 32, NT)
        gemm_resid(cx, G, AT, 32, T["wo"], 0, T, True)
    if stop_after == "attn":
        return
    with cx.scope() as s:
        AT = s.sb("AT", [128, 32, NT], BF16)
        norm_load(cx, G, AT, T["xT"], 32, NT, PC["g_ffn"], T["rstd"], [2] * 32)
        gemm_up(cx, G, AT, T)
    for p, (k0, kc) in enumerate([(0, 29), (29, 29), (58, 28)]):
        with cx.scope() as s:
            AT = s.sb("AT", [128, 29, NT], BF16)
            plain_load(cx, AT, T["actT"][k0 * 128:(k0 + kc) * 128, :], kc, NT)
            gemm_resid(cx, G, AT, kc, T["w_down"], k0, T, p == 2)
        if stop_after == "down0":
            return
    final_out(cx, G, T)


def _host_inputs(inp, cores):
    f = np.float32
    x = np.asarray(inp["x"], f)
    mem = np.asarray(inp["mem"], f)
    ident = np.eye(128, dtype=f)
    bd = np.kron(np.eye(8, dtype=f), np.ones((16, 16), f))
    mask8 = np.kron(np.eye(8, dtype=f), np.ones((16, 1), f))
    sgn = np.concatenate([-np.ones((64, 1), f), np.ones((64, 1), f)], 0)
    cmask = np.zeros((128, 8, 128), f)
    for gl in range(8):
        cmask[:, gl, gl * 16:(gl + 1) * 16] = 1.0
    vecs = {
        "g_mix": inp["norm_mix_g"][0], "ssm_d": inp["ssm_d"][0], "glu_b": inp["ssm_glu_b"][0],
        "cw0": inp["conv_w"][0, 0], "cw1": inp["conv_w"][0, 1], "cw2": inp["conv_w"][0, 2],
        "g_omix": np.concatenate([inp["out_norm_ssm_g"][0], inp["out_norm_conv_g"][0]]),
        "g_xattn": inp["norm_xattn_g"][0], "g_mem": inp["norm_mem_g"][0], "g_ffn": inp["norm_ffn_g"][0],
        "fw0": inp["ffn_conv_w"][0, 0], "fw1": inp["ffn_conv_w"][0, 1], "fw2": inp["ffn_conv_w"][0, 2],
        "fcb": inp["ffn_conv_b"][0], "g_final": inp["norm_final_g"],
    }
    pvec = np.zeros((NPC, 128), f)
    for k, v in vecs.items():
        v = np.asarray(v, f).reshape(-1, 128)
        pvec[PC[k]:PC[k] + v.shape[0]] = v
    lr = np.asarray(inp["ssm_lambda_re"][0], f).T
    li = np.asarray(inp["ssm_lambda_im"][0], f).T
    ls = np.broadcast_to(np.asarray(inp["ssm_log_step"][0], f)[None, :], (128, 128))
    br = np.asarray(inp["ssm_b_re"][0], f).transpose(1, 0, 2).reshape(64, 2048)
    bi = np.asarray(inp["ssm_b_im"][0], f).transpose(1, 0, 2).reshape(64, 2048)
    cr = np.asarray(inp["ssm_c_re"][0], f).transpose(2, 0, 1).reshape(64, 2048)
    ci = np.asarray(inp["ssm_c_im"][0], f).transpose(2, 0, 1).reshape(64, 2048)
    ssmin = np.concatenate([
        np.concatenate([lr, lr], 0), np.concatenate([li, li], 0), ls,
        np.concatenate([br, bi], 0), np.concatenate([bi, br], 0),
        np.concatenate([cr, ci], 0), np.concatenate([ci, cr], 0)], 1).astype(f)
    shared = {
        "cmask": cmask, "pvec": pvec, "ssmin": np.ascontiguousarray(ssmin),
        "w_in": np.asarray(inp["w_in"][0], f), "glu_w": np.asarray(inp["ssm_glu_w"][0], f),
        "w_out": np.asarray(inp["w_out"][0], f), "wq": np.asarray(inp["xattn_wq"][0], f),
        "wk": np.asarray(inp["xattn_wk"][0], f), "wv": np.asarray(inp["xattn_wv"][0], f),
        "wo": np.asarray(inp["xattn_wo"][0], f), "w_up": np.asarray(inp["ffn_w_up"][0], f),
        "w_down": np.asarray(inp["ffn_w_down"][0], f),
    }
    maps = []
    for c in cores:
        b, half = c // 2, c % 2
        xown = np.zeros((NT, D), f)
        xprev = np.zeros((NPREV, D), f)
        if half == 0:
            xown[HALO:] = x[b, 0:NTOK]
        else:
            xown[:] = x[b, NTOK - HALO:2 * NTOK]
            xprev[HALO:] = x[b, 0:NTOK - HALO]
        consts = np.concatenate([ident, bd, mask8, sgn, np.full((128, 1), float(half), f)], 1).astype(f)
        m = dict(shared)
        m.update({"xown": xown, "xprev": xprev, "memt": np.ascontiguousarray(mem[b]), "consts": np.ascontiguousarray(consts)})
        maps.append(m)
    return maps


_NC_CACHE = {}


def kernel(**inputs):
    cores = list(range(8))
    maps = _host_inputs(inputs, cores)
    if "nc" not in _NC_CACHE:
        _NC_CACHE["nc"] = build_program(None)
    res = run_bass_kernel_spmd(_NC_CACHE["nc"], maps, core_ids=cores)
    out = np.zeros((4, 2 * NTOK, D), np.float32)
    for c in cores:
        out[c // 2, (c % 2) * NTOK:(c % 2 + 1) * NTOK] = res.results[c]["out"]
    return out
```
